# Optimizing a Trainium2 kernel written in Bass

```python
import jax, jax.numpy as jnp
from jax import lax
import numpy as np

D_MODEL = 1024
BATCH = 8
SEQ = 2048
DEPTH = 2

CHUNK = 64
MIX_WIDTH = D_MODEL
ATT_HEADS = 8
HEAD_DIM = 64
ATT_WIDTH = ATT_HEADS * HEAD_DIM
LEFT_CHUNKS = 8
BAND = (LEFT_CHUNKS + 1) * CHUNK
MAX_REL = 2 * CHUNK
GMLP_WIDTH = MIX_WIDTH - ATT_WIDTH
GMLP_GROUPS = 8
GMLP_GROUP_DIM = GMLP_WIDTH // GMLP_GROUPS
GMLP_BLOCK = 128
D_FF = ((8 * D_MODEL // 3 + 255) // 256) * 256
IN_WIDTH = 3 * ATT_WIDTH + 2 * GMLP_WIDTH
EPS = 1e-6
NEG_INF = -1e30

kernel_name = "hybrid_bandattn_gmlp_streaming_block"


def rmsnorm(x, g):
    x32 = x.astype(jnp.float32)
    y = x32 * lax.rsqrt(jnp.mean(x32 * x32, axis=-1, keepdims=True) + EPS)
    return (y * g.astype(jnp.float32)).astype(x.dtype)


def band_attention(q, k, v, q_g, k_g, rel_table):
    B, S, _ = q.shape
    nc = S // CHUNK
    shp = (B, nc, CHUNK, ATT_HEADS, HEAD_DIM)
    q = rmsnorm(q.reshape(shp), q_g)
    k = rmsnorm(k.reshape(shp), k_g)
    v = v.reshape(shp)
    pad = ((0, 0), (LEFT_CHUNKS, 0), (0, 0), (0, 0), (0, 0))
    kp = jnp.pad(k, pad)
    vp = jnp.pad(v, pad)
    k_band = jnp.concatenate([kp[:, j:j + nc] for j in range(LEFT_CHUNKS + 1)], axis=2)
    v_band = jnp.concatenate([vp[:, j:j + nc] for j in range(LEFT_CHUNKS + 1)], axis=2)
    scores = jnp.einsum('bnqhd,bnkhd->bhnqk', q, k_band).astype(jnp.float32) * (HEAD_DIM ** -0.5)
    q_pos = LEFT_CHUNKS * CHUNK + jnp.arange(CHUNK)[:, None]
    k_pos = jnp.arange(BAND)[None, :]
    rel_idx = jnp.clip(q_pos - k_pos, -MAX_REL, MAX_REL) + MAX_REL
    bias = rel_table[:, rel_idx].astype(jnp.float32)
    scores = scores + bias[None, :, None, :, :]
    key_chunk = jnp.arange(nc)[:, None] - LEFT_CHUNKS + (jnp.arange(BAND) // CHUNK)[None, :]
    valid = key_chunk >= 0
    scores = jnp.where(valid[None, None, :, None, :], scores, NEG_INF)
    p = jax.nn.softmax(scores, axis=-1).astype(v.dtype)
    out = jnp.einsum('bhnqk,bnkhd->bnqhd', p, v_band)
    return out.reshape(B, S, ATT_WIDTH)


def spatial_gating(u, vg, norm_g, w_s, b_s):
    B, S, _ = u.shape
    nb = S // GMLP_BLOCK
    u = jax.nn.gelu(u)
    vg = rmsnorm(jax.nn.gelu(vg), norm_g)
    vg = vg.reshape(B, nb, GMLP_BLOCK, GMLP_GROUPS, GMLP_GROUP_DIM)
    t = jnp.arange(GMLP_BLOCK)
    mask = (t[:, None] // CHUNK) >= (t[None, :] // CHUNK)
    w = jnp.where(mask[None], w_s, jnp.zeros_like(w_s))
    mixed = jnp.einsum('gts,bnsgc->bntgc', w, vg) + b_s.T[None, None, :, :, None]
    return u * mixed.reshape(B, S, GMLP_WIDTH)


def setup_inputs(seed: int = 0) -> dict:
    key = jax.random.key(seed)
    ks = jax.random.split(key, 16)
    f32 = jnp.float32
    nrm = lambda k, shp, s: jax.random.normal(k, shp, f32) * s
    return {
        "x": nrm(ks[0], (BATCH, SEQ, D_MODEL), 1.0),
        "mix_norm_g": 1.0 + nrm(ks[1], (DEPTH, D_MODEL), 0.02),
        "w_in": nrm(ks[2], (DEPTH, D_MODEL, IN_WIDTH), D_MODEL ** -0.5),
        "q_norm_g": 1.0 + nrm(ks[3], (DEPTH, HEAD_DIM), 0.02),
        "k_norm_g": 1.0 + nrm(ks[4], (DEPTH, HEAD_DIM), 0.02),
        "rel_bias": nrm(ks[5], (DEPTH, ATT_HEADS, 2 * MAX_REL + 1), 0.1),
        "sgu_norm_g": 1.0 + nrm(ks[6], (DEPTH, GMLP_WIDTH), 0.02),
        "w_spatial": nrm(ks[7], (DEPTH, GMLP_GROUPS, GMLP_BLOCK, GMLP_BLOCK), 0.5 * GMLP_BLOCK ** -0.5),
        "b_spatial": 1.0 + nrm(ks[8], (DEPTH, GMLP_GROUPS, GMLP_BLOCK), 0.01),
        "att_out_norm_g": 1.0 + nrm(ks[9], (DEPTH, ATT_WIDTH), 0.02),
        "gmlp_out_norm_g": 1.0 + nrm(ks[10], (DEPTH, GMLP_WIDTH), 0.02),
        "w_out": nrm(ks[11], (DEPTH, MIX_WIDTH, D_MODEL), MIX_WIDTH ** -0.5),
        "ffn_norm_g": 1.0 + nrm(ks[12], (DEPTH, D_MODEL), 0.02),
        "w_ffn_in": nrm(ks[13], (DEPTH, D_MODEL, 2 * D_FF), D_MODEL ** -0.5),
        "w_ffn_out": nrm(ks[14], (DEPTH, D_FF, D_MODEL), D_FF ** -0.5),
    }


def reference(x, mix_norm_g, w_in, q_norm_g, k_norm_g, rel_bias, sgu_norm_g, w_spatial,
              b_spatial, att_out_norm_g, gmlp_out_norm_g, w_out, ffn_norm_g, w_ffn_in,
              w_ffn_out):
    split_pts = [ATT_WIDTH, 2 * ATT_WIDTH, 3 * ATT_WIDTH, 3 * ATT_WIDTH + GMLP_WIDTH]
    for l in range(DEPTH):
        h = rmsnorm(x, mix_norm_g[l])
        proj = jnp.einsum('bsd,de->bse', h, w_in[l])
        q, k, v, u, vg = jnp.split(proj, split_pts, axis=-1)
        a = band_attention(q, k, v, q_norm_g[l], k_norm_g[l], rel_bias[l])
        g = spatial_gating(u, vg, sgu_norm_g[l], w_spatial[l], b_spatial[l])
        mix = jnp.concatenate([rmsnorm(a, att_out_norm_g[l]),
                               rmsnorm(g, gmlp_out_norm_g[l])], axis=-1)
        x = x + jnp.einsum('bse,ed->bsd', mix, w_out[l])
        h = rmsnorm(x, ffn_norm_g[l])
        gate, up = jnp.split(jnp.einsum('bsd,df->bsf', h, w_ffn_in[l]), 2, axis=-1)
        x = x + jnp.einsum('bsf,fd->bsd', jax.nn.silu(gate) * up, w_ffn_out[l])
    return x
```

```python
import numpy as np
from contextlib import ExitStack
import concourse.bass as bass
import concourse.mybir as mybir
from concourse.bass_utils import run_bass_kernel_spmd

F32 = mybir.dt.float32
BF16 = mybir.dt.bfloat16
ALU = mybir.AluOpType
AF = mybir.ActivationFunctionType

ENGS = ("pe", "act", "dve", "pool", "sp")

S = 2048
D = 1024
NT = 16
NB = 4
DC = 8
DEPTH = 2
DFF = 2816
FC = 22
EPS = 1e-6
NEG = -30000.0
TABX = 768
SAME_ENGINE_WAW = True
NR = 5


class T:
    __slots__ = ("name", "last_w", "readers", "dsem", "dcount", "rng")

    def __init__(self, name, prog=None, rng=None):
        self.name = name
        self.last_w = None
        self.rng = rng
        self.readers = []
        if prog is not None:
            if rng is None:
                self.readers = list(prog.floor.items())
            else:
                m = {}
                for (lo, hi, tk) in prog.retired:
                    if lo < rng[1] and rng[0] < hi:
                        for s_, v in tk.items():
                            if m.get(s_, 0) < v:
                                m[s_] = v
                self.readers = list(m.items())
        self.dsem = None
        self.dcount = 0


class _Rec:
    def __init__(self):
        self.call = None

    def __getattr__(self, name):
        def f(*a, **k):
            self.call = (name, a, k)
        return f


class Prog:
    def __init__(self, nc, stack):
        self.nc = nc
        self.stack = stack
        self.ops = {e: [] for e in ENGS}
        self.sem = {e: stack.enter_context(nc.semaphore("s_" + e)) for e in ENGS}
        self.n = {e: 0 for e in ENGS}
        self.waited = {e: {} for e in ENGS}
        self.semkey = {self.sem[e]: e for e in ENGS}
        self.nd = 0
        self.pending = {e: [] for e in ENGS}
        self.floor = {}
        self.retired = []

    def T(self, name, buf=None):
        rng = None
        if buf is not None:
            ml = self.nc.lookup_mloc(buf)
            rng = (int(ml.addr), int(ml.addr) + int(ml.dims[1]))
        return T(name, self, rng)

    def retire(self, tiles):
        for t in tiles:
            tks = list(t.readers)
            if t.last_w is not None:
                tks.append(t.last_w)
            if t.rng is None:
                for s, v in tks:
                    if self.floor.get(s, 0) < v:
                        self.floor[s] = v
            else:
                d = {}
                for s, v in tks:
                    if d.get(s, 0) < v:
                        d[s] = v
                if d:
                    self.retired.append((t.rng[0], t.rng[1], d))

    def _collect(self, eng, reads, writes):
        w = {}
        own = self.sem.get(eng)

        def add(tk, raw):
            if tk is None:
                return
            s, v = tk
            if s is own and not raw and not SAME_ENGINE_WAW:
                return
            if w.get(s, 0) < v:
                w[s] = v
        for t in reads:
            add(t.last_w, True)
        for t in writes:
            add(t.last_w, False)
            for r in t.readers:
                add(r, False)
        need = []
        for s, v in w.items():
            if self.waited[eng].get(s, 0) >= v:
                continue
            if s is own and eng == "pe":
                continue
            if s in self.semkey:
                e2 = self.semkey[s]
                assert v <= self.n[e2], f"wait on unrecorded inc {e2} {v}>{self.n[e2]}"
            self.waited[eng][s] = v
            need.append((s, v))
        return need

    def op(self, eng, fn, reads=(), writes=(), inc=True):
        rec = _Rec()
        fn(rec)
        name_, a_, k_ = rec.call
        fn = lambda e: getattr(e, name_)(*a_, **k_)
        need = self._collect(eng, reads, writes)
        if inc:
            self.n[eng] += 1
            tk = (self.sem[eng], self.n[eng])
            for t, isw in self.pending[eng]:
                if isw:
                    t.last_w = tk
                    t.readers = []
                else:
                    t.readers.append(tk)
            self.pending[eng] = []
            for t in reads:
                t.readers.append(tk)
            for t in writes:
                t.last_w = tk
                t.readers = []
        else:
            for t in reads:
                self.pending[eng].append((t, False))
            for t in writes:
                self.pending[eng].append((t, True))
        self.ops[eng].append((need, fn, inc, None))

    def dma(self, eng, out, in_, reads=(), writes=(), owner=None, **kw):
        need = self._collect(eng, reads, writes)
        if owner is None:
            owner = writes[0] if writes else reads[0]
        if owner.dsem is None:
            owner.dsem = self.stack.enter_context(self.nc.semaphore(f"d{self.nd}"))
            self.nd += 1
        owner.dcount += 1
        tk = (owner.dsem, 16 * owner.dcount)
        for t in reads:
            t.readers.append(tk)
        for t in writes:
            t.last_w = tk
            t.readers = []
        fn = lambda e: e.dma_start(out=out, in_=in_, **kw)
        self.ops[eng].append((need, fn, False, (owner.dsem, 16)))
        return tk

    def wait_all(self, eng, tickets):
        need = []
        for s, v in tickets:
            if self.waited[eng].get(s, 0) < v:
                self.waited[eng][s] = v
                need.append((s, v))
        self.ops[eng].append((need, None, False, None))

    def replay(self):
        nc = self.nc
        for e in ENGS:
            assert not self.pending[e], f"pending tiles on {e}"
        with nc.Block() as block:
            def run(name):
                def f(e):
                    for need, fn, inc, dinc in self.ops[name]:
                        for s, v in need:
                            e.wait_ge(s, v)
                        if fn is None:
                            continue
                        ins = fn(e)
                        if inc:
                            ins.then_inc(self.sem[name], 1)
                        if dinc is not None:
                            ins.then_inc(dinc[0], dinc[1])
                return f
            block.tensor(run("pe"))
            block.scalar(run("act"))
            block.vector(run("dve"))
            block.gpsimd(run("pool"))
            block.sync(run("sp"))


def build_nc(n_layers=DEPTH, debug=(), dbg_l=0):
    nc = bass.Bass("TRN2", target_bir_lowering=False)
    dt_in = lambda name, shape: nc.dram_tensor(name, shape, F32, kind="ExternalInput")
    x_d = dt_in("x", [S, D]).ap()
    mixg_d = dt_in("mix_norm_g", [DEPTH, D])
    win_d = dt_in("w_in", [DEPTH, D, 2560]).ap()
    qg_d = dt_in("q_norm_g", [DEPTH, 64])
    kg_d = dt_in("k_norm_g", [DEPTH, 64])
    tabx_d = dt_in("tabx", [DEPTH, 8, TABX])
    sgug_d = dt_in("sgu_norm_g", [DEPTH, 512])
    wsp_d = dt_in("w_spatial", [DEPTH, 8, 128, 128]).ap()
    bsp_d = dt_in("b_spatial", [DEPTH, 8, 128]).ap()
    attg_d = dt_in("att_out_norm_g", [DEPTH, 512])
    gmg_d = dt_in("gmlp_out_norm_g", [DEPTH, 512])
    wout_d = dt_in("w_out", [DEPTH, D, D]).ap()
    ffng_d = dt_in("ffn_norm_g", [DEPTH, D])
    wfi_d = dt_in("w_ffn_in", [DEPTH, D, 2 * DFF]).ap()
    wfo_d = dt_in("w_ffn_out", [DEPTH, DFF, D]).ap()
    ident_d = dt_in("ident", [128, 128]).ap()
    jrev_d = dt_in("jrev", [128, 128]).ap()
    gind_d = dt_in("gind", [128, 512]).ap()
    y_d = nc.dram_tensor("y", [S, D], F32, kind="ExternalOutput").ap()
    dbg_tk = []
    HF = FC // 2

    with ExitStack() as st:
        P = Prog(nc, st)
        _cnt = [0]

        def sb(stack, name, shape, dt):
            _cnt[0] += 1
            return stack.enter_context(nc.sbuf_tensor(f"sb{_cnt[0]}_{name}", shape, dt))

        def sbT(stack, name, shape, dt):
            b = sb(stack, name, shape, dt)
            return b, P.T(name, b)

        def sbN(stack, name, shape, dt, n):
            bs = [sb(stack, f"{name}{i}", shape, dt) for i in range(n)]
            return bs, [P.T(f"{name}{i}", bs[i]) for i in range(n)]

        def dump(name, buf, tiles, shape, dt):
            dd = nc.dram_tensor("dbg_" + name, shape, dt, kind="ExternalOutput").ap()
            dbg_tk.append(P.dma("sp", dd, buf[:], reads=list(tiles)))

        def blk(n):
            return slice(n * 512, (n + 1) * 512)

        def tile_(t):
            return slice(t * 128, (t + 1) * 128)

        flat = lambda ll: [x for y in ll for x in y]

        xT = sb(st, "xT", [128, DC, S], F32)
        t_xT = [[P.T(f"xT{n}_{c}", xT) for c in range(DC)] for n in range(NB)]
        hT = sb(st, "hT", [128, DC, S], BF16)
        t_hT = [[P.T(f"hT{n}_{c}", hT) for c in range(DC)] for n in range(NB)]
        mixTg = sb(st, "mixTg", [128, 4, S], BF16)
        t_mixTg = [P.T(f"mTg{t}", mixTg) for t in range(NT)]
        ident, t_ident = sbT(st, "ident", [128, 128], F32)
        jrev, t_jrev = sbT(st, "jrev", [128, 128], F32)
        identb, t_identb = sbT(st, "identb", [128, 128], BF16)
        ones_bf, t_ones = sbT(st, "ones_bf", [128, 128], BF16)
        bd_bf, t_bd = sbT(st, "bd_bf", [128, 128], BF16)
        mhalf, t_mhalf = sbT(st, "mhalf", [128, 1], F32)
        mixg = sb(st, "mixg", [128, DEPTH, 8], F32); t_mixg = [P.T(f"mixg{l}", mixg) for l in range(DEPTH)]
        ffng = sb(st, "ffng", [128, DEPTH, 8], F32); t_ffng = [P.T(f"ffng{l}", ffng) for l in range(DEPTH)]
        attg = sb(st, "attg", [128, DEPTH, 4], F32); t_attg = [P.T(f"attg{l}", attg) for l in range(DEPTH)]
        gmg = sb(st, "gmg", [128, DEPTH, 4], F32); t_gmg = [P.T(f"gmg{l}", gmg) for l in range(DEPTH)]
        qg = sb(st, "qg", [128, DEPTH], F32); t_qg = [[P.T(f"qg{l}{h}", qg) for h in range(2)] for l in range(DEPTH)]
        qg8 = sb(st, "qg8", [128, DEPTH], F32); t_qg8 = [P.T(f"qg8{l}", qg8) for l in range(DEPTH)]
        kg = sb(st, "kg", [128, DEPTH], F32); t_kg = [[P.T(f"kg{l}{h}", kg) for h in range(2)] for l in range(DEPTH)]
        ring = [sb(st, f"ring{i}", [128, 2048], BF16) for i in range(NR)]
        t_ring = [[P.T(f"ring{i}a", ring[i]), P.T(f"ring{i}b", ring[i])] for i in range(NR)]
        pAll = st.enter_context(nc.psum_tensor("pAll", [128, 4096], F32))
        pB = pAll[:, 2048:3072]
        pC = pAll[:, 3072:4096]
        t_bank = [P.T(f"bank{i}") for i in range(8)]

        def bank(i):
            return pAll[:, i * 512:(i + 1) * 512]

        loads = []
        lidx = {}

        def add_load(key, view, src):
            lidx[key] = len(loads)
            if isinstance(src, list):
                loads.append((view, src))
            else:
                loads.append((view, [(view, src, (0, 1))]))

        v_half = lambda b: b[:].rearrange("p (c n) -> p c n", c=4)
        v_blk = lambda b: b[:, 0:1024].rearrange("p (c n) -> p c n", c=8)
        v_w1 = lambda b: b[:].rearrange("p (g c n) -> p g c n", g=2, c=8)
        v_w1g = lambda b: b[:, 0:1024].rearrange("p (c n) -> p c n", c=8)
        v_w1u = lambda b: b[:, 1024:2048].rearrange("p (c n) -> p c n", c=8)
        v_w2 = lambda b: b[:, 0:HF * 128].rearrange("p (f n) -> p f n", f=HF)
        for l in range(n_layers):
            win_v = win_d[l].rearrange("(c p) n -> p c n", p=128)
            for nm, c0 in (("Wu", 1536), ("Wg", 2048), ("Wv", 1024)):
                for hc in range(2):
                    add_load((l, nm, hc), v_half, win_v[:, hc * 4:(hc + 1) * 4, c0:c0 + 512])
            for fb in range(8):
                add_load((l, "Wqk", fb), v_blk, win_v[:, :, fb * 128:(fb + 1) * 128])
            wout_v = wout_d[l].rearrange("(c p) n -> p c n", p=128)
            for db in range(8):
                add_load((l, "Wo", db), v_blk, wout_v[:, :, db * 128:(db + 1) * 128])
            wfi_v = wfi_d[l].rearrange("(c p) (g f n) -> p c g f n", p=128, g=2, n=128)
            for half in range(2):
                for fc in range(HF):
                    f = half * HF + fc
                    add_load((l, "W1", f), v_w1, [(v_w1g, wfi_v[:, :, 0, f, :], (0,)), (v_w1u, wfi_v[:, :, 1, f, :], (1,))])
                for db in range(8):
                    src = wfo_d[l][half * HF * 128:(half + 1) * HF * 128, db * 128:(db + 1) * 128].rearrange(
                        "(f p) n -> p f n", p=128)
                    add_load((l, "W2", half, db), v_w2, src)
        issued = [0]

        def issue_upto(k):
            while issued[0] <= k and issued[0] < len(loads):
                j = issued[0]
                view, parts = loads[j]
                for (dfn, src, sel) in parts:
                    P.dma("pool", dfn(ring[j % NR]), src, writes=[t_ring[j % NR][k2] for k2 in sel])
                issued[0] += 1

        def done(key):
            issue_upto(lidx[key] + NR)

        def W(key):
            j = lidx[key]
            assert j < issued[0], f"load {key} not issued"
            view, _ = loads[j]
            return view(ring[j % NR]), list(t_ring[j % NR])

        P.dma("sp", ident[:], ident_d, writes=[t_ident])
        P.dma("sp", jrev[:], jrev_d, writes=[t_jrev])
        issue_upto(NR - 1)
        P.op("dve", lambda e: e.memset(ones_bf[:], 1.0), writes=[t_ones])
        P.op("dve", lambda e: e.memset(bd_bf[:], 0.0), writes=[t_bd])
        P.op("dve", lambda e: e.memset(bd_bf[0:64, 0:64], 1.0), writes=[t_bd])
        P.op("dve", lambda e: e.memset(bd_bf[64:128, 64:128], 1.0), writes=[t_bd])
        P.op("dve", lambda e: e.memset(mhalf[:], -0.5), writes=[t_mhalf])
        P.op("dve", lambda e: e.tensor_copy(identb[:], ident[:]), reads=[t_ident], writes=[t_identb])

        def load_params():
            for l in range(DEPTH):
                def colload(dst, src_t, n, c, tl):
                    src = bass.AP(src_t, l * n, [[1, 128], [128, c]])
                    P.dma("sp", dst, src, writes=[tl], allow_slow_non_contiguous=True)
                colload(mixg[:, l, :], mixg_d, D, 8, t_mixg[l])
                colload(ffng[:, l, :], ffng_d, D, 8, t_ffng[l])
                colload(attg[:, l, :], attg_d, 512, 4, t_attg[l])
                colload(gmg[:, l, :], gmg_d, 512, 4, t_gmg[l])
                for hb in range(2):
                    P.dma("sp", qg[hb * 64:(hb + 1) * 64, l:l + 1], bass.AP(qg_d, l * 64, [[1, 64], [1, 1]]),
                          writes=[t_qg[l][hb]])
                    P.dma("sp", kg[hb * 64:(hb + 1) * 64, l:l + 1], bass.AP(kg_d, l * 64, [[1, 64], [1, 1]]),
                          writes=[t_kg[l][hb]])
                P.op("dve", lambda e: e.tensor_scalar(qg8[:, l:l + 1], qg[:, l:l + 1], 0.125, None, ALU.mult),
                     reads=t_qg[l], writes=[t_qg8[l]])

        NQ = 6

        def norm_alloc(stack):
            sq, t_sq = sbN(stack, "nsq", [128, 512], BF16, NQ)
            rs, t_rs = sbN(stack, "nrs", [128, 512], F32, 2)
            return (sq, t_sq, rs, t_rs)

        def norm_block(N, gcols, t_g, l, n):
            sq, t_sq, rs, t_rs = N
            b = 6 + n % 2
            for c in range(DC):
                k = (n * DC + c) % NQ
                if c % 3 == 1:
                    P.op("pool", lambda e: e.tensor_tensor(sq[k][:], xT[:, c, blk(n)], xT[:, c, blk(n)], ALU.mult),
                         reads=[t_xT[n][c]], writes=[t_sq[k]])
                else:
                    P.op("act", lambda e: e.activation(sq[k][:], xT[:, c, blk(n)], AF.Square),
                         reads=[t_xT[n][c]], writes=[t_sq[k]])
                P.op("pe", lambda e: e.matmul(bank(b), lhsT=ones_bf[:], rhs=sq[k][:],
                                              start=(c == 0), stop=(c == DC - 1)),
                     reads=[t_ones, t_sq[k]], writes=[t_bank[b]], inc=True)
            r = n % 2
            P.op("act", lambda e: e.activation(rs[r][:], bank(b), AF.Ln, bias=EPS, scale=1.0 / D),
                 reads=[t_bank[b]], writes=[t_rs[r]])
            P.op("act", lambda e: e.activation(rs[r][:], rs[r][:], AF.Exp, scale=-0.5),
                 reads=[t_rs[r]], writes=[t_rs[r]])
            for c in range(DC):
                P.op("dve", lambda e: e.scalar_tensor_tensor(
                    out=hT[:, c, blk(n)], in0=xT[:, c, blk(n)], scalar=gcols[:, l, c:c + 1], in1=rs[r][:],
                    op0=ALU.mult, op1=ALU.mult),
                    reads=[t_xT[n][c], t_g, t_rs[r]], writes=[t_hT[n][c]])

        for l in range(n_layers):
            with ExitStack() as sl:
                sgug, t_sgug = sbT(sl, "sgug", [128, 512], F32)
                wnat, t_wnat = sbT(sl, "wnat", [128, 8, 128], F32)
                wT, t_wT = sbT(sl, "wT", [128, 8, 128], BF16)
                bnat, t_bnat = sbT(sl, "bnat", [8, 128], F32)
                bhi, t_bhi = sbT(sl, "bhi", [128, 128], BF16)
                blo, t_blo = sbT(sl, "blo", [128, 128], BF16)
                gindb, t_gindb = sbT(sl, "gindb", [128, 512], BF16)
                P.dma("pool", gindb[:], gind_d, writes=[t_gindb])
                P.dma("sp", sgug[:], bass.AP(sgug_d, l * 512, [[0, 128], [1, 512]]), writes=[t_sgug])
                P.dma("sp", wnat[:], wsp_d[l].rearrange("g t s -> t g s"), writes=[t_wnat])
                P.dma("sp", bnat[:], bsp_d[l], writes=[t_bnat])
                P.op("dve", lambda e: e.memset(bhi[:], 0.0), writes=[t_bhi])
                P.op("dve", lambda e: e.memset(blo[:], 0.0), writes=[t_blo])
                P.op("dve", lambda e: e.tensor_copy(bhi[0:8, :], bnat[:]), reads=[t_bnat], writes=[t_bhi])
                P.op("dve", lambda e: e.tensor_tensor(blo[0:8, :], bnat[:], bhi[0:8, :], ALU.subtract),
                     reads=[t_bnat, t_bhi], writes=[t_blo])

                if l == 0:
                    with ExitStack() as s0:
                        NX = 3
                        xin, t_xin = sbN(s0, "xin", [128, D], F32, NX)
                        N0 = norm_alloc(s0)
                        for tt in range(NT):
                            sl_ = tt % NX
                            P.dma("sp", xin[sl_][:], x_d[tt * 128:(tt + 1) * 128, :], writes=[t_xin[sl_]])
                            if tt == 1:
                                load_params()
                            for half in range(2):
                                b = 2 * sl_ + half
                                for cc in range(4):
                                    c = half * 4 + cc
                                    P.op("pe", lambda e: e.transpose(
                                        bank(b)[:, cc * 128:(cc + 1) * 128], xin[sl_][:, c * 128:(c + 1) * 128], ident[:]),
                                        reads=[t_xin[sl_], t_ident], writes=[t_bank[b]], inc=(cc == 3))
                                dst = xT[:, half * 4:half * 4 + 4, tile_(tt)]
                                src = bank(b).rearrange("p (c t) -> p c t", c=4)
                                tw = t_xT[tt // 4][half * 4:half * 4 + 4]
                                if half == 0:
                                    P.op("act", lambda e: e.copy(dst, src), reads=[t_bank[b]], writes=tw)
                                else:
                                    P.op("dve", lambda e: e.tensor_copy(dst, src), reads=[t_bank[b]], writes=tw)
                            if tt % 4 == 3 and tt >= 7:
                                norm_block(N0, mixg, t_mixg[l], l, tt // 4 - 1)
                        norm_block(N0, mixg, t_mixg[l], l, NB - 1)
                        P.retire(t_xin + N0[1] + N0[3])

                for hb in range(2):
                    for gg in range(4):
                        g = hb * 4 + gg
                        P.op("pe", lambda e: e.transpose(
                            bank(4 + hb)[:, gg * 128:(gg + 1) * 128], wnat[:, g, :], ident[:]),
                            reads=[t_wnat, t_ident], writes=[t_bank[4 + hb]], inc=(gg == 3))
                    P.op("dve", lambda e: e.tensor_copy(
                        wT[:, hb * 4:hb * 4 + 4, :], bank(4 + hb).rearrange("p (g t) -> p g t", g=4)),
                        reads=[t_bank[4 + hb]], writes=[t_wT])
                P.op("dve", lambda e: e.memset(wT[64:128, :, 0:64], 0.0), writes=[t_wT])

                if l > 0:
                    with ExitStack() as sn:
                        Nn = norm_alloc(sn)
                        for n in range(NB):
                            norm_block(Nn, mixg, t_mixg[l], l, n)
                        P.retire(Nn[1] + Nn[3])
                if "hT" in debug and l == dbg_l:
                    dump("hT", hT, flat(t_hT), [128, DC, S], BF16)

                with ExitStack() as sg:
                    NS = 7
                    u_sb, t_u = sbN(sg, "u_sb", [128, 512], F32, NS)
                    vgg, t_vgg = sbN(sg, "vgg", [128, 512], F32, NS)
                    vgn, t_vgn = sbN(sg, "vgn", [128, 512], BF16, NS)
                    gm, t_gm = sbN(sg, "gm", [128, 512], F32, NS)
                    junk, t_junk = sbT(sg, "gjunk", [128, 512], BF16)
                    st4 = [sb(sg, f"gst{i}", [128, 4], F32) for i in range(NS)]
                    t_st = [[P.T(f"gst{i}{j}", st4[i]) for j in range(4)] for i in range(NS)]
                    Wu = [W((l, "Wu", hc)) for hc in range(2)]
                    Wg = [W((l, "Wg", hc)) for hc in range(2)]

                    def gA(tt):
                        bU, bG = 2 * (tt % 2), 2 * (tt % 2) + 1
                        n = tt // 4
                        for (Wx, bX) in ((Wg, bG), (Wu, bU)):
                            for c in range(DC):
                                wv, wt = Wx[c // 4]
                                P.op("pe", lambda e: e.matmul(
                                    bank(bX), lhsT=hT[:, c, tile_(tt)], rhs=wv[:, c % 4, :], start=(c == 0), stop=(c == DC - 1)),
                                    reads=[t_hT[n][c]] + wt, writes=[t_bank[bX]], inc=(c == DC - 1))

                    def rstd_chain(stt, t_s, i0):
                        P.op("pool", lambda e: e.tensor_scalar(stt[:, i0:i0 + 1], stt[:, i0:i0 + 1], 1.0 / 512, EPS, ALU.mult, ALU.add),
                             reads=[t_s[i0]], writes=[t_s[i0]])
                        P.op("pool", lambda e: e.tensor_tensor(stt[:, i0 + 1:i0 + 2], stt[:, i0:i0 + 1], mhalf[:], ALU.pow),
                             reads=[t_s[i0], t_mhalf], writes=[t_s[i0 + 1]])

                    def gB(tt):
                        s_ = tt % NS
                        bU, bG = 2 * (tt % 2), 2 * (tt % 2) + 1
                        P.op("act", lambda e: e.activation(vgg[s_][:], bank(bG), AF.Gelu_apprx_tanh),
                             reads=[t_bank[bG]], writes=[t_vgg[s_]])
                        P.op("act", lambda e: e.activation(junk[:], vgg[s_][:], AF.Square, accum_out=st4[s_][:, 0:1]),
                             reads=[t_vgg[s_]], writes=[t_junk, t_st[s_][0]])
                        rstd_chain(st4[s_], t_st[s_], 0)
                        P.op("act", lambda e: e.activation(u_sb[s_][:], bank(bU), AF.Gelu_apprx_tanh),
                             reads=[t_bank[bU]], writes=[t_u[s_]])
                        P.op("dve", lambda e: e.scalar_tensor_tensor(
                            out=vgn[s_][:], in0=vgg[s_][:], scalar=st4[s_][:, 1:2], in1=sgug[:], op0=ALU.mult, op1=ALU.mult),
                            reads=[t_vgg[s_], t_st[s_][1], t_sgug], writes=[t_vgn[s_]])

                    def gC(tt):
                        s_ = tt % NS
                        bM = 4 + tt % 2
                        P.op("pe", lambda e: e.matmul(bank(bM), lhsT=bhi[:], rhs=gindb[:], start=True, stop=False),
                             reads=[t_bhi, t_gindb], writes=[t_bank[bM]], inc=False)
                        P.op("pe", lambda e: e.matmul(bank(bM), lhsT=blo[:], rhs=gindb[:], start=False, stop=False),
                             reads=[t_blo, t_gindb], writes=[t_bank[bM]], inc=False)
                        for g in range(8):
                            P.op("pe", lambda e: e.matmul(
                                bank(bM)[:, g * 64:(g + 1) * 64], lhsT=wT[:, g, :], rhs=vgn[s_][:, g * 64:(g + 1) * 64],
                                start=False, stop=(g == 7)),
                                reads=[t_wT, t_vgn[s_]], writes=[t_bank[bM]], inc=(g == 7))

                    def gD(tt):
                        s_ = tt % NS
                        bM = 4 + tt % 2
                        P.op("dve", lambda e: e.tensor_tensor(gm[s_][:], bank(bM), u_sb[s_][:], ALU.mult),
                             reads=[t_bank[bM], t_u[s_]], writes=[t_gm[s_]])
                        P.op("act", lambda e: e.activation(junk[:], gm[s_][:], AF.Square, accum_out=st4[s_][:, 2:3]),
                             reads=[t_gm[s_]], writes=[t_junk, t_st[s_][2]])
                        rstd_chain(st4[s_], t_st[s_], 2)
                        P.op("act", lambda e: e.activation(gm[s_][:], gm[s_][:], AF.Copy, scale=st4[s_][:, 3:4]),
                             reads=[t_gm[s_], t_st[s_][3]], writes=[t_gm[s_]])

                    def gE(tt):
                        s_ = tt % NS
                        bTr = 6 + tt % 2
                        for j in range(4):
                            P.op("pe", lambda e: e.transpose(
                                bank(bTr)[:, j * 128:(j + 1) * 128], gm[s_][:, j * 128:(j + 1) * 128], ident[:]),
                                reads=[t_gm[s_], t_ident], writes=[t_bank[bTr]], inc=(j == 3))
                        P.op("dve", lambda e: e.tensor_tensor(
                            mixTg[:, :, tile_(tt)], bank(bTr).rearrange("p (j t) -> p j t", j=4),
                            gmg[:, l, :].rearrange("p (j o) -> p j o", o=1).to_broadcast([128, 4, 128]), ALU.mult),
                            reads=[t_bank[bTr], t_gmg[l]], writes=[t_mixTg[tt]])

                    for s_ in range(NT + 6):
                        if 0 <= s_ - 3 < NT:
                            gC(s_ - 3); gD(s_ - 3)
                        if 0 <= s_ - 6 < NT:
                            gE(s_ - 6)
                        if s_ < NT:
                            gA(s_); gB(s_)
                    for hc in range(2):
                        done((l, "Wu", hc)); done((l, "Wg", hc))
                    P.retire([t_junk] + t_u + t_vgg + t_vgn + t_gm + flat(t_st))
                P.retire([t_sgug, t_wnat, t_wT, t_bnat, t_bhi, t_blo, t_gindb])
            if "mixTg" in debug and l == dbg_l:
                dump("mixTg", mixTg, t_mixTg, [128, 4, S], BF16)

            with ExitStack() as sa:
                bias = sb(sa, "expB", [128, 4, 5, 2, 128], BF16)
                t_bias = [P.T(f"expB{h}", bias) for h in range(8)]
                V = sb(sa, "V", [128, NT, 8, 65], BF16)
                t_V = [P.T(f"V{t}", V) for t in range(NT)]
                qT = sb(sa, "qT", [128, 4, S], BF16)
                kT = sb(sa, "kT", [128, 4, S], BF16)
                t_qT = [[P.T(f"qT{f}{n}", qT) for n in range(NB)] for f in range(4)]
                t_kT = [[P.T(f"kT{f}{n}", kT) for n in range(NB)] for f in range(4)]
                P.op("dve", lambda e: e.memset(V[:, :, :, 64:65], 1.0), writes=t_V)

                with ExitStack() as shk:
                    hk, t_hk = sbN(shk, "hk", [128, 640], F32, 2)

                    def bias_head(h):
                        k = h % 2
                        src = bass.AP(tabx_d, (l * 8 + h) * TABX + 1, [[1, 128], [128, 5], [1, 128]])
                        P.dma("sp", hk[k][:].rearrange("p (d q) -> p d q", d=5), src, writes=[t_hk[k]])
                        b0 = 4 + 2 * k
                        pX = pB if k == 0 else pC
                        P.op("pe", lambda e: e.matmul(pX[:, 0:512], lhsT=jrev[:], rhs=hk[k][:, 0:512], start=True, stop=True),
                             reads=[t_jrev, t_hk[k]], writes=[t_bank[b0]])
                        P.op("pe", lambda e: e.matmul(pX[:, 512:640], lhsT=jrev[:], rhs=hk[k][:, 512:640], start=True, stop=True),
                             reads=[t_jrev, t_hk[k]], writes=[t_bank[b0 + 1]])
                        P.op("act", lambda e: e.activation(bias[:, h // 2, :, h % 2, :],
                                                          pX[:, 0:640].rearrange("p (d q) -> p d q", d=5), AF.Exp),
                             reads=[t_bank[b0], t_bank[b0 + 1]], writes=[t_bias[h]])
                        P.op("dve", lambda e: e.memset(bias[64:128, h // 2, 0, h % 2, 0:64], 0.0), writes=[t_bias[h]])
                        P.op("dve", lambda e: e.memset(bias[0:64, h // 2, 4, h % 2, 64:128], 0.0), writes=[t_bias[h]])

                    Wv = [W((l, "Wv", hc)) for hc in range(2)]
                    for tt in range(NT):
                        b = tt % 4
                        n = tt // 4
                        for c in range(DC):
                            wv, wt = Wv[c // 4]
                            P.op("pe", lambda e: e.matmul(
                                bank(b), lhsT=hT[:, c, tile_(tt)], rhs=wv[:, c % 4, :], start=(c == 0), stop=(c == DC - 1)),
                                reads=[t_hT[n][c]] + wt, writes=[t_bank[b]], inc=(c == DC - 1))
                        src = bank(b).rearrange("p (h d) -> p h d", h=8)
                        if tt % 2 == 0:
                            P.op("act", lambda e: e.copy(V[:, tt, :, 0:64], src), reads=[t_bank[b]], writes=[t_V[tt]])
                        else:
                            P.op("dve", lambda e: e.tensor_copy(V[:, tt, :, 0:64], src), reads=[t_bank[b]], writes=[t_V[tt]])
                        if tt % 2 == 1:
                            bias_head(tt // 2)
                    for hc in range(2):
                        done((l, "Wv", hc))
                    P.retire(t_hk)

                with ExitStack() as sq_:
                    NQK = 3
                    qsq, t_qsq = sbN(sq_, "qsq", [128, 512], BF16, NQK)
                    qrs, t_qrs = sbN(sq_, "qrs", [128, 512], F32, NQK)
                    NIT = 32

                    def qA(it):
                        fb, n = it // 4, it % 4
                        bQ = it % 3
                        wv, wt = W((l, "Wqk", fb))
                        for c in range(DC):
                            P.op("pe", lambda e: e.matmul(
                                bank(bQ), lhsT=wv[:, c, :], rhs=hT[:, c, blk(n)], start=(c == 0), stop=(c == DC - 1)),
                                reads=wt + [t_hT[n][c]], writes=[t_bank[bQ]], inc=(c == DC - 1))
                        if n == 3:
                            done((l, "Wqk", fb))
                        r = it % NQK
                        P.op("act", lambda e: e.activation(qsq[r][:], bank(bQ), AF.Square),
                             reads=[t_bank[bQ]], writes=[t_qsq[r]])

                    def qC(it):
                        fb, n = it // 4, it % 4
                        bQ = it % 3
                        bS = 4 + it % 2
                        r = it % NQK
                        isq = fb < 4
                        dstT = qT if isq else kT
                        t_dst = t_qT if isq else t_kT
                        gcol = qg8 if isq else kg
                        t_gc = [t_qg8[l]] if isq else t_kg[l]
                        P.op("pe", lambda e: e.matmul(bank(bS), lhsT=bd_bf[:], rhs=qsq[r][:], start=True, stop=True),
                             reads=[t_bd, t_qsq[r]], writes=[t_bank[bS]])
                        P.op("act", lambda e: e.activation(qrs[r][:], bank(bS), AF.Ln, bias=EPS, scale=1.0 / 64),
                             reads=[t_bank[bS]], writes=[t_qrs[r]])
                        P.op("act", lambda e: e.activation(qrs[r][:], qrs[r][:], AF.Exp, scale=-0.5),
                             reads=[t_qrs[r]], writes=[t_qrs[r]])
                        P.op("dve", lambda e: e.scalar_tensor_tensor(
                            out=dstT[:, fb % 4, blk(n)], in0=bank(bQ), scalar=gcol[:, l:l + 1], in1=qrs[r][:],
                            op0=ALU.mult, op1=ALU.mult),
                            reads=[t_bank[bQ]] + t_gc + [t_qrs[r]], writes=[t_dst[fb % 4][n]])

                    for s_ in range(NIT + 1):
                        if s_ < NIT:
                            qA(s_)
                        if s_ - 1 >= 0:
                            qC(s_ - 1)
                    P.retire(t_qsq + t_qrs)
                if "qT" in debug and l == dbg_l:
                    dump("qT", qT, flat(t_qT), [128, 4, S], BF16)
                    dump("kT", kT, flat(t_kT), [128, 4, S], BF16)
                    dump("V", V, t_V, [128, NT, 8, 65], BF16)

                P.retire(flat(t_hT))
                t_mixTa = [P.T(f"mTa{t}", hT) for t in range(NT)]
                mixTa = hT

                with ExitStack() as sat:
                    NQM = 3
                    qm, t_qm = sbN(sat, "qm", [128, 256], BF16, NQM)
                    NEP = 3
                    EP, t_EP = sbN(sat, "EP", [128, 1280], BF16, NEP)
                    rc, t_rc = sbT(sat, "rc", [128, 8], F32)
                    atok, t_atok = sbT(sat, "atok", [128, 512], F32)
                    anb, t_anb = sbT(sat, "anb", [128, 512], BF16)
                    ast = sb(sat, "ast", [128, 2], F32)
                    t_ast = [P.T(f"ast{j}", ast) for j in range(2)]
                    for i in range(NQM):
                        P.op("pool", lambda e: e.memset(qm[i][:], 0.0), writes=[t_qm[i]])

                    P.retire(t_bank[0:6])
                    t_S = [P.T(f"S{i}") for i in range(2)]
                    NP = NT * 4
                    trb = pAll[:, 7 * 512 + 128:7 * 512 + 384].bitcast(BF16)

                    def aQ(p):
                        m, hp = p // 4, p % 4
                        k = p % NQM
                        P.op("pool", lambda e: e.tensor_copy(qm[k][0:64, 0:128], qT[0:64, hp, tile_(m)]),
                             reads=[t_qT[hp][m // 4]], writes=[t_qm[k]])
                        P.op("pool", lambda e: e.tensor_copy(qm[k][64:128, 128:256], qT[64:128, hp, tile_(m)]),
                             reads=[t_qT[hp][m // 4]], writes=[t_qm[k]])

                    def aA(p):
                        m, hp = p // 4, p % 4
                        nb_ = min(m, 4) + 1
                        st_ = p % 2
                        c0 = st_ * 1536
                        k = p % NQM
                        for d in range(nb_):
                            j = m - d
                            P.op("pe", lambda e: e.matmul(
                                pAll[:, c0 + d * 256:c0 + (d + 1) * 256], lhsT=kT[:, hp, tile_(j)], rhs=qm[k][:],
                                start=True, stop=True),
                                reads=[t_kT[hp][j // 4], t_qm[k]], writes=[t_S[st_]], inc=(d == nb_ - 1))
                        w = nb_ * 256
                        ep = p % NEP
                        P.op("act", lambda e: e.activation(EP[ep][:, 0:w], pAll[:, c0:c0 + w], AF.Exp),
                             reads=[t_S[st_]], writes=[t_EP[ep]])
                        P.op("dve", lambda e: e.tensor_tensor(
                            EP[ep][:, 0:w], EP[ep][:, 0:w], bias[:, hp, 0:nb_, :, :].rearrange("p d u q -> p (d u q)"), ALU.mult),
                            reads=[t_EP[ep], t_bias[2 * hp], t_bias[2 * hp + 1]], writes=[t_EP[ep]])

                    deferred = {}

                    def aC(p, step):
                        m, hp = p // 4, p % 4
                        nb_ = min(m, 4) + 1
                        ep = p % NEP
                        for u in range(2):
                            h = 2 * hp + u
                            ob = 6 if h < 7 else 7
                            oc = (6 * 512 + h * 65) if h < 7 else 7 * 512
                            for d in range(nb_):
                                j = m - d
                                P.op("pe", lambda e: e.matmul(
                                    pAll[:, oc:oc + 65], lhsT=EP[ep][:, (d * 2 + u) * 128:(d * 2 + u + 1) * 128], rhs=V[:, j, h, :],
                                    start=(d == 0), stop=(d == nb_ - 1)),
                                    reads=[t_EP[ep], t_V[j]], writes=[t_bank[ob]], inc=(d == nb_ - 1))
                        if hp == 3:
                            aD1(m)
                            deferred.setdefault(step + 1, []).append((aD2, m))
                            deferred.setdefault(step + 2, []).append((aD3, m))
                            deferred.setdefault(step + 3, []).append((aE, m))

                    def aD1(m):
                        ov6 = pAll[:, 6 * 512:6 * 512 + 455].rearrange("p (h d) -> p h d", h=7)
                        ov7 = pAll[:, 7 * 512:7 * 512 + 65]
                        P.op("dve", lambda e: e.reciprocal(rc[:, 0:7].rearrange("p (h o) -> p h o", o=1), ov6[:, :, 64:65]),
                             reads=[t_bank[6]], writes=[t_rc])
                        P.op("dve", lambda e: e.reciprocal(rc[:, 7:8], ov7[:, 64:65]), reads=[t_bank[7]], writes=[t_rc])
                        P.op("dve", lambda e: e.tensor_tensor(
                            atok[:, 0:448].rearrange("p (h d) -> p h d", h=7), ov6[:, :, 0:64],
                            rc[:, 0:7].rearrange("p (h o) -> p h o", o=1).to_broadcast([128, 7, 64]), ALU.mult),
                            reads=[t_bank[6], t_rc], writes=[t_atok])
                        P.op("dve", lambda e: e.tensor_scalar(atok[:, 448:512], ov7[:, 0:64], rc[:, 7:8], None, ALU.mult),
                             reads=[t_bank[7], t_rc], writes=[t_atok])

                    def aD2(m):
                        P.op("act", lambda e: e.activation(anb[:], atok[:], AF.Square, accum_out=ast[:, 0:1]),
                             reads=[t_atok], writes=[t_anb, t_ast[0]])
                        P.op("pool", lambda e: e.tensor_scalar(ast[:, 0:1], ast[:, 0:1], 1.0 / 512, EPS, ALU.mult, ALU.add),
                             reads=[t_ast[0]], writes=[t_ast[0]])
                        P.op("pool", lambda e: e.tensor_tensor(ast[:, 1:2], ast[:, 0:1], mhalf[:], ALU.pow),
                             reads=[t_ast[0], t_mhalf], writes=[t_ast[1]])

                    def aD3(m):
                        P.op("act", lambda e: e.activation(anb[:], atok[:], AF.Copy, scale=ast[:, 1:2]),
                             reads=[t_atok, t_ast[1]], writes=[t_anb])

                    def aE(m):
                        for j in range(4):
                            P.op("pe", lambda e: e.transpose(
                                trb[:, j * 128:(j + 1) * 128], anb[:, j * 128:(j + 1) * 128], identb[:]),
                                reads=[t_anb, t_identb], writes=[t_bank[7]], inc=(j == 3))
                        P.op("dve", lambda e: e.tensor_tensor(
                            mixTa[:, 0:4, tile_(m)], trb.rearrange("p (j t) -> p j t", j=4),
                            attg[:, l, :].rearrange("p (j o) -> p j o", o=1).to_broadcast([128, 4, 128]), ALU.mult),
                            reads=[t_bank[7], t_attg[l]], writes=[t_mixTa[m]])

                    aQ(0); aQ(1)
                    for s_ in range(NP + 6):
                        if s_ + 2 < NP:
                            aQ(s_ + 2)
                        if s_ < NP:
                            aA(s_)
                        if 0 <= s_ - 1 < NP:
                            aC(s_ - 1, s_)
                        for (fn_, m) in deferred.pop(s_, []):
                            fn_(m)
                    assert not deferred
                    P.retire(t_S)
                    t_bank[0:6] = [P.T(f"bank{i}") for i in range(6)]
                    P.retire(t_qm + t_EP + [t_rc, t_atok, t_anb] + t_ast)
                P.retire(t_V + flat(t_qT) + flat(t_kT) + t_bias)
            if "mixTa" in debug and l == dbg_l:
                dump("mixTa", hT, t_mixTa, [128, DC, S], BF16)

            it = 0
            for db in range(8):
                wv, wt = W((l, "Wo", db))
                for n in range(NB):
                    b = it % 4
                    it += 1
                    for c in range(DC):
                        rhs = mixTa[:, c, blk(n)] if c < 4 else mixTg[:, c - 4, blk(n)]
                        tr = (t_mixTa if c < 4 else t_mixTg)[4 * n:4 * n + 4]
                        P.op("pe", lambda e: e.matmul(
                            bank(b), lhsT=wv[:, c, :], rhs=rhs, start=(c == 0), stop=(c == DC - 1)),
                            reads=wt + tr, writes=[t_bank[b]], inc=(c == DC - 1))
                    P.op("dve", lambda e: e.tensor_tensor(xT[:, db, blk(n)], xT[:, db, blk(n)], bank(b), ALU.add),
                         reads=[t_bank[b], t_xT[n][db]], writes=[t_xT[n][db]])
                done((l, "Wo", db))
            P.retire(t_mixTa)
            t_hT = [[P.T(f"hT{n}_{c}", hT) for c in range(DC)] for n in range(NB)]
            if "x1" in debug and l == dbg_l:
                dump("x1", xT, flat(t_xT), [128, DC, S], F32)

            with ExitStack() as sf:
                actT = sb(sf, "actT", [128, HF, S], BF16)
                t_actT = [[P.T(f"actT{n}_{f}", actT) for f in range(HF)] for n in range(NB)]
                NSG = 3
                sg_, t_sg = sbN(sf, "sg", [128, 512], F32, NSG)
                with ExitStack() as sn:
                    Nn = norm_alloc(sn)
                    for n in range(NB):
                        norm_block(Nn, ffng, t_ffng[l], l, n)
                    P.retire(Nn[1] + Nn[3])
                it = 0
                it2 = 0
                for half in range(2):
                    for fc in range(HF):
                        f = half * HF + fc
                        wv, wt = W((l, "W1", f))
                        for n in range(NB):
                            bA = 2 * (it % 2)
                            bB = bA + 1
                            r = it % NSG
                            it += 1
                            for gi, bX in ((0, bA), (1, bB)):
                                for c in range(DC):
                                    P.op("pe", lambda e: e.matmul(
                                        bank(bX), lhsT=wv[:, gi, c, :], rhs=hT[:, c, blk(n)], start=(c == 0), stop=(c == DC - 1)),
                                        reads=wt + [t_hT[n][c]], writes=[t_bank[bX]], inc=(c == DC - 1))
                            P.op("act", lambda e: e.activation(sg_[r][:], bank(bA), AF.Silu),
                                 reads=[t_bank[bA]], writes=[t_sg[r]])
                            P.op("dve", lambda e: e.tensor_tensor(actT[:, fc, blk(n)], sg_[r][:], bank(bB), ALU.mult),
                                 reads=[t_sg[r], t_bank[bB]], writes=[t_actT[n][fc]])
                        done((l, "W1", f))
                    for db in range(8):
                        wv, wt = W((l, "W2", half, db))
                        for n in range(NB):
                            b = 4 + it2 % 4
                            it2 += 1
                            for fc in range(HF):
                                P.op("pe", lambda e: e.matmul(
                                    bank(b), lhsT=wv[:, fc, :], rhs=actT[:, fc, blk(n)], start=(fc == 0), stop=(fc == HF - 1)),
                                    reads=wt + [t_actT[n][fc]], writes=[t_bank[b]], inc=(fc == HF - 1))
                            P.op("dve", lambda e: e.tensor_tensor(xT[:, db, blk(n)], xT[:, db, blk(n)], bank(b), ALU.add),
                                 reads=[t_bank[b], t_xT[n][db]], writes=[t_xT[n][db]])
                        done((l, "W2", half, db))
                P.retire(flat(t_actT) + t_sg)

        out_tk = []
        with ExitStack() as sy:
            NY = 3
            yo = [sb(sy, f"yo{i}", [128, D], F32) for i in range(NY)]
            t_yo = [[P.T(f"yo{i}_{h}", yo[i]) for h in range(2)] for i in range(NY)]
            for tt in range(NT):
                sl_ = tt % NY
                for half in range(2):
                    b = 2 * (tt % 4) + half
                    for cc in range(4):
                        c = half * 4 + cc
                        P.op("pe", lambda e: e.transpose(
                            bank(b)[:, cc * 128:(cc + 1) * 128], xT[:, c, tile_(tt)], ident[:]),
                            reads=[t_xT[tt // 4][c], t_ident], writes=[t_bank[b]], inc=(cc == 3))
                    if half == 0:
                        P.op("act", lambda e: e.copy(yo[sl_][:, 0:512], bank(b)), reads=[t_bank[b]], writes=[t_yo[sl_][0]])
                    else:
                        P.op("dve", lambda e: e.tensor_copy(yo[sl_][:, 512:1024], bank(b)), reads=[t_bank[b]], writes=[t_yo[sl_][1]])
                out_tk.append(P.dma("sp", y_d[tt * 128:(tt + 1) * 128, :], yo[sl_][:], reads=t_yo[sl_], owner=t_yo[sl_][0]))
            P.wait_all("sp", out_tk + dbg_tk)
            P.replay()
    return nc


_NC_CACHE = {}


_GIND = np.zeros((128, 512), dtype=np.float32)
for _g in range(8):
    _GIND[_g, _g * 64:(_g + 1) * 64] = 1.0


def _host_inputs(inputs):
    f32 = lambda a: np.ascontiguousarray(np.asarray(a, dtype=np.float32))
    rel = f32(inputs["rel_bias"])
    tabx = np.concatenate([rel, np.repeat(rel[..., -1:], TABX - rel.shape[-1], axis=-1)], axis=-1)
    shared = {
        "mix_norm_g": f32(inputs["mix_norm_g"]), "w_in": f32(inputs["w_in"]),
        "q_norm_g": f32(inputs["q_norm_g"]), "k_norm_g": f32(inputs["k_norm_g"]),
        "tabx": np.ascontiguousarray(tabx), "sgu_norm_g": f32(inputs["sgu_norm_g"]),
        "w_spatial": f32(inputs["w_spatial"]), "b_spatial": f32(inputs["b_spatial"]),
        "att_out_norm_g": f32(inputs["att_out_norm_g"]), "gmlp_out_norm_g": f32(inputs["gmlp_out_norm_g"]),
        "w_out": f32(inputs["w_out"]), "ffn_norm_g": f32(inputs["ffn_norm_g"]),
        "w_ffn_in": f32(inputs["w_ffn_in"]), "w_ffn_out": f32(inputs["w_ffn_out"]),
        "ident": np.eye(128, dtype=np.float32),
        "jrev": np.ascontiguousarray(np.eye(128, dtype=np.float32)[:, ::-1]),
        "gind": _GIND,
    }
    return shared


def kernel(**inputs):
    x = np.asarray(inputs["x"], dtype=np.float32)
    B = x.shape[0]
    shared = _host_inputs(inputs)
    if "nc" not in _NC_CACHE:
        _NC_CACHE["nc"] = build_nc()
    nc = _NC_CACHE["nc"]
    in_maps = [dict(shared, x=np.ascontiguousarray(x[b])) for b in range(B)]
    res = run_bass_kernel_spmd(nc, in_maps, core_ids=list(range(B)))
    return np.stack([np.asarray(r["y"], dtype=np.float32) for r in res.results], axis=0)
```

```python
import numpy as np
from contextlib import ExitStack
import concourse.bass as bass
import concourse.mybir as mybir
from concourse.bass_utils import run_bass_kernel_spmd

F32 = mybir.dt.float32
BF16 = mybir.dt.bfloat16
ALU = mybir.AluOpType
AF = mybir.ActivationFunctionType

ENGS = ("pe", "act", "dve", "pool", "sp")

S = 2048
D = 1024
NT = 16
NB = 4
DC = 8
DEPTH = 2
DFF = 2816
FC = 22
EPS = 1e-6
NEG = -30000.0
TABX = 768
SAME_ENGINE_WAW = True
NR = 5


class T:
    __slots__ = ("name", "last_w", "readers", "dsem", "dcount", "rng")

    def __init__(self, name, prog=None, rng=None):
        self.name = name
        self.last_w = None
        self.rng = rng
        self.readers = []
        if prog is not None:
            if rng is None:
                self.readers = list(prog.floor.items())
            else:
                m = {}
                for (lo, hi, tk) in prog.retired:
                    if lo < rng[1] and rng[0] < hi:
                        for s_, v in tk.items():
                            if m.get(s_, 0) < v:
                                m[s_] = v
                self.readers = list(m.items())
        self.dsem = None
        self.dcount = 0


class _Rec:
    def __init__(self):
        self.call = None

    def __getattr__(self, name):
        def f(*a, **k):
            self.call = (name, a, k)
        return f


class Prog:
    def __init__(self, nc, stack):
        self.nc = nc
        self.stack = stack
        self.ops = {e: [] for e in ENGS}
        self.sem = {e: stack.enter_context(nc.semaphore("s_" + e)) for e in ENGS}
        self.n = {e: 0 for e in ENGS}
        self.waited = {e: {} for e in ENGS}
        self.semkey = {self.sem[e]: e for e in ENGS}
        self.nd = 0
        self.pending = {e: [] for e in ENGS}
        self.floor = {}
        self.retired = []

    def T(self, name, buf=None):
        rng = None
        if buf is not None:
            ml = self.nc.lookup_mloc(buf)
            rng = (int(ml.addr), int(ml.addr) + int(ml.dims[1]))
        return T(name, self, rng)

    def retire(self, tiles):
        for t in tiles:
            tks = list(t.readers)
            if t.last_w is not None:
                tks.append(t.last_w)
            if t.rng is None:
                for s, v in tks:
                    if self.floor.get(s, 0) < v:
                        self.floor[s] = v
            else:
                d = {}
                for s, v in tks:
                    if d.get(s, 0) < v:
                        d[s] = v
                if d:
                    self.retired.append((t.rng[0], t.rng[1], d))

    def _collect(self, eng, reads, writes):
        w = {}
        own = self.sem.get(eng)

        def add(tk, raw):
            if tk is None:
                return
            s, v = tk
            if s is own and not raw and not SAME_ENGINE_WAW:
                return
            if w.get(s, 0) < v:
                w[s] = v
        for t in reads:
            add(t.last_w, True)
        for t in writes:
            add(t.last_w, False)
            for r in t.readers:
                add(r, False)
        need = []
        for s, v in w.items():
            if self.waited[eng].get(s, 0) >= v:
                continue
            if s is own and eng == "pe":
                continue
            if s in self.semkey:
                e2 = self.semkey[s]
                assert v <= self.n[e2], f"wait on unrecorded inc {e2} {v}>{self.n[e2]}"
            self.waited[eng][s] = v
            need.append((s, v))
        return need

    def op(self, eng, fn, reads=(), writes=(), inc=True):
        rec = _Rec()
        fn(rec)
        name_, a_, k_ = rec.call
        fn = lambda e: getattr(e, name_)(*a_, **k_)
        need = self._collect(eng, reads, writes)
        if inc:
            self.n[eng] += 1
            tk = (self.sem[eng], self.n[eng])
            for t, isw in self.pending[eng]:
                if isw:
                    t.last_w = tk
                    t.readers = []
                else:
                    t.readers.append(tk)
            self.pending[eng] = []
            for t in reads:
                t.readers.append(tk)
            for t in writes:
                t.last_w = tk
                t.readers = []
        else:
            for t in reads:
                self.pending[eng].append((t, False))
            for t in writes:
                self.pending[eng].append((t, True))
        self.ops[eng].append((need, fn, inc, None))

    def dma(self, eng, out, in_, reads=(), writes=(), owner=None, **kw):
        need = self._collect(eng, reads, writes)
        if owner is None:
            owner = writes[0] if writes else reads[0]
        if owner.dsem is None:
            owner.dsem = self.stack.enter_context(self.nc.semaphore(f"d{self.nd}"))
            self.nd += 1
        owner.dcount += 1
        tk = (owner.dsem, 16 * owner.dcount)
        for t in reads:
            t.readers.append(tk)
        for t in writes:
            t.last_w = tk
            t.readers = []
        fn = lambda e: e.dma_start(out=out, in_=in_, **kw)
        self.ops[eng].append((need, fn, False, (owner.dsem, 16)))
        return tk

    def wait_all(self, eng, tickets):
        need = []
        for s, v in tickets:
            if self.waited[eng].get(s, 0) < v:
                self.waited[eng][s] = v
                need.append((s, v))
        self.ops[eng].append((need, None, False, None))

    def replay(self):
        nc = self.nc
        for e in ENGS:
            assert not self.pending[e], f"pending tiles on {e}"
        with nc.Block() as block:
            def run(name):
                def f(e):
                    for need, fn, inc, dinc in self.ops[name]:
                        for s, v in need:
                            e.wait_ge(s, v)
                        if fn is None:
                            continue
                        ins = fn(e)
                        if inc:
                            ins.then_inc(self.sem[name], 1)
                        if dinc is not None:
                            ins.then_inc(dinc[0], dinc[1])
                return f
            block.tensor(run("pe"))
            block.scalar(run("act"))
            block.vector(run("dve"))
            block.gpsimd(run("pool"))
            block.sync(run("sp"))


def build_nc(n_layers=DEPTH, debug=(), dbg_l=0):
    nc = bass.Bass("TRN2", target_bir_lowering=False)
    dt_in = lambda name, shape: nc.dram_tensor(name, shape, F32, kind="ExternalInput")
    x_d = dt_in("x", [S, D]).ap()
    mixg_d = dt_in("mix_norm_g", [DEPTH, D])
    win_d = dt_in("w_in", [DEPTH, D, 2560]).ap()
    qg_d = dt_in("q_norm_g", [DEPTH, 64])
    kg_d = dt_in("k_norm_g", [DEPTH, 64])
    tabx_d = dt_in("tabx", [DEPTH, 8, TABX])
    sgug_d = dt_in("sgu_norm_g", [DEPTH, 512])
    wsp_d = dt_in("w_spatial", [DEPTH, 8, 128, 128]).ap()
    bsp_d = dt_in("b_spatial", [DEPTH, 8, 128]).ap()
    attg_d = dt_in("att_out_norm_g", [DEPTH, 512])
    gmg_d = dt_in("gmlp_out_norm_g", [DEPTH, 512])
    wout_d = dt_in("w_out", [DEPTH, D, D]).ap()
    ffng_d = dt_in("ffn_norm_g", [DEPTH, D])
    wfi_d = dt_in("w_ffn_in", [DEPTH, D, 2 * DFF]).ap()
    wfo_d = dt_in("w_ffn_out", [DEPTH, DFF, D]).ap()
    ident_d = dt_in("ident", [128, 128]).ap()
    jrev_d = dt_in("jrev", [128, 128]).ap()
    gind_d = dt_in("gind", [128, 512]).ap()
    y_d = nc.dram_tensor("y", [S, D], F32, kind="ExternalOutput").ap()
    dbg_tk = []
    HF = FC // 2

    with ExitStack() as st:
        P = Prog(nc, st)
        _cnt = [0]

        def sb(stack, name, shape, dt):
            _cnt[0] += 1
            return stack.enter_context(nc.sbuf_tensor(f"sb{_cnt[0]}_{name}", shape, dt))

        def sbT(stack, name, shape, dt):
            b = sb(stack, name, shape, dt)
            return b, P.T(name, b)

        def sbN(stack, name, shape, dt, n):
            bs = [sb(stack, f"{name}{i}", shape, dt) for i in range(n)]
            return bs, [P.T(f"{name}{i}", bs[i]) for i in range(n)]

        def dump(name, buf, tiles, shape, dt):
            dd = nc.dram_tensor("dbg_" + name, shape, dt, kind="ExternalOutput").ap()
            dbg_tk.append(P.dma("sp", dd, buf[:], reads=list(tiles)))

        def blk(n):
            return slice(n * 512, (n + 1) * 512)

        def tile_(t):
            return slice(t * 128, (t + 1) * 128)

        flat = lambda ll: [x for y in ll for x in y]

        xT = sb(st, "xT", [128, DC, S], F32)
        t_xT = [[P.T(f"xT{n}_{c}", xT) for c in range(DC)] for n in range(NB)]
        hT = sb(st, "hT", [128, DC, S], BF16)
        t_hT = [[P.T(f"hT{n}_{c}", hT) for c in range(DC)] for n in range(NB)]
        mixTg = sb(st, "mixTg", [128, 4, S], BF16)
        t_mixTg = [P.T(f"mTg{t}", mixTg) for t in range(NT)]
        ident, t_ident = sbT(st, "ident", [128, 128], F32)
        jrev, t_jrev = sbT(st, "jrev", [128, 128], F32)
        identb, t_identb = sbT(st, "identb", [128, 128], BF16)
        ones_bf, t_ones = sbT(st, "ones_bf", [128, 128], BF16)
        bd_bf, t_bd = sbT(st, "bd_bf", [128, 128], BF16)
        mhalf, t_mhalf = sbT(st, "mhalf", [128, 1], F32)
        mixg = sb(st, "mixg", [128, DEPTH, 8], F32); t_mixg = [P.T(f"mixg{l}", mixg) for l in range(DEPTH)]
        ffng = sb(st, "ffng", [128, DEPTH, 8], F32); t_ffng = [P.T(f"ffng{l}", ffng) for l in range(DEPTH)]
        attg = sb(st, "attg", [128, DEPTH, 4], F32); t_attg = [P.T(f"attg{l}", attg) for l in range(DEPTH)]
        gmg = sb(st, "gmg", [128, DEPTH, 4], F32); t_gmg = [P.T(f"gmg{l}", gmg) for l in range(DEPTH)]
        qg = sb(st, "qg", [128, DEPTH], F32); t_qg = [[P.T(f"qg{l}{h}", qg) for h in range(2)] for l in range(DEPTH)]
        qg8 = sb(st, "qg8", [128, DEPTH], F32); t_qg8 = [P.T(f"qg8{l}", qg8) for l in range(DEPTH)]
        kg = sb(st, "kg", [128, DEPTH], F32); t_kg = [[P.T(f"kg{l}{h}", kg) for h in range(2)] for l in range(DEPTH)]
        ring = [sb(st, f"ring{i}", [128, 2048], BF16) for i in range(NR)]
        t_ring = [[P.T(f"ring{i}a", ring[i]), P.T(f"ring{i}b", ring[i])] for i in range(NR)]
        pAll = st.enter_context(nc.psum_tensor("pAll", [128, 4096], F32))
        pB = pAll[:, 2048:3072]
        pC = pAll[:, 3072:4096]
        t_bank = [P.T(f"bank{i}") for i in range(8)]

        def bank(i):
            return pAll[:, i * 512:(i + 1) * 512]

        loads = []
        lidx = {}

        def add_load(key, view, src):
            lidx[key] = len(loads)
            if isinstance(src, list):
                loads.append((view, src))
            else:
                loads.append((view, [(view, src, (0, 1))]))

        v_half = lambda b: b[:].rearrange("p (c n) -> p c n", c=4)
        v_blk = lambda b: b[:, 0:1024].rearrange("p (c n) -> p c n", c=8)
        v_blk2 = lambda b: b[:].rearrange("p (c n) -> p c n", c=8)
        v_w1 = lambda b: b[:].rearrange("p (g c n) -> p g c n", g=2, c=8)
        v_w1g = lambda b: b[:, 0:1024].rearrange("p (c n) -> p c n", c=8)
        v_w1u = lambda b: b[:, 1024:2048].rearrange("p (c n) -> p c n", c=8)
        v_w2 = lambda b: b[:, 0:HF * 128].rearrange("p (f n) -> p f n", f=HF)
        for l in range(n_layers):
            win_v = win_d[l].rearrange("(c p) n -> p c n", p=128)
            for nm, c0 in (("Wu", 1536), ("Wg", 2048), ("Wv", 1024)):
                for hc in range(2):
                    add_load((l, nm, hc), v_half, win_v[:, hc * 4:(hc + 1) * 4, c0:c0 + 512])
            for fb in range(8):
                add_load((l, "Wqk", fb), v_blk, win_v[:, :, fb * 128:(fb + 1) * 128])
            wout_v = wout_d[l].rearrange("(c p) n -> p c n", p=128)
            for k in range(4):
                add_load((l, "Wo2", k), v_blk2, wout_v[:, :, k * 256:(k + 1) * 256])
            wfi_v = wfi_d[l].rearrange("(c p) (g f n) -> p c g f n", p=128, g=2, n=128)
            for half in range(2):
                for fc in range(HF):
                    f = half * HF + fc
                    add_load((l, "W1", f), v_w1, [(v_w1g, wfi_v[:, :, 0, f, :], (0,)), (v_w1u, wfi_v[:, :, 1, f, :], (1,))])
                ngs = 2 if (half == 1 and l == n_layers - 1) else 1
                for ng in range(ngs):
                    for db in range(8):
                        src = wfo_d[l][half * HF * 128:(half + 1) * HF * 128, db * 128:(db + 1) * 128].rearrange(
                            "(f p) n -> p f n", p=128)
                        add_load((l, "W2", half, ng, db), v_w2, src)
        issued = [0]

        def issue_upto(k):
            while issued[0] <= k and issued[0] < len(loads):
                j = issued[0]
                view, parts = loads[j]
                for (dfn, src, sel) in parts:
                    P.dma("pool", dfn(ring[j % NR]), src, writes=[t_ring[j % NR][k2] for k2 in sel])
                issued[0] += 1

        def done(key):
            issue_upto(lidx[key] + NR)

        def W(key):
            j = lidx[key]
            assert j < issued[0], f"load {key} not issued"
            view, _ = loads[j]
            return view(ring[j % NR]), list(t_ring[j % NR])

        P.dma("sp", ident[:], ident_d, writes=[t_ident])
        P.dma("sp", jrev[:], jrev_d, writes=[t_jrev])
        issue_upto(NR - 1)
        P.op("dve", lambda e: e.memset(ones_bf[:], 1.0), writes=[t_ones])
        P.op("dve", lambda e: e.memset(bd_bf[:], 0.0), writes=[t_bd])
        P.op("dve", lambda e: e.memset(bd_bf[0:64, 0:64], 1.0), writes=[t_bd])
        P.op("dve", lambda e: e.memset(bd_bf[64:128, 64:128], 1.0), writes=[t_bd])
        P.op("dve", lambda e: e.memset(mhalf[:], -0.5), writes=[t_mhalf])
        P.op("dve", lambda e: e.tensor_copy(identb[:], ident[:]), reads=[t_ident], writes=[t_identb])

        def load_params():
            for l in range(DEPTH):
                def colload(dst, src_t, n, c, tl):
                    src = bass.AP(src_t, l * n, [[1, 128], [128, c]])
                    P.dma("sp", dst, src, writes=[tl], allow_slow_non_contiguous=True)
                colload(mixg[:, l, :], mixg_d, D, 8, t_mixg[l])
                colload(ffng[:, l, :], ffng_d, D, 8, t_ffng[l])
                colload(attg[:, l, :], attg_d, 512, 4, t_attg[l])
                colload(gmg[:, l, :], gmg_d, 512, 4, t_gmg[l])
                for hb in range(2):
                    P.dma("sp", qg[hb * 64:(hb + 1) * 64, l:l + 1], bass.AP(qg_d, l * 64, [[1, 64], [1, 1]]),
                          writes=[t_qg[l][hb]])
                    P.dma("sp", kg[hb * 64:(hb + 1) * 64, l:l + 1], bass.AP(kg_d, l * 64, [[1, 64], [1, 1]]),
                          writes=[t_kg[l][hb]])
                P.op("dve", lambda e: e.tensor_scalar(qg8[:, l:l + 1], qg[:, l:l + 1], 0.125, None, ALU.mult),
                     reads=t_qg[l], writes=[t_qg8[l]])

        NQ = 6

        def norm_alloc(stack):
            sq, t_sq = sbN(stack, "nsq", [128, 512], BF16, NQ)
            rs, t_rs = sbN(stack, "nrs", [128, 512], F32, 2)
            return (sq, t_sq, rs, t_rs)

        def norm_block(N, gcols, t_g, l, n):
            sq, t_sq, rs, t_rs = N
            b = 6 + n % 2
            for c in range(DC):
                k = (n * DC + c) % NQ
                if c % 3 == 1:
                    P.op("pool", lambda e: e.tensor_tensor(sq[k][:], xT[:, c, blk(n)], xT[:, c, blk(n)], ALU.mult),
                         reads=[t_xT[n][c]], writes=[t_sq[k]])
                else:
                    P.op("act", lambda e: e.activation(sq[k][:], xT[:, c, blk(n)], AF.Square),
                         reads=[t_xT[n][c]], writes=[t_sq[k]])
                P.op("pe", lambda e: e.matmul(bank(b), lhsT=ones_bf[:], rhs=sq[k][:],
                                              start=(c == 0), stop=(c == DC - 1)),
                     reads=[t_ones, t_sq[k]], writes=[t_bank[b]], inc=True)
            r = n % 2
            P.op("act", lambda e: e.activation(rs[r][:], bank(b), AF.Ln, bias=EPS, scale=1.0 / D),
                 reads=[t_bank[b]], writes=[t_rs[r]])
            P.op("act", lambda e: e.activation(rs[r][:], rs[r][:], AF.Exp, scale=-0.5),
                 reads=[t_rs[r]], writes=[t_rs[r]])
            for c in range(DC):
                P.op("dve", lambda e: e.scalar_tensor_tensor(
                    out=hT[:, c, blk(n)], in0=xT[:, c, blk(n)], scalar=gcols[:, l, c:c + 1], in1=rs[r][:],
                    op0=ALU.mult, op1=ALU.mult),
                    reads=[t_xT[n][c], t_g, t_rs[r]], writes=[t_hT[n][c]])

        out_tk = []
        out_ctx = {}

        def out_tiles(t0_, t1_):
            yo, t_yo, NY = out_ctx["yo"], out_ctx["t_yo"], out_ctx["NY"]
            for tt in range(t0_, t1_):
                sl_ = tt % NY
                for half in range(2):
                    b = 2 * (tt % 2) + half
                    for cc in range(4):
                        c = half * 4 + cc
                        P.op("pe", lambda e: e.transpose(
                            bank(b)[:, cc * 128:(cc + 1) * 128], xT[:, c, tile_(tt)], ident[:]),
                            reads=[t_xT[tt // 4][c], t_ident], writes=[t_bank[b]], inc=(cc == 3))
                    if half == 0:
                        P.op("act", lambda e: e.copy(yo[sl_][:, 0:512], bank(b)), reads=[t_bank[b]], writes=[t_yo[sl_][0]])
                    else:
                        P.op("dve", lambda e: e.tensor_copy(yo[sl_][:, 512:1024], bank(b)), reads=[t_bank[b]], writes=[t_yo[sl_][1]])
                out_tk.append(P.dma("sp", y_d[tt * 128:(tt + 1) * 128, :], yo[sl_][:], reads=t_yo[sl_], owner=t_yo[sl_][0]))

        for l in range(n_layers):
            with ExitStack() as sl:
                sgug, t_sgug = sbT(sl, "sgug", [128, 512], F32)
                wnat, t_wnat = sbT(sl, "wnat", [128, 8, 128], F32)
                wT, t_wT = sbT(sl, "wT", [128, 8, 128], BF16)
                bnat, t_bnat = sbT(sl, "bnat", [8, 128], F32)
                bhi, t_bhi = sbT(sl, "bhi", [128, 128], BF16)
                blo, t_blo = sbT(sl, "blo", [128, 128], BF16)
                gindb, t_gindb = sbT(sl, "gindb", [128, 512], BF16)
                P.dma("pool", gindb[:], gind_d, writes=[t_gindb])
                P.dma("sp", sgug[:], bass.AP(sgug_d, l * 512, [[0, 128], [1, 512]]), writes=[t_sgug])
                P.dma("sp", wnat[:], wsp_d[l].rearrange("g t s -> t g s"), writes=[t_wnat])
                P.dma("sp", bnat[:], bsp_d[l], writes=[t_bnat])
                P.op("dve", lambda e: e.memset(bhi[:], 0.0), writes=[t_bhi])
                P.op("dve", lambda e: e.memset(blo[:], 0.0), writes=[t_blo])
                P.op("dve", lambda e: e.tensor_copy(bhi[0:8, :], bnat[:]), reads=[t_bnat], writes=[t_bhi])
                P.op("dve", lambda e: e.tensor_tensor(blo[0:8, :], bnat[:], bhi[0:8, :], ALU.subtract),
                     reads=[t_bnat, t_bhi], writes=[t_blo])

                if l == 0:
                    with ExitStack() as s0:
                        NX = 3
                        xin, t_xin = sbN(s0, "xin", [128, D], F32, NX)
                        N0 = norm_alloc(s0)
                        for tt in range(NT):
                            sl_ = tt % NX
                            P.dma("sp", xin[sl_][:], x_d[tt * 128:(tt + 1) * 128, :], writes=[t_xin[sl_]])
                            if tt == 1:
                                load_params()
                            for half in range(2):
                                b = 2 * sl_ + half
                                for cc in range(4):
                                    c = half * 4 + cc
                                    P.op("pe", lambda e: e.transpose(
                                        bank(b)[:, cc * 128:(cc + 1) * 128], xin[sl_][:, c * 128:(c + 1) * 128], ident[:]),
                                        reads=[t_xin[sl_], t_ident], writes=[t_bank[b]], inc=(cc == 3))
                                dst = xT[:, half * 4:half * 4 + 4, tile_(tt)]
                                src = bank(b).rearrange("p (c t) -> p c t", c=4)
                                tw = t_xT[tt // 4][half * 4:half * 4 + 4]
                                if half == 0:
                                    P.op("act", lambda e: e.copy(dst, src), reads=[t_bank[b]], writes=tw)
                                else:
                                    P.op("dve", lambda e: e.tensor_copy(dst, src), reads=[t_bank[b]], writes=tw)
                            if tt % 4 == 3 and tt >= 7:
                                norm_block(N0, mixg, t_mixg[l], l, tt // 4 - 1)
                        norm_block(N0, mixg, t_mixg[l], l, NB - 1)
                        P.retire(t_xin + N0[1] + N0[3])

                for hb in range(2):
                    for gg in range(4):
                        g = hb * 4 + gg
                        P.op("pe", lambda e: e.transpose(
                            bank(4 + hb)[:, gg * 128:(gg + 1) * 128], wnat[:, g, :], ident[:]),
                            reads=[t_wnat, t_ident], writes=[t_bank[4 + hb]], inc=(gg == 3))
                    P.op("dve", lambda e: e.tensor_copy(
                        wT[:, hb * 4:hb * 4 + 4, :], bank(4 + hb).rearrange("p (g t) -> p g t", g=4)),
                        reads=[t_bank[4 + hb]], writes=[t_wT])
                P.op("dve", lambda e: e.memset(wT[64:128, :, 0:64], 0.0), writes=[t_wT])

                if l > 0:
                    with ExitStack() as sn:
                        Nn = norm_alloc(sn)
                        for n in range(NB):
                            norm_block(Nn, mixg, t_mixg[l], l, n)
                        P.retire(Nn[1] + Nn[3])
                if "hT" in debug and l == dbg_l:
                    dump("hT", hT, flat(t_hT), [128, DC, S], BF16)

                with ExitStack() as sg:
                    NS = 7
                    u_sb, t_u = sbN(sg, "u_sb", [128, 512], F32, NS)
                    vgg, t_vgg = sbN(sg, "vgg", [128, 512], F32, NS)
                    vgn, t_vgn = sbN(sg, "vgn", [128, 512], BF16, NS)
                    gm, t_gm = sbN(sg, "gm", [128, 512], F32, NS)
                    junk, t_junk = sbT(sg, "gjunk", [128, 512], BF16)
                    st4 = [sb(sg, f"gst{i}", [128, 4], F32) for i in range(NS)]
                    t_st = [[P.T(f"gst{i}{j}", st4[i]) for j in range(4)] for i in range(NS)]
                    Wu = [W((l, "Wu", hc)) for hc in range(2)]
                    Wg = [W((l, "Wg", hc)) for hc in range(2)]

                    def gA(tt):
                        bU, bG = 2 * (tt % 2), 2 * (tt % 2) + 1
                        n = tt // 4
                        for (Wx, bX) in ((Wg, bG), (Wu, bU)):
                            for c in range(DC):
                                wv, wt = Wx[c // 4]
                                P.op("pe", lambda e: e.matmul(
                                    bank(bX), lhsT=hT[:, c, tile_(tt)], rhs=wv[:, c % 4, :], start=(c == 0), stop=(c == DC - 1)),
                                    reads=[t_hT[n][c]] + wt, writes=[t_bank[bX]], inc=(c == DC - 1))

                    def rstd_chain(stt, t_s, i0):
                        P.op("pool", lambda e: e.tensor_scalar(stt[:, i0:i0 + 1], stt[:, i0:i0 + 1], 1.0 / 512, EPS, ALU.mult, ALU.add),
                             reads=[t_s[i0]], writes=[t_s[i0]])
                        P.op("pool", lambda e: e.tensor_tensor(stt[:, i0 + 1:i0 + 2], stt[:, i0:i0 + 1], mhalf[:], ALU.pow),
                             reads=[t_s[i0], t_mhalf], writes=[t_s[i0 + 1]])

                    def gB(tt):
                        s_ = tt % NS
                        bU, bG = 2 * (tt % 2), 2 * (tt % 2) + 1
                        P.op("act", lambda e: e.activation(vgg[s_][:], bank(bG), AF.Gelu_apprx_tanh),
                             reads=[t_bank[bG]], writes=[t_vgg[s_]])
                        P.op("act", lambda e: e.activation(junk[:], vgg[s_][:], AF.Square, accum_out=st4[s_][:, 0:1]),
                             reads=[t_vgg[s_]], writes=[t_junk, t_st[s_][0]])
                        rstd_chain(st4[s_], t_st[s_], 0)
                        P.op("act", lambda e: e.activation(u_sb[s_][:], bank(bU), AF.Gelu_apprx_tanh),
                             reads=[t_bank[bU]], writes=[t_u[s_]])
                        P.op("dve", lambda e: e.scalar_tensor_tensor(
                            out=vgn[s_][:], in0=vgg[s_][:], scalar=st4[s_][:, 1:2], in1=sgug[:], op0=ALU.mult, op1=ALU.mult),
                            reads=[t_vgg[s_], t_st[s_][1], t_sgug], writes=[t_vgn[s_]])

                    def gC(tt):
                        s_ = tt % NS
                        bM = 4 + tt % 2
                        P.op("pe", lambda e: e.matmul(bank(bM), lhsT=bhi[:], rhs=gindb[:], start=True, stop=False),
                             reads=[t_bhi, t_gindb], writes=[t_bank[bM]], inc=False)
                        P.op("pe", lambda e: e.matmul(bank(bM), lhsT=blo[:], rhs=gindb[:], start=False, stop=False),
                             reads=[t_blo, t_gindb], writes=[t_bank[bM]], inc=False)
                        for g in range(8):
                            P.op("pe", lambda e: e.matmul(
                                bank(bM)[:, g * 64:(g + 1) * 64], lhsT=wT[:, g, :], rhs=vgn[s_][:, g * 64:(g + 1) * 64],
                                start=False, stop=(g == 7)),
                                reads=[t_wT, t_vgn[s_]], writes=[t_bank[bM]], inc=(g == 7))

                    def gD(tt):
                        s_ = tt % NS
                        bM = 4 + tt % 2
                        P.op("dve", lambda e: e.tensor_tensor(gm[s_][:], bank(bM), u_sb[s_][:], ALU.mult),
                             reads=[t_bank[bM], t_u[s_]], writes=[t_gm[s_]])
                        P.op("act", lambda e: e.activation(junk[:], gm[s_][:], AF.Square, accum_out=st4[s_][:, 2:3]),
                             reads=[t_gm[s_]], writes=[t_junk, t_st[s_][2]])
                        rstd_chain(st4[s_], t_st[s_], 2)
                        P.op("act", lambda e: e.activation(gm[s_][:], gm[s_][:], AF.Copy, scale=st4[s_][:, 3:4]),
                             reads=[t_gm[s_], t_st[s_][3]], writes=[t_gm[s_]])

                    def gE(tt):
                        s_ = tt % NS
                        bTr = 6 + tt % 2
                        for j in range(4):
                            P.op("pe", lambda e: e.transpose(
                                bank(bTr)[:, j * 128:(j + 1) * 128], gm[s_][:, j * 128:(j + 1) * 128], ident[:]),
                                reads=[t_gm[s_], t_ident], writes=[t_bank[bTr]], inc=(j == 3))
                        P.op("dve", lambda e: e.tensor_tensor(
                            mixTg[:, :, tile_(tt)], bank(bTr).rearrange("p (j t) -> p j t", j=4),
                            gmg[:, l, :].rearrange("p (j o) -> p j o", o=1).to_broadcast([128, 4, 128]), ALU.mult),
                            reads=[t_bank[bTr], t_gmg[l]], writes=[t_mixTg[tt]])

                    for s_ in range(NT + 6):
                        if 0 <= s_ - 3 < NT:
                            gC(s_ - 3); gD(s_ - 3)
                        if 0 <= s_ - 6 < NT:
                            gE(s_ - 6)
                        if s_ < NT:
                            gA(s_); gB(s_)
                    for hc in range(2):
                        done((l, "Wu", hc)); done((l, "Wg", hc))
                    P.retire([t_junk] + t_u + t_vgg + t_vgn + t_gm + flat(t_st))
                P.retire([t_sgug, t_wnat, t_wT, t_bnat, t_bhi, t_blo, t_gindb])
            if "mixTg" in debug and l == dbg_l:
                dump("mixTg", mixTg, t_mixTg, [128, 4, S], BF16)

            with ExitStack() as sa:
                bias = sb(sa, "expB", [128, 4, 5, 2, 128], BF16)
                t_bias = [P.T(f"expB{h}", bias) for h in range(8)]
                V = sb(sa, "V", [128, NT, 8, 65], BF16)
                t_V = [P.T(f"V{t}", V) for t in range(NT)]
                qT = sb(sa, "qT", [128, 4, S], BF16)
                kT = sb(sa, "kT", [128, 4, S], BF16)
                t_qT = [[P.T(f"qT{f}{n}", qT) for n in range(NB)] for f in range(4)]
                t_kT = [[P.T(f"kT{f}{n}", kT) for n in range(NB)] for f in range(4)]
                P.op("dve", lambda e: e.memset(V[:, :, :, 64:65], 1.0), writes=t_V)

                with ExitStack() as shk:
                    hk, t_hk = sbN(shk, "hk", [128, 640], F32, 2)

                    def bias_head(h):
                        k = h % 2
                        src = bass.AP(tabx_d, (l * 8 + h) * TABX + 1, [[1, 128], [128, 5], [1, 128]])
                        P.dma("sp", hk[k][:].rearrange("p (d q) -> p d q", d=5), src, writes=[t_hk[k]])
                        b0 = 4 + 2 * k
                        pX = pB if k == 0 else pC
                        P.op("pe", lambda e: e.matmul(pX[:, 0:512], lhsT=jrev[:], rhs=hk[k][:, 0:512], start=True, stop=True),
                             reads=[t_jrev, t_hk[k]], writes=[t_bank[b0]])
                        P.op("pe", lambda e: e.matmul(pX[:, 512:640], lhsT=jrev[:], rhs=hk[k][:, 512:640], start=True, stop=True),
                             reads=[t_jrev, t_hk[k]], writes=[t_bank[b0 + 1]])
                        P.op("act", lambda e: e.activation(bias[:, h // 2, :, h % 2, :],
                                                          pX[:, 0:640].rearrange("p (d q) -> p d q", d=5), AF.Exp),
                             reads=[t_bank[b0], t_bank[b0 + 1]], writes=[t_bias[h]])
                        P.op("dve", lambda e: e.memset(bias[64:128, h // 2, 0, h % 2, 0:64], 0.0), writes=[t_bias[h]])
                        P.op("dve", lambda e: e.memset(bias[0:64, h // 2, 4, h % 2, 64:128], 0.0), writes=[t_bias[h]])

                    Wv = [W((l, "Wv", hc)) for hc in range(2)]
                    for tt in range(NT):
                        b = tt % 4
                        n = tt // 4
                        for c in range(DC):
                            wv, wt = Wv[c // 4]
                            P.op("pe", lambda e: e.matmul(
                                bank(b), lhsT=hT[:, c, tile_(tt)], rhs=wv[:, c % 4, :], start=(c == 0), stop=(c == DC - 1)),
                                reads=[t_hT[n][c]] + wt, writes=[t_bank[b]], inc=(c == DC - 1))
                        src = bank(b).rearrange("p (h d) -> p h d", h=8)
                        if tt % 2 == 0:
                            P.op("act", lambda e: e.copy(V[:, tt, :, 0:64], src), reads=[t_bank[b]], writes=[t_V[tt]])
                        else:
                            P.op("dve", lambda e: e.tensor_copy(V[:, tt, :, 0:64], src), reads=[t_bank[b]], writes=[t_V[tt]])
                        if tt % 2 == 1:
                            bias_head(tt // 2)
                    for hc in range(2):
                        done((l, "Wv", hc))
                    P.retire(t_hk)

                with ExitStack() as sq_:
                    NQK = 3
                    qsq, t_qsq = sbN(sq_, "qsq", [128, 512], BF16, NQK)
                    qrs, t_qrs = sbN(sq_, "qrs", [128, 512], F32, NQK)
                    NIT = 32

                    def qA(it):
                        fb, n = it // 4, it % 4
                        bQ = it % 3
                        wv, wt = W((l, "Wqk", fb))
                        for c in range(DC):
                            P.op("pe", lambda e: e.matmul(
                                bank(bQ), lhsT=wv[:, c, :], rhs=hT[:, c, blk(n)], start=(c == 0), stop=(c == DC - 1)),
                                reads=wt + [t_hT[n][c]], writes=[t_bank[bQ]], inc=(c == DC - 1))
                        if n == 3:
                            done((l, "Wqk", fb))
                        r = it % NQK
                        P.op("act", lambda e: e.activation(qsq[r][:], bank(bQ), AF.Square),
                             reads=[t_bank[bQ]], writes=[t_qsq[r]])

                    def qC(it):
                        fb, n = it // 4, it % 4
                        bQ = it % 3
                        bS = 4 + it % 2
                        r = it % NQK
                        isq = fb < 4
                        dstT = qT if isq else kT
                        t_dst = t_qT if isq else t_kT
                        gcol = qg8 if isq else kg
                        t_gc = [t_qg8[l]] if isq else t_kg[l]
                        P.op("pe", lambda e: e.matmul(bank(bS), lhsT=bd_bf[:], rhs=qsq[r][:], start=True, stop=True),
                             reads=[t_bd, t_qsq[r]], writes=[t_bank[bS]])
                        P.op("act", lambda e: e.activation(qrs[r][:], bank(bS), AF.Ln, bias=EPS, scale=1.0 / 64),
                             reads=[t_bank[bS]], writes=[t_qrs[r]])
                        P.op("act", lambda e: e.activation(qrs[r][:], qrs[r][:], AF.Exp, scale=-0.5),
                             reads=[t_qrs[r]], writes=[t_qrs[r]])
                        P.op("dve", lambda e: e.scalar_tensor_tensor(
                            out=dstT[:, fb % 4, blk(n)], in0=bank(bQ), scalar=gcol[:, l:l + 1], in1=qrs[r][:],
                            op0=ALU.mult, op1=ALU.mult),
                            reads=[t_bank[bQ]] + t_gc + [t_qrs[r]], writes=[t_dst[fb % 4][n]])

                    for s_ in range(NIT + 1):
                        if s_ < NIT:
                            qA(s_)
                        if s_ - 1 >= 0:
                            qC(s_ - 1)
                    P.retire(t_qsq + t_qrs)
                if "qT" in debug and l == dbg_l:
                    dump("qT", qT, flat(t_qT), [128, 4, S], BF16)
                    dump("kT", kT, flat(t_kT), [128, 4, S], BF16)
                    dump("V", V, t_V, [128, NT, 8, 65], BF16)

                P.retire(flat(t_hT))
                t_mixTa = [P.T(f"mTa{t}", hT) for t in range(NT)]
                mixTa = hT

                with ExitStack() as sat:
                    NQM = 3
                    qm, t_qm = sbN(sat, "qm", [128, 256], BF16, NQM)
                    NEP = 3
                    EP, t_EP = sbN(sat, "EP", [128, 1280], BF16, NEP)
                    rc, t_rc = sbT(sat, "rc", [128, 8], F32)
                    atok, t_atok = sbT(sat, "atok", [128, 512], F32)
                    anb, t_anb = sbT(sat, "anb", [128, 512], BF16)
                    ast = sb(sat, "ast", [128, 2], F32)
                    t_ast = [P.T(f"ast{j}", ast) for j in range(2)]
                    for i in range(NQM):
                        P.op("pool", lambda e: e.memset(qm[i][:], 0.0), writes=[t_qm[i]])

                    P.retire(t_bank[0:6])
                    t_S = [P.T(f"S{i}") for i in range(2)]
                    NP = NT * 4
                    trb = pAll[:, 7 * 512 + 128:7 * 512 + 384].bitcast(BF16)

                    def aQ(p):
                        m, hp = p // 4, p % 4
                        k = p % NQM
                        P.op("pool", lambda e: e.tensor_copy(qm[k][0:64, 0:128], qT[0:64, hp, tile_(m)]),
                             reads=[t_qT[hp][m // 4]], writes=[t_qm[k]])
                        P.op("pool", lambda e: e.tensor_copy(qm[k][64:128, 128:256], qT[64:128, hp, tile_(m)]),
                             reads=[t_qT[hp][m // 4]], writes=[t_qm[k]])

                    def aA(p):
                        m, hp = p // 4, p % 4
                        nb_ = min(m, 4) + 1
                        st_ = p % 2
                        c0 = st_ * 1536
                        k = p % NQM
                        for d in range(nb_):
                            j = m - d
                            P.op("pe", lambda e: e.matmul(
                                pAll[:, c0 + d * 256:c0 + (d + 1) * 256], lhsT=kT[:, hp, tile_(j)], rhs=qm[k][:],
                                start=True, stop=True),
                                reads=[t_kT[hp][j // 4], t_qm[k]], writes=[t_S[st_]], inc=(d == nb_ - 1))
                        w = nb_ * 256
                        ep = p % NEP
                        P.op("act", lambda e: e.activation(EP[ep][:, 0:w], pAll[:, c0:c0 + w], AF.Exp),
                             reads=[t_S[st_]], writes=[t_EP[ep]])
                        P.op("dve", lambda e: e.tensor_tensor(
                            EP[ep][:, 0:w], EP[ep][:, 0:w], bias[:, hp, 0:nb_, :, :].rearrange("p d u q -> p (d u q)"), ALU.mult),
                            reads=[t_EP[ep], t_bias[2 * hp], t_bias[2 * hp + 1]], writes=[t_EP[ep]])

                    deferred = {}

                    def aC(p, step):
                        m, hp = p // 4, p % 4
                        nb_ = min(m, 4) + 1
                        ep = p % NEP
                        for u in range(2):
                            h = 2 * hp + u
                            ob = 6 if h < 7 else 7
                            oc = (6 * 512 + h * 65) if h < 7 else 7 * 512
                            for d in range(nb_):
                                j = m - d
                                P.op("pe", lambda e: e.matmul(
                                    pAll[:, oc:oc + 65], lhsT=EP[ep][:, (d * 2 + u) * 128:(d * 2 + u + 1) * 128], rhs=V[:, j, h, :],
                                    start=(d == 0), stop=(d == nb_ - 1)),
                                    reads=[t_EP[ep], t_V[j]], writes=[t_bank[ob]], inc=(d == nb_ - 1))
                        if hp == 3:
                            aD1(m)
                            deferred.setdefault(step + 1, []).append((aD2, m))
                            deferred.setdefault(step + 2, []).append((aD3, m))
                            deferred.setdefault(step + 3, []).append((aE, m))

                    def aD1(m):
                        ov6 = pAll[:, 6 * 512:6 * 512 + 455].rearrange("p (h d) -> p h d", h=7)
                        ov7 = pAll[:, 7 * 512:7 * 512 + 65]
                        P.op("dve", lambda e: e.reciprocal(rc[:, 0:7].rearrange("p (h o) -> p h o", o=1), ov6[:, :, 64:65]),
                             reads=[t_bank[6]], writes=[t_rc])
                        P.op("dve", lambda e: e.reciprocal(rc[:, 7:8], ov7[:, 64:65]), reads=[t_bank[7]], writes=[t_rc])
                        P.op("dve", lambda e: e.tensor_tensor(
                            atok[:, 0:448].rearrange("p (h d) -> p h d", h=7), ov6[:, :, 0:64],
                            rc[:, 0:7].rearrange("p (h o) -> p h o", o=1).to_broadcast([128, 7, 64]), ALU.mult),
                            reads=[t_bank[6], t_rc], writes=[t_atok])
                        P.op("dve", lambda e: e.tensor_scalar(atok[:, 448:512], ov7[:, 0:64], rc[:, 7:8], None, ALU.mult),
                             reads=[t_bank[7], t_rc], writes=[t_atok])

                    def aD2(m):
                        P.op("act", lambda e: e.activation(anb[:], atok[:], AF.Square, accum_out=ast[:, 0:1]),
                             reads=[t_atok], writes=[t_anb, t_ast[0]])
                        P.op("pool", lambda e: e.tensor_scalar(ast[:, 0:1], ast[:, 0:1], 1.0 / 512, EPS, ALU.mult, ALU.add),
                             reads=[t_ast[0]], writes=[t_ast[0]])
                        P.op("pool", lambda e: e.tensor_tensor(ast[:, 1:2], ast[:, 0:1], mhalf[:], ALU.pow),
                             reads=[t_ast[0], t_mhalf], writes=[t_ast[1]])

                    def aD3(m):
                        P.op("act", lambda e: e.activation(anb[:], atok[:], AF.Copy, scale=ast[:, 1:2]),
                             reads=[t_atok, t_ast[1]], writes=[t_anb])

                    def aE(m):
                        for j in range(4):
                            P.op("pe", lambda e: e.transpose(
                                trb[:, j * 128:(j + 1) * 128], anb[:, j * 128:(j + 1) * 128], identb[:]),
                                reads=[t_anb, t_identb], writes=[t_bank[7]], inc=(j == 3))
                        P.op("dve", lambda e: e.tensor_tensor(
                            mixTa[:, 0:4, tile_(m)], trb.rearrange("p (j t) -> p j t", j=4),
                            attg[:, l, :].rearrange("p (j o) -> p j o", o=1).to_broadcast([128, 4, 128]), ALU.mult),
                            reads=[t_bank[7], t_attg[l]], writes=[t_mixTa[m]])

                    aQ(0); aQ(1)
                    for s_ in range(NP + 6):
                        if s_ + 2 < NP:
                            aQ(s_ + 2)
                        if s_ < NP:
                            aA(s_)
                        if 0 <= s_ - 1 < NP:
                            aC(s_ - 1, s_)
                        for (fn_, m) in deferred.pop(s_, []):
                            fn_(m)
                    assert not deferred
                    P.retire(t_S)
                    t_bank[0:6] = [P.T(f"bank{i}") for i in range(6)]
                    P.retire(t_qm + t_EP + [t_rc, t_atok, t_anb] + t_ast)
                P.retire(t_V + flat(t_qT) + flat(t_kT) + t_bias)
            if "mixTa" in debug and l == dbg_l:
                dump("mixTa", hT, t_mixTa, [128, DC, S], BF16)

            with ExitStack() as sn:
                Nn = norm_alloc(sn)
                Wo2 = [W((l, "Wo2", k)) for k in range(4)]
                new_hT = [None] * NB
                it = 0
                for n in range(NB + 1):
                    if n < NB:
                        for db in range(8):
                            wv, wt = Wo2[db // 2]
                            b = it % 4
                            it += 1
                            for c in range(DC):
                                rhs = mixTa[:, c, blk(n)] if c < 4 else mixTg[:, c - 4, blk(n)]
                                tr = (t_mixTa if c < 4 else t_mixTg)[4 * n:4 * n + 4]
                                P.op("pe", lambda e: e.matmul(
                                    bank(b), lhsT=wv[:, c, (db % 2) * 128:(db % 2) * 128 + 128], rhs=rhs,
                                    start=(c == 0), stop=(c == DC - 1)),
                                    reads=wt + tr, writes=[t_bank[b]], inc=(c == DC - 1))
                            P.op("dve", lambda e: e.tensor_tensor(xT[:, db, blk(n)], xT[:, db, blk(n)], bank(b), ALU.add),
                                 reads=[t_bank[b], t_xT[n][db]], writes=[t_xT[n][db]])
                        P.retire(t_mixTa[4 * n:4 * n + 4])
                        t_hT[n] = [P.T(f"hT{n}_{c}", hT) for c in range(DC)]
                    if n >= 1:
                        norm_block(Nn, ffng, t_ffng[l], l, n - 1)
                for k in range(4):
                    done((l, "Wo2", k))
                P.retire(Nn[1] + Nn[3])
            if "x1" in debug and l == dbg_l:
                dump("x1", xT, flat(t_xT), [128, DC, S], F32)

            with ExitStack() as sf:
                actT = sb(sf, "actT", [128, HF, S], BF16)
                t_actT = [[P.T(f"actT{n}_{f}", actT) for f in range(HF)] for n in range(NB)]
                NSG = 3
                sg_, t_sg = sbN(sf, "sg", [128, 512], F32, NSG)
                last = (l == n_layers - 1)
                if last:
                    NY = 3
                    yo = [sb(sf, f"yo{i}", [128, D], F32) for i in range(NY)]
                    t_yo = [[P.T(f"yo{i}_{h}", yo[i]) for h in range(2)] for i in range(NY)]
                    out_ctx.update(yo=yo, t_yo=t_yo, NY=NY)
                it = 0
                it2 = 0
                for half in range(2):
                    for fc in range(HF):
                        f = half * HF + fc
                        wv, wt = W((l, "W1", f))
                        for n in range(NB):
                            bA = 2 * (it % 2)
                            bB = bA + 1
                            r = it % NSG
                            it += 1
                            for gi, bX in ((0, bA), (1, bB)):
                                for c in range(DC):
                                    P.op("pe", lambda e: e.matmul(
                                        bank(bX), lhsT=wv[:, gi, c, :], rhs=hT[:, c, blk(n)], start=(c == 0), stop=(c == DC - 1)),
                                        reads=wt + [t_hT[n][c]], writes=[t_bank[bX]], inc=(c == DC - 1))
                            P.op("act", lambda e: e.activation(sg_[r][:], bank(bA), AF.Silu),
                                 reads=[t_bank[bA]], writes=[t_sg[r]])
                            P.op("dve", lambda e: e.tensor_tensor(actT[:, fc, blk(n)], sg_[r][:], bank(bB), ALU.mult),
                                 reads=[t_sg[r], t_bank[bB]], writes=[t_actT[n][fc]])
                        done((l, "W1", f))
                    ngs = 2 if (half == 1 and last) else 1
                    for ng in range(ngs):
                        nlist = list(range(NB)) if ngs == 1 else [2 * ng, 2 * ng + 1]
                        for db in range(8):
                            wv, wt = W((l, "W2", half, ng, db))
                            for n in nlist:
                                b = 4 + it2 % 4
                                it2 += 1
                                for fc in range(HF):
                                    P.op("pe", lambda e: e.matmul(
                                        bank(b), lhsT=wv[:, fc, :], rhs=actT[:, fc, blk(n)], start=(fc == 0), stop=(fc == HF - 1)),
                                        reads=wt + [t_actT[n][fc]], writes=[t_bank[b]], inc=(fc == HF - 1))
                                P.op("dve", lambda e: e.tensor_tensor(xT[:, db, blk(n)], xT[:, db, blk(n)], bank(b), ALU.add),
                                     reads=[t_bank[b], t_xT[n][db]], writes=[t_xT[n][db]])
                            done((l, "W2", half, ng, db))
                            if ngs == 2 and ng == 1:
                                out_tiles(db, db + 1)
                if last:
                    out_tiles(8, NT)
                P.retire(flat(t_actT) + t_sg)

        if True:
            P.wait_all("sp", out_tk + dbg_tk)
            P.replay()
    return nc


_NC_CACHE = {}


_GIND = np.zeros((128, 512), dtype=np.float32)
for _g in range(8):
    _GIND[_g, _g * 64:(_g + 1) * 64] = 1.0


def _host_inputs(inputs):
    f32 = lambda a: np.ascontiguousarray(np.asarray(a, dtype=np.float32))
    rel = f32(inputs["rel_bias"])
    tabx = np.concatenate([rel, np.repeat(rel[..., -1:], TABX - rel.shape[-1], axis=-1)], axis=-1)
    shared = {
        "mix_norm_g": f32(inputs["mix_norm_g"]), "w_in": f32(inputs["w_in"]),
        "q_norm_g": f32(inputs["q_norm_g"]), "k_norm_g": f32(inputs["k_norm_g"]),
        "tabx": np.ascontiguousarray(tabx), "sgu_norm_g": f32(inputs["sgu_norm_g"]),
        "w_spatial": f32(inputs["w_spatial"]), "b_spatial": f32(inputs["b_spatial"]),
        "att_out_norm_g": f32(inputs["att_out_norm_g"]), "gmlp_out_norm_g": f32(inputs["gmlp_out_norm_g"]),
        "w_out": f32(inputs["w_out"]), "ffn_norm_g": f32(inputs["ffn_norm_g"]),
        "w_ffn_in": f32(inputs["w_ffn_in"]), "w_ffn_out": f32(inputs["w_ffn_out"]),
        "ident": np.eye(128, dtype=np.float32),
        "jrev": np.ascontiguousarray(np.eye(128, dtype=np.float32)[:, ::-1]),
        "gind": _GIND,
    }
    return shared


def kernel(**inputs):
    x = np.asarray(inputs["x"], dtype=np.float32)
    B = x.shape[0]
    shared = _host_inputs(inputs)
    if "nc" not in _NC_CACHE:
        _NC_CACHE["nc"] = build_nc()
    nc = _NC_CACHE["nc"]
    in_maps = [dict(shared, x=np.ascontiguousarray(x[b])) for b in range(B)]
    res = run_bass_kernel_spmd(nc, in_maps, core_ids=list(range(B)))
    return np.stack([np.asarray(r["y"], dtype=np.float32) for r in res.results], axis=0)
```

```python
import numpy as np
from contextlib import ExitStack
import concourse.bass as bass
import concourse.mybir as mybir
from concourse.bass_utils import run_bass_kernel_spmd

F32 = mybir.dt.float32
BF16 = mybir.dt.bfloat16
ALU = mybir.AluOpType
AF = mybir.ActivationFunctionType

ENGS = ("pe", "act", "dve", "pool", "sp")

S = 2048
D = 1024
NT = 16
NB = 4
DC = 8
DEPTH = 2
DFF = 2816
FC = 22
EPS = 1e-6
NEG = -30000.0
TABX = 768
SAME_ENGINE_WAW = True
NR = 5


class T:
    __slots__ = ("name", "last_w", "readers", "dsem", "dcount", "rng")

    def __init__(self, name, prog=None, rng=None):
        self.name = name
        self.last_w = None
        self.rng = rng
        self.readers = []
        if prog is not None:
            if rng is None:
                self.readers = list(prog.floor.items())
            else:
                m = {}
                for (lo, hi, tk) in prog.retired:
                    if lo < rng[1] and rng[0] < hi:
                        for s_, v in tk.items():
                            if m.get(s_, 0) < v:
                                m[s_] = v
                self.readers = list(m.items())
        self.dsem = None
        self.dcount = 0


class _Rec:
    def __init__(self):
        self.call = None

    def __getattr__(self, name):
        def f(*a, **k):
            self.call = (name, a, k)
        return f


class Prog:
    def __init__(self, nc, stack):
        self.nc = nc
        self.stack = stack
        self.ops = {e: [] for e in ENGS}
        self.sem = {e: stack.enter_context(nc.semaphore("s_" + e)) for e in ENGS}
        self.n = {e: 0 for e in ENGS}
        self.waited = {e: {} for e in ENGS}
        self.semkey = {self.sem[e]: e for e in ENGS}
        self.nd = 0
        self.pending = {e: [] for e in ENGS}
        self.floor = {}
        self.retired = []

    def T(self, name, buf=None):
        rng = None
        if buf is not None:
            ml = self.nc.lookup_mloc(buf)
            rng = (int(ml.addr), int(ml.addr) + int(ml.dims[1]))
        return T(name, self, rng)

    def retire(self, tiles):
        for t in tiles:
            tks = list(t.readers)
            if t.last_w is not None:
                tks.append(t.last_w)
            if t.rng is None:
                for s, v in tks:
                    if self.floor.get(s, 0) < v:
                        self.floor[s] = v
            else:
                d = {}
                for s, v in tks:
                    if d.get(s, 0) < v:
                        d[s] = v
                if d:
                    self.retired.append((t.rng[0], t.rng[1], d))

    def _collect(self, eng, reads, writes):
        w = {}
        own = self.sem.get(eng)

        def add(tk, raw):
            if tk is None:
                return
            s, v = tk
            if s is own and not raw and not SAME_ENGINE_WAW:
                return
            if w.get(s, 0) < v:
                w[s] = v
        for t in reads:
            add(t.last_w, True)
        for t in writes:
            add(t.last_w, False)
            for r in t.readers:
                add(r, False)
        need = []
        for s, v in w.items():
            if self.waited[eng].get(s, 0) >= v:
                continue
            if s is own and eng == "pe":
                continue
            if s in self.semkey:
                e2 = self.semkey[s]
                assert v <= self.n[e2], f"wait on unrecorded inc {e2} {v}>{self.n[e2]}"
            self.waited[eng][s] = v
            need.append((s, v))
        return need

    def op(self, eng, fn, reads=(), writes=(), inc=True):
        rec = _Rec()
        fn(rec)
        name_, a_, k_ = rec.call
        fn = lambda e: getattr(e, name_)(*a_, **k_)
        need = self._collect(eng, reads, writes)
        if inc:
            self.n[eng] += 1
            tk = (self.sem[eng], self.n[eng])
            for t, isw in self.pending[eng]:
                if isw:
                    t.last_w = tk
                    t.readers = []
                else:
                    t.readers.append(tk)
            self.pending[eng] = []
            for t in reads:
                t.readers.append(tk)
            for t in writes:
                t.last_w = tk
                t.readers = []
        else:
            for t in reads:
                self.pending[eng].append((t, False))
            for t in writes:
                self.pending[eng].append((t, True))
        self.ops[eng].append((need, fn, inc, None))

    def dma(self, eng, out, in_, reads=(), writes=(), owner=None, **kw):
        need = self._collect(eng, reads, writes)
        if owner is None:
            owner = writes[0] if writes else reads[0]
        if owner.dsem is None:
            owner.dsem = self.stack.enter_context(self.nc.semaphore(f"d{self.nd}"))
            self.nd += 1
        owner.dcount += 1
        tk = (owner.dsem, 16 * owner.dcount)
        for t in reads:
            t.readers.append(tk)
        for t in writes:
            t.last_w = tk
            t.readers = []
        fn = lambda e: e.dma_start(out=out, in_=in_, **kw)
        self.ops[eng].append((need, fn, False, (owner.dsem, 16)))
        return tk

    def wait_all(self, eng, tickets):
        need = []
        for s, v in tickets:
            if self.waited[eng].get(s, 0) < v:
                self.waited[eng][s] = v
                need.append((s, v))
        self.ops[eng].append((need, None, False, None))

    def replay(self):
        nc = self.nc
        for e in ENGS:
            assert not self.pending[e], f"pending tiles on {e}"
        with nc.Block() as block:
            def run(name):
                def f(e):
                    for need, fn, inc, dinc in self.ops[name]:
                        for s, v in need:
                            e.wait_ge(s, v)
                        if fn is None:
                            continue
                        ins = fn(e)
                        if inc:
                            ins.then_inc(self.sem[name], 1)
                        if dinc is not None:
                            ins.then_inc(dinc[0], dinc[1])
                return f
            block.tensor(run("pe"))
            block.scalar(run("act"))
            block.vector(run("dve"))
            block.gpsimd(run("pool"))
            block.sync(run("sp"))


def build_nc(n_layers=DEPTH, debug=(), dbg_l=0):
    nc = bass.Bass("TRN2", target_bir_lowering=False)
    dt_in = lambda name, shape: nc.dram_tensor(name, shape, F32, kind="ExternalInput")
    x_d = dt_in("x", [S, D]).ap()
    mixg_d = dt_in("mix_norm_g", [DEPTH, D])
    win_d = dt_in("w_in", [DEPTH, D, 2560]).ap()
    qg_d = dt_in("q_norm_g", [DEPTH, 64])
    kg_d = dt_in("k_norm_g", [DEPTH, 64])
    tabx_d = dt_in("tabx", [DEPTH, 8, TABX])
    sgug_d = dt_in("sgu_norm_g", [DEPTH, 512])
    wsp_d = dt_in("w_spatial", [DEPTH, 8, 128, 128]).ap()
    bsp_d = dt_in("b_spatial", [DEPTH, 8, 128]).ap()
    attg_d = dt_in("att_out_norm_g", [DEPTH, 512])
    gmg_d = dt_in("gmlp_out_norm_g", [DEPTH, 512])
    wout_d = dt_in("w_out", [DEPTH, D, D]).ap()
    ffng_d = dt_in("ffn_norm_g", [DEPTH, D])
    wfi_d = dt_in("w_ffn_in", [DEPTH, D, 2 * DFF]).ap()
    wfo_d = dt_in("w_ffn_out", [DEPTH, DFF, D]).ap()
    ident_d = dt_in("ident", [128, 128]).ap()
    jrev_d = dt_in("jrev", [128, 128]).ap()
    gind_d = dt_in("gind", [128, 512]).ap()
    y_d = nc.dram_tensor("y", [S, D], F32, kind="ExternalOutput").ap()
    dbg_tk = []
    HF = FC // 2

    with ExitStack() as st:
        P = Prog(nc, st)
        _cnt = [0]

        def sb(stack, name, shape, dt):
            _cnt[0] += 1
            return stack.enter_context(nc.sbuf_tensor(f"sb{_cnt[0]}_{name}", shape, dt))

        def sbT(stack, name, shape, dt):
            b = sb(stack, name, shape, dt)
            return b, P.T(name, b)

        def sbN(stack, name, shape, dt, n):
            bs = [sb(stack, f"{name}{i}", shape, dt) for i in range(n)]
            return bs, [P.T(f"{name}{i}", bs[i]) for i in range(n)]

        def dump(name, buf, tiles, shape, dt):
            dd = nc.dram_tensor("dbg_" + name, shape, dt, kind="ExternalOutput").ap()
            dbg_tk.append(P.dma("sp", dd, buf[:], reads=list(tiles)))

        def blk(n):
            return slice(n * 512, (n + 1) * 512)

        def tile_(t):
            return slice(t * 128, (t + 1) * 128)

        flat = lambda ll: [x for y in ll for x in y]

        xT = sb(st, "xT", [128, DC, S], F32)
        t_xT = [[P.T(f"xT{n}_{c}", xT) for c in range(DC)] for n in range(NB)]
        hT = sb(st, "hT", [128, DC, S], BF16)
        t_hT = [[P.T(f"hT{n}_{c}", hT) for c in range(DC)] for n in range(NB)]
        mixTg = sb(st, "mixTg", [128, 4, S], BF16)
        t_mixTg = [P.T(f"mTg{t}", mixTg) for t in range(NT)]
        ident, t_ident = sbT(st, "ident", [128, 128], F32)
        jrev, t_jrev = sbT(st, "jrev", [128, 128], F32)
        identb, t_identb = sbT(st, "identb", [128, 128], BF16)
        ones_bf, t_ones = sbT(st, "ones_bf", [128, 128], BF16)
        bd_bf, t_bd = sbT(st, "bd_bf", [128, 128], BF16)
        mhalf, t_mhalf = sbT(st, "mhalf", [128, 1], F32)
        mixg = sb(st, "mixg", [128, DEPTH, 8], F32); t_mixg = [P.T(f"mixg{l}", mixg) for l in range(DEPTH)]
        ffng = sb(st, "ffng", [128, DEPTH, 8], F32); t_ffng = [P.T(f"ffng{l}", ffng) for l in range(DEPTH)]
        attg = sb(st, "attg", [128, DEPTH, 4], F32); t_attg = [P.T(f"attg{l}", attg) for l in range(DEPTH)]
        gmg = sb(st, "gmg", [128, DEPTH, 4], F32); t_gmg = [P.T(f"gmg{l}", gmg) for l in range(DEPTH)]
        qg = sb(st, "qg", [128, DEPTH], F32); t_qg = [[P.T(f"qg{l}{h}", qg) for h in range(2)] for l in range(DEPTH)]
        qg8 = sb(st, "qg8", [128, DEPTH], F32); t_qg8 = [P.T(f"qg8{l}", qg8) for l in range(DEPTH)]
        kg = sb(st, "kg", [128, DEPTH], F32); t_kg = [[P.T(f"kg{l}{h}", kg) for h in range(2)] for l in range(DEPTH)]
        ring = [sb(st, f"ring{i}", [128, 2048], BF16) for i in range(NR)]
        t_ring = [[P.T(f"ring{i}a", ring[i]), P.T(f"ring{i}b", ring[i])] for i in range(NR)]
        pAll = st.enter_context(nc.psum_tensor("pAll", [128, 4096], F32))
        pB = pAll[:, 2048:3072]
        pC = pAll[:, 3072:4096]
        t_bank = [P.T(f"bank{i}") for i in range(8)]

        def bank(i):
            return pAll[:, i * 512:(i + 1) * 512]

        loads = []
        lidx = {}

        def add_load(key, view, src):
            lidx[key] = len(loads)
            if isinstance(src, list):
                loads.append((view, src))
            else:
                loads.append((view, [(view, src, (0, 1))]))

        v_half = lambda b: b[:].rearrange("p (c n) -> p c n", c=4)
        v_blk = lambda b: b[:, 0:1024].rearrange("p (c n) -> p c n", c=8)
        v_blk2 = lambda b: b[:].rearrange("p (c n) -> p c n", c=8)
        v_w1 = lambda b: b[:].rearrange("p (g c n) -> p g c n", g=2, c=8)
        v_w1g = lambda b: b[:, 0:1024].rearrange("p (c n) -> p c n", c=8)
        v_w1u = lambda b: b[:, 1024:2048].rearrange("p (c n) -> p c n", c=8)
        v_w2 = lambda b: b[:, 0:HF * 128].rearrange("p (f n) -> p f n", f=HF)
        for l in range(n_layers):
            win_v = win_d[l].rearrange("(c p) n -> p c n", p=128)
            for nm, c0 in (("Wu", 1536), ("Wg", 2048), ("Wv", 1024)):
                for hc in range(2):
                    add_load((l, nm, hc), v_half, win_v[:, hc * 4:(hc + 1) * 4, c0:c0 + 512])
            for fb in range(8):
                add_load((l, "Wqk", fb), v_blk, win_v[:, :, fb * 128:(fb + 1) * 128])
            wout_v = wout_d[l].rearrange("(c p) n -> p c n", p=128)
            for k in range(4):
                add_load((l, "Wo2", k), v_blk2, wout_v[:, :, k * 256:(k + 1) * 256])
            wfi_v = wfi_d[l].rearrange("(c p) (g f n) -> p c g f n", p=128, g=2, n=128)
            for half in range(2):
                for fc in range(HF):
                    f = half * HF + fc
                    add_load((l, "W1", f), v_w1, [(v_w1g, wfi_v[:, :, 0, f, :], (0,)), (v_w1u, wfi_v[:, :, 1, f, :], (1,))])
                ngs = 2 if (half == 1 and l == n_layers - 1) else 1
                for ng in range(ngs):
                    for db in range(8):
                        src = wfo_d[l][half * HF * 128:(half + 1) * HF * 128, db * 128:(db + 1) * 128].rearrange(
                            "(f p) n -> p f n", p=128)
                        add_load((l, "W2", half, ng, db), v_w2, src)
        issued = [0]

        def issue_upto(k):
            while issued[0] <= k and issued[0] < len(loads):
                j = issued[0]
                view, parts = loads[j]
                for (dfn, src, sel) in parts:
                    P.dma("pool", dfn(ring[j % NR]), src, writes=[t_ring[j % NR][k2] for k2 in sel])
                issued[0] += 1

        def done(key):
            issue_upto(lidx[key] + NR)

        def W(key):
            j = lidx[key]
            assert j < issued[0], f"load {key} not issued"
            view, _ = loads[j]
            return view(ring[j % NR]), list(t_ring[j % NR])

        P.dma("sp", ident[:], ident_d, writes=[t_ident])
        P.dma("sp", jrev[:], jrev_d, writes=[t_jrev])
        P.op("dve", lambda e: e.memset(ones_bf[:], 1.0), writes=[t_ones])
        P.op("dve", lambda e: e.memset(bd_bf[:], 0.0), writes=[t_bd])
        P.op("dve", lambda e: e.memset(bd_bf[0:64, 0:64], 1.0), writes=[t_bd])
        P.op("dve", lambda e: e.memset(bd_bf[64:128, 64:128], 1.0), writes=[t_bd])
        P.op("dve", lambda e: e.memset(mhalf[:], -0.5), writes=[t_mhalf])
        P.op("dve", lambda e: e.tensor_copy(identb[:], ident[:]), reads=[t_ident], writes=[t_identb])

        def load_params():
            for l in range(DEPTH):
                def colload(dst, src_t, n, c, tl):
                    src = bass.AP(src_t, l * n, [[1, 128], [128, c]])
                    P.dma("sp", dst, src, writes=[tl], allow_slow_non_contiguous=True)
                colload(mixg[:, l, :], mixg_d, D, 8, t_mixg[l])
                colload(ffng[:, l, :], ffng_d, D, 8, t_ffng[l])
                colload(attg[:, l, :], attg_d, 512, 4, t_attg[l])
                colload(gmg[:, l, :], gmg_d, 512, 4, t_gmg[l])
                for hb in range(2):
                    P.dma("sp", qg[hb * 64:(hb + 1) * 64, l:l + 1], bass.AP(qg_d, l * 64, [[1, 64], [1, 1]]),
                          writes=[t_qg[l][hb]])
                    P.dma("sp", kg[hb * 64:(hb + 1) * 64, l:l + 1], bass.AP(kg_d, l * 64, [[1, 64], [1, 1]]),
                          writes=[t_kg[l][hb]])
                P.op("dve", lambda e: e.tensor_scalar(qg8[:, l:l + 1], qg[:, l:l + 1], 0.125, None, ALU.mult),
                     reads=t_qg[l], writes=[t_qg8[l]])

        NQ = 6

        def norm_alloc(stack):
            sq, t_sq = sbN(stack, "nsq", [128, 512], BF16, NQ)
            rs, t_rs = sbN(stack, "nrs", [128, 512], F32, 2)
            return (sq, t_sq, rs, t_rs)

        def norm_block(N, gcols, t_g, l, n):
            sq, t_sq, rs, t_rs = N
            b = 6 + n % 2
            for c in range(DC):
                k = (n * DC + c) % NQ
                if c % 3 == 1:
                    P.op("pool", lambda e: e.tensor_tensor(sq[k][:], xT[:, c, blk(n)], xT[:, c, blk(n)], ALU.mult),
                         reads=[t_xT[n][c]], writes=[t_sq[k]])
                else:
                    P.op("act", lambda e: e.activation(sq[k][:], xT[:, c, blk(n)], AF.Square),
                         reads=[t_xT[n][c]], writes=[t_sq[k]])
                P.op("pe", lambda e: e.matmul(bank(b), lhsT=ones_bf[:], rhs=sq[k][:],
                                              start=(c == 0), stop=(c == DC - 1)),
                     reads=[t_ones, t_sq[k]], writes=[t_bank[b]], inc=True)
            r = n % 2
            P.op("act", lambda e: e.activation(rs[r][:], bank(b), AF.Ln, bias=EPS, scale=1.0 / D),
                 reads=[t_bank[b]], writes=[t_rs[r]])
            P.op("act", lambda e: e.activation(rs[r][:], rs[r][:], AF.Exp, scale=-0.5),
                 reads=[t_rs[r]], writes=[t_rs[r]])
            for c in range(DC):
                P.op("dve", lambda e: e.scalar_tensor_tensor(
                    out=hT[:, c, blk(n)], in0=xT[:, c, blk(n)], scalar=gcols[:, l, c:c + 1], in1=rs[r][:],
                    op0=ALU.mult, op1=ALU.mult),
                    reads=[t_xT[n][c], t_g, t_rs[r]], writes=[t_hT[n][c]])

        out_tk = []
        out_ctx = {}

        def out_tiles(t0_, t1_):
            yo, t_yo, NY = out_ctx["yo"], out_ctx["t_yo"], out_ctx["NY"]
            for tt in range(t0_, t1_):
                sl_ = tt % NY
                for half in range(2):
                    b = 2 * (tt % 2) + half
                    for cc in range(4):
                        c = half * 4 + cc
                        P.op("pe", lambda e: e.transpose(
                            bank(b)[:, cc * 128:(cc + 1) * 128], xT[:, c, tile_(tt)], ident[:]),
                            reads=[t_xT[tt // 4][c], t_ident], writes=[t_bank[b]], inc=(cc == 3))
                    if half == 0:
                        P.op("act", lambda e: e.copy(yo[sl_][:, 0:512], bank(b)), reads=[t_bank[b]], writes=[t_yo[sl_][0]])
                    else:
                        P.op("dve", lambda e: e.tensor_copy(yo[sl_][:, 512:1024], bank(b)), reads=[t_bank[b]], writes=[t_yo[sl_][1]])
                out_tk.append(P.dma("sp", y_d[tt * 128:(tt + 1) * 128, :], yo[sl_][:], reads=t_yo[sl_], owner=t_yo[sl_][0]))

        for l in range(n_layers):
            with ExitStack() as sl:
                sgug, t_sgug = sbT(sl, "sgug", [128, 512], F32)
                wnat, t_wnat = sbT(sl, "wnat", [128, 8, 128], F32)
                wT, t_wT = sbT(sl, "wT", [128, 8, 128], BF16)
                bnat, t_bnat = sbT(sl, "bnat", [8, 128], F32)
                bhi, t_bhi = sbT(sl, "bhi", [128, 128], BF16)
                blo, t_blo = sbT(sl, "blo", [128, 128], BF16)
                gindb, t_gindb = sbT(sl, "gindb", [128, 512], BF16)
                P.dma("pool", gindb[:], gind_d, writes=[t_gindb])
                P.dma("sp", sgug[:], bass.AP(sgug_d, l * 512, [[0, 128], [1, 512]]), writes=[t_sgug])
                P.dma("sp", wnat[:], wsp_d[l].rearrange("g t s -> t g s"), writes=[t_wnat])
                P.dma("sp", bnat[:], bsp_d[l], writes=[t_bnat])
                P.op("dve", lambda e: e.memset(bhi[:], 0.0), writes=[t_bhi])
                P.op("dve", lambda e: e.memset(blo[:], 0.0), writes=[t_blo])
                P.op("dve", lambda e: e.tensor_copy(bhi[0:8, :], bnat[:]), reads=[t_bnat], writes=[t_bhi])
                P.op("dve", lambda e: e.tensor_tensor(blo[0:8, :], bnat[:], bhi[0:8, :], ALU.subtract),
                     reads=[t_bnat, t_bhi], writes=[t_blo])

                if l == 0:
                    with ExitStack() as s0:
                        NX = 3
                        xin, t_xin = sbN(s0, "xin", [128, D], F32, NX)
                        N0 = norm_alloc(s0)
                        for tt in range(NT):
                            sl_ = tt % NX
                            P.dma("sp", xin[sl_][:], x_d[tt * 128:(tt + 1) * 128, :], writes=[t_xin[sl_]])
                            if tt == 1:
                                load_params()
                            if tt == 3:
                                issue_upto(NR - 1)
                            for half in range(2):
                                b = 2 * sl_ + half
                                for cc in range(4):
                                    c = half * 4 + cc
                                    P.op("pe", lambda e: e.transpose(
                                        bank(b)[:, cc * 128:(cc + 1) * 128], xin[sl_][:, c * 128:(c + 1) * 128], ident[:]),
                                        reads=[t_xin[sl_], t_ident], writes=[t_bank[b]], inc=(cc == 3))
                                dst = xT[:, half * 4:half * 4 + 4, tile_(tt)]
                                src = bank(b).rearrange("p (c t) -> p c t", c=4)
                                tw = t_xT[tt // 4][half * 4:half * 4 + 4]
                                if half == 0:
                                    P.op("act", lambda e: e.copy(dst, src), reads=[t_bank[b]], writes=tw)
                                else:
                                    P.op("dve", lambda e: e.tensor_copy(dst, src), reads=[t_bank[b]], writes=tw)
                            if tt % 4 == 3 and tt >= 7:
                                norm_block(N0, mixg, t_mixg[l], l, tt // 4 - 1)
                        norm_block(N0, mixg, t_mixg[l], l, NB - 1)
                        P.retire(t_xin + N0[1] + N0[3])

                for hb in range(2):
                    for gg in range(4):
                        g = hb * 4 + gg
                        P.op("pe", lambda e: e.transpose(
                            bank(4 + hb)[:, gg * 128:(gg + 1) * 128], wnat[:, g, :], ident[:]),
                            reads=[t_wnat, t_ident], writes=[t_bank[4 + hb]], inc=(gg == 3))
                    P.op("dve", lambda e: e.tensor_copy(
                        wT[:, hb * 4:hb * 4 + 4, :], bank(4 + hb).rearrange("p (g t) -> p g t", g=4)),
                        reads=[t_bank[4 + hb]], writes=[t_wT])
                P.op("dve", lambda e: e.memset(wT[64:128, :, 0:64], 0.0), writes=[t_wT])

                if l > 0:
                    with ExitStack() as sn:
                        Nn = norm_alloc(sn)
                        for n in range(NB):
                            norm_block(Nn, mixg, t_mixg[l], l, n)
                        P.retire(Nn[1] + Nn[3])
                if "hT" in debug and l == dbg_l:
                    dump("hT", hT, flat(t_hT), [128, DC, S], BF16)

                with ExitStack() as sg:
                    NS = 7
                    u_sb, t_u = sbN(sg, "u_sb", [128, 512], F32, NS)
                    vgg, t_vgg = sbN(sg, "vgg", [128, 512], F32, NS)
                    vgn, t_vgn = sbN(sg, "vgn", [128, 512], BF16, NS)
                    gm, t_gm = sbN(sg, "gm", [128, 512], F32, NS)
                    junk, t_junk = sbT(sg, "gjunk", [128, 512], BF16)
                    st4 = [sb(sg, f"gst{i}", [128, 4], F32) for i in range(NS)]
                    t_st = [[P.T(f"gst{i}{j}", st4[i]) for j in range(4)] for i in range(NS)]
                    Wu = [W((l, "Wu", hc)) for hc in range(2)]
                    Wg = [W((l, "Wg", hc)) for hc in range(2)]

                    def gA(tt):
                        bU, bG = 2 * (tt % 2), 2 * (tt % 2) + 1
                        n = tt // 4
                        for (Wx, bX) in ((Wg, bG), (Wu, bU)):
                            for c in range(DC):
                                wv, wt = Wx[c // 4]
                                P.op("pe", lambda e: e.matmul(
                                    bank(bX), lhsT=hT[:, c, tile_(tt)], rhs=wv[:, c % 4, :], start=(c == 0), stop=(c == DC - 1)),
                                    reads=[t_hT[n][c]] + wt, writes=[t_bank[bX]], inc=(c == DC - 1))

                    def rstd_chain(stt, t_s, i0):
                        P.op("pool", lambda e: e.tensor_scalar(stt[:, i0:i0 + 1], stt[:, i0:i0 + 1], 1.0 / 512, EPS, ALU.mult, ALU.add),
                             reads=[t_s[i0]], writes=[t_s[i0]])
                        P.op("pool", lambda e: e.tensor_tensor(stt[:, i0 + 1:i0 + 2], stt[:, i0:i0 + 1], mhalf[:], ALU.pow),
                             reads=[t_s[i0], t_mhalf], writes=[t_s[i0 + 1]])

                    def gB(tt):
                        s_ = tt % NS
                        bU, bG = 2 * (tt % 2), 2 * (tt % 2) + 1
                        P.op("act", lambda e: e.activation(vgg[s_][:], bank(bG), AF.Gelu_apprx_tanh),
                             reads=[t_bank[bG]], writes=[t_vgg[s_]])
                        P.op("act", lambda e: e.activation(junk[:], vgg[s_][:], AF.Square, accum_out=st4[s_][:, 0:1]),
                             reads=[t_vgg[s_]], writes=[t_junk, t_st[s_][0]])
                        rstd_chain(st4[s_], t_st[s_], 0)
                        P.op("act", lambda e: e.activation(u_sb[s_][:], bank(bU), AF.Gelu_apprx_tanh),
                             reads=[t_bank[bU]], writes=[t_u[s_]])
                        P.op("dve", lambda e: e.scalar_tensor_tensor(
                            out=vgn[s_][:], in0=vgg[s_][:], scalar=st4[s_][:, 1:2], in1=sgug[:], op0=ALU.mult, op1=ALU.mult),
                            reads=[t_vgg[s_], t_st[s_][1], t_sgug], writes=[t_vgn[s_]])

                    def gC(tt):
                        s_ = tt % NS
                        bM = 4 + tt % 2
                        P.op("pe", lambda e: e.matmul(bank(bM), lhsT=bhi[:], rhs=gindb[:], start=True, stop=False),
                             reads=[t_bhi, t_gindb], writes=[t_bank[bM]], inc=False)
                        P.op("pe", lambda e: e.matmul(bank(bM), lhsT=blo[:], rhs=gindb[:], start=False, stop=False),
                             reads=[t_blo, t_gindb], writes=[t_bank[bM]], inc=False)
                        for g in range(8):
                            P.op("pe", lambda e: e.matmul(
                                bank(bM)[:, g * 64:(g + 1) * 64], lhsT=wT[:, g, :], rhs=vgn[s_][:, g * 64:(g + 1) * 64],
                                start=False, stop=(g == 7)),
                                reads=[t_wT, t_vgn[s_]], writes=[t_bank[bM]], inc=(g == 7))

                    def gD(tt):
                        s_ = tt % NS
                        bM = 4 + tt % 2
                        P.op("dve", lambda e: e.tensor_tensor(gm[s_][:], bank(bM), u_sb[s_][:], ALU.mult),
                             reads=[t_bank[bM], t_u[s_]], writes=[t_gm[s_]])
                        P.op("act", lambda e: e.activation(junk[:], gm[s_][:], AF.Square, accum_out=st4[s_][:, 2:3]),
                             reads=[t_gm[s_]], writes=[t_junk, t_st[s_][2]])
                        rstd_chain(st4[s_], t_st[s_], 2)
                        P.op("act", lambda e: e.activation(gm[s_][:], gm[s_][:], AF.Copy, scale=st4[s_][:, 3:4]),
                             reads=[t_gm[s_], t_st[s_][3]], writes=[t_gm[s_]])

                    def gE(tt):
                        s_ = tt % NS
                        bTr = 6 + tt % 2
                        for j in range(4):
                            P.op("pe", lambda e: e.transpose(
                                bank(bTr)[:, j * 128:(j + 1) * 128], gm[s_][:, j * 128:(j + 1) * 128], ident[:]),
                                reads=[t_gm[s_], t_ident], writes=[t_bank[bTr]], inc=(j == 3))
                        P.op("dve", lambda e: e.tensor_tensor(
                            mixTg[:, :, tile_(tt)], bank(bTr).rearrange("p (j t) -> p j t", j=4),
                            gmg[:, l, :].rearrange("p (j o) -> p j o", o=1).to_broadcast([128, 4, 128]), ALU.mult),
                            reads=[t_bank[bTr], t_gmg[l]], writes=[t_mixTg[tt]])

                    for s_ in range(NT + 6):
                        if 0 <= s_ - 3 < NT:
                            gC(s_ - 3); gD(s_ - 3)
                        if 0 <= s_ - 6 < NT:
                            gE(s_ - 6)
                        if s_ < NT:
                            gA(s_); gB(s_)
                        if s_ == NT - 1:
                            for hc in range(2):
                                done((l, "Wu", hc)); done((l, "Wg", hc))
                    P.retire([t_junk] + t_u + t_vgg + t_vgn + t_gm + flat(t_st))
                P.retire([t_sgug, t_wnat, t_wT, t_bnat, t_bhi, t_blo, t_gindb])
            if "mixTg" in debug and l == dbg_l:
                dump("mixTg", mixTg, t_mixTg, [128, 4, S], BF16)

            with ExitStack() as sa:
                bias = sb(sa, "expB", [128, 4, 5, 2, 128], BF16)
                t_bias = [P.T(f"expB{h}", bias) for h in range(8)]
                V = sb(sa, "V", [128, NT, 8, 65], BF16)
                t_V = [P.T(f"V{t}", V) for t in range(NT)]
                qT = sb(sa, "qT", [128, 4, S], BF16)
                kT = sb(sa, "kT", [128, 4, S], BF16)
                t_qT = [[P.T(f"qT{f}{n}", qT) for n in range(NB)] for f in range(4)]
                t_kT = [[P.T(f"kT{f}{n}", kT) for n in range(NB)] for f in range(4)]
                P.op("dve", lambda e: e.memset(V[:, :, :, 64:65], 1.0), writes=t_V)

                with ExitStack() as shk:
                    hk, t_hk = sbN(shk, "hk", [128, 640], F32, 2)

                    def bias_head(h):
                        k = h % 2
                        src = bass.AP(tabx_d, (l * 8 + h) * TABX + 1, [[1, 128], [128, 5], [1, 128]])
                        P.dma("sp", hk[k][:].rearrange("p (d q) -> p d q", d=5), src, writes=[t_hk[k]])
                        b0 = 4 + 2 * k
                        pX = pB if k == 0 else pC
                        P.op("pe", lambda e: e.matmul(pX[:, 0:512], lhsT=jrev[:], rhs=hk[k][:, 0:512], start=True, stop=True),
                             reads=[t_jrev, t_hk[k]], writes=[t_bank[b0]])
                        P.op("pe", lambda e: e.matmul(pX[:, 512:640], lhsT=jrev[:], rhs=hk[k][:, 512:640], start=True, stop=True),
                             reads=[t_jrev, t_hk[k]], writes=[t_bank[b0 + 1]])
                        P.op("act", lambda e: e.activation(bias[:, h // 2, :, h % 2, :],
                                                          pX[:, 0:640].rearrange("p (d q) -> p d q", d=5), AF.Exp),
                             reads=[t_bank[b0], t_bank[b0 + 1]], writes=[t_bias[h]])
                        P.op("dve", lambda e: e.memset(bias[64:128, h // 2, 0, h % 2, 0:64], 0.0), writes=[t_bias[h]])
                        P.op("dve", lambda e: e.memset(bias[0:64, h // 2, 4, h % 2, 64:128], 0.0), writes=[t_bias[h]])

                    Wv = [W((l, "Wv", hc)) for hc in range(2)]
                    for tt in range(NT):
                        b = tt % 4
                        n = tt // 4
                        for c in range(DC):
                            wv, wt = Wv[c // 4]
                            P.op("pe", lambda e: e.matmul(
                                bank(b), lhsT=hT[:, c, tile_(tt)], rhs=wv[:, c % 4, :], start=(c == 0), stop=(c == DC - 1)),
                                reads=[t_hT[n][c]] + wt, writes=[t_bank[b]], inc=(c == DC - 1))
                        src = bank(b).rearrange("p (h d) -> p h d", h=8)
                        if tt % 2 == 0:
                            P.op("act", lambda e: e.copy(V[:, tt, :, 0:64], src), reads=[t_bank[b]], writes=[t_V[tt]])
                        else:
                            P.op("dve", lambda e: e.tensor_copy(V[:, tt, :, 0:64], src), reads=[t_bank[b]], writes=[t_V[tt]])
                        if tt % 2 == 1:
                            bias_head(tt // 2)
                    for hc in range(2):
                        done((l, "Wv", hc))
                    P.retire(t_hk)

                with ExitStack() as sq_:
                    NQK = 3
                    qsq, t_qsq = sbN(sq_, "qsq", [128, 512], BF16, NQK)
                    qrs, t_qrs = sbN(sq_, "qrs", [128, 512], F32, NQK)
                    NIT = 32

                    def qA(it):
                        fb, n = it // 4, it % 4
                        bQ = it % 3
                        wv, wt = W((l, "Wqk", fb))
                        for c in range(DC):
                            P.op("pe", lambda e: e.matmul(
                                bank(bQ), lhsT=wv[:, c, :], rhs=hT[:, c, blk(n)], start=(c == 0), stop=(c == DC - 1)),
                                reads=wt + [t_hT[n][c]], writes=[t_bank[bQ]], inc=(c == DC - 1))
                        if n == 3:
                            done((l, "Wqk", fb))
                        r = it % NQK
                        P.op("act", lambda e: e.activation(qsq[r][:], bank(bQ), AF.Square),
                             reads=[t_bank[bQ]], writes=[t_qsq[r]])

                    def qC(it):
                        fb, n = it // 4, it % 4
                        bQ = it % 3
                        bS = 4 + it % 2
                        r = it % NQK
                        isq = fb < 4
                        dstT = qT if isq else kT
                        t_dst = t_qT if isq else t_kT
                        gcol = qg8 if isq else kg
                        t_gc = [t_qg8[l]] if isq else t_kg[l]
                        P.op("pe", lambda e: e.matmul(bank(bS), lhsT=bd_bf[:], rhs=qsq[r][:], start=True, stop=True),
                             reads=[t_bd, t_qsq[r]], writes=[t_bank[bS]])
                        P.op("act", lambda e: e.activation(qrs[r][:], bank(bS), AF.Ln, bias=EPS, scale=1.0 / 64),
                             reads=[t_bank[bS]], writes=[t_qrs[r]])
                        P.op("act", lambda e: e.activation(qrs[r][:], qrs[r][:], AF.Exp, scale=-0.5),
                             reads=[t_qrs[r]], writes=[t_qrs[r]])
                        P.op("dve", lambda e: e.scalar_tensor_tensor(
                            out=dstT[:, fb % 4, blk(n)], in0=bank(bQ), scalar=gcol[:, l:l + 1], in1=qrs[r][:],
                            op0=ALU.mult, op1=ALU.mult),
                            reads=[t_bank[bQ]] + t_gc + [t_qrs[r]], writes=[t_dst[fb % 4][n]])

                    for s_ in range(NIT + 1):
                        if s_ < NIT:
                            qA(s_)
                        if s_ - 1 >= 0:
                            qC(s_ - 1)
                    P.retire(t_qsq + t_qrs)
                if "qT" in debug and l == dbg_l:
                    dump("qT", qT, flat(t_qT), [128, 4, S], BF16)
                    dump("kT", kT, flat(t_kT), [128, 4, S], BF16)
                    dump("V", V, t_V, [128, NT, 8, 65], BF16)

                P.retire(flat(t_hT))
                t_mixTa = [P.T(f"mTa{t}", hT) for t in range(NT)]
                mixTa = hT

                with ExitStack() as sat:
                    NQM = 3
                    qm, t_qm = sbN(sat, "qm", [128, 256], BF16, NQM)
                    NEP = 3
                    EP, t_EP = sbN(sat, "EP", [128, 1280], BF16, NEP)
                    rc, t_rc = sbT(sat, "rc", [128, 8], F32)
                    atok, t_atok = sbT(sat, "atok", [128, 512], F32)
                    anb, t_anb = sbT(sat, "anb", [128, 512], BF16)
                    ast = sb(sat, "ast", [128, 2], F32)
                    t_ast = [P.T(f"ast{j}", ast) for j in range(2)]
                    for i in range(NQM):
                        P.op("pool", lambda e: e.memset(qm[i][:], 0.0), writes=[t_qm[i]])

                    P.retire(t_bank[0:6])
                    t_S = [P.T(f"S{i}") for i in range(2)]
                    NP = NT * 4
                    trb = pAll[:, 7 * 512 + 128:7 * 512 + 384].bitcast(BF16)

                    def aQ(p):
                        m, hp = p // 4, p % 4
                        k = p % NQM
                        P.op("pool", lambda e: e.tensor_copy(qm[k][0:64, 0:128], qT[0:64, hp, tile_(m)]),
                             reads=[t_qT[hp][m // 4]], writes=[t_qm[k]])
                        P.op("pool", lambda e: e.tensor_copy(qm[k][64:128, 128:256], qT[64:128, hp, tile_(m)]),
                             reads=[t_qT[hp][m // 4]], writes=[t_qm[k]])

                    def aA(p):
                        m, hp = p // 4, p % 4
                        nb_ = min(m, 4) + 1
                        st_ = p % 2
                        c0 = st_ * 1536
                        k = p % NQM
                        for d in range(nb_):
                            j = m - d
                            P.op("pe", lambda e: e.matmul(
                                pAll[:, c0 + d * 256:c0 + (d + 1) * 256], lhsT=kT[:, hp, tile_(j)], rhs=qm[k][:],
                                start=True, stop=True),
                                reads=[t_kT[hp][j // 4], t_qm[k]], writes=[t_S[st_]], inc=(d == nb_ - 1))
                        w = nb_ * 256
                        ep = p % NEP
                        P.op("act", lambda e: e.activation(EP[ep][:, 0:w], pAll[:, c0:c0 + w], AF.Exp),
                             reads=[t_S[st_]], writes=[t_EP[ep]])
                        P.op("dve", lambda e: e.tensor_tensor(
                            EP[ep][:, 0:w], EP[ep][:, 0:w], bias[:, hp, 0:nb_, :, :].rearrange("p d u q -> p (d u q)"), ALU.mult),
                            reads=[t_EP[ep], t_bias[2 * hp], t_bias[2 * hp + 1]], writes=[t_EP[ep]])

                    deferred = {}

                    def aC(p, step):
                        m, hp = p // 4, p % 4
                        nb_ = min(m, 4) + 1
                        ep = p % NEP
                        for u in range(2):
                            h = 2 * hp + u
                            ob = 6 if h < 7 else 7
                            oc = (6 * 512 + h * 65) if h < 7 else 7 * 512
                            for d in range(nb_):
                                j = m - d
                                P.op("pe", lambda e: e.matmul(
                                    pAll[:, oc:oc + 65], lhsT=EP[ep][:, (d * 2 + u) * 128:(d * 2 + u + 1) * 128], rhs=V[:, j, h, :],
                                    start=(d == 0), stop=(d == nb_ - 1)),
                                    reads=[t_EP[ep], t_V[j]], writes=[t_bank[ob]], inc=(d == nb_ - 1))
                        if hp == 3:
                            aD1(m)
                            deferred.setdefault(step + 1, []).append((aD2, m))
                            deferred.setdefault(step + 2, []).append((aD3, m))
                            deferred.setdefault(step + 3, []).append((aE, m))

                    def aD1(m):
                        ov6 = pAll[:, 6 * 512:6 * 512 + 455].rearrange("p (h d) -> p h d", h=7)
                        ov7 = pAll[:, 7 * 512:7 * 512 + 65]
                        P.op("dve", lambda e: e.reciprocal(rc[:, 0:7].rearrange("p (h o) -> p h o", o=1), ov6[:, :, 64:65]),
                             reads=[t_bank[6]], writes=[t_rc])
                        P.op("dve", lambda e: e.reciprocal(rc[:, 7:8], ov7[:, 64:65]), reads=[t_bank[7]], writes=[t_rc])
                        P.op("dve", lambda e: e.tensor_tensor(
                            atok[:, 0:448].rearrange("p (h d) -> p h d", h=7), ov6[:, :, 0:64],
                            rc[:, 0:7].rearrange("p (h o) -> p h o", o=1).to_broadcast([128, 7, 64]), ALU.mult),
                            reads=[t_bank[6], t_rc], writes=[t_atok])
                        P.op("dve", lambda e: e.tensor_scalar(atok[:, 448:512], ov7[:, 0:64], rc[:, 7:8], None, ALU.mult),
                             reads=[t_bank[7], t_rc], writes=[t_atok])

                    def aD2(m):
                        P.op("act", lambda e: e.activation(anb[:], atok[:], AF.Square, accum_out=ast[:, 0:1]),
                             reads=[t_atok], writes=[t_anb, t_ast[0]])
                        P.op("pool", lambda e: e.tensor_scalar(ast[:, 0:1], ast[:, 0:1], 1.0 / 512, EPS, ALU.mult, ALU.add),
                             reads=[t_ast[0]], writes=[t_ast[0]])
                        P.op("pool", lambda e: e.tensor_tensor(ast[:, 1:2], ast[:, 0:1], mhalf[:], ALU.pow),
                             reads=[t_ast[0], t_mhalf], writes=[t_ast[1]])

                    def aD3(m):
                        P.op("act", lambda e: e.activation(anb[:], atok[:], AF.Copy, scale=ast[:, 1:2]),
                             reads=[t_atok, t_ast[1]], writes=[t_anb])

                    def aE(m):
                        for j in range(4):
                            P.op("pe", lambda e: e.transpose(
                                trb[:, j * 128:(j + 1) * 128], anb[:, j * 128:(j + 1) * 128], identb[:]),
                                reads=[t_anb, t_identb], writes=[t_bank[7]], inc=(j == 3))
                        P.op("dve", lambda e: e.tensor_tensor(
                            mixTa[:, 0:4, tile_(m)], trb.rearrange("p (j t) -> p j t", j=4),
                            attg[:, l, :].rearrange("p (j o) -> p j o", o=1).to_broadcast([128, 4, 128]), ALU.mult),
                            reads=[t_bank[7], t_attg[l]], writes=[t_mixTa[m]])

                    aQ(0); aQ(1)
                    for s_ in range(NP + 6):
                        if s_ + 2 < NP:
                            aQ(s_ + 2)
                        if s_ < NP:
                            aA(s_)
                        if 0 <= s_ - 1 < NP:
                            aC(s_ - 1, s_)
                        for (fn_, m) in deferred.pop(s_, []):
                            fn_(m)
                    assert not deferred
                    P.retire(t_S)
                    t_bank[0:6] = [P.T(f"bank{i}") for i in range(6)]
                    P.retire(t_qm + t_EP + [t_rc, t_atok, t_anb] + t_ast)
                P.retire(t_V + flat(t_qT) + flat(t_kT) + t_bias)
            if "mixTa" in debug and l == dbg_l:
                dump("mixTa", hT, t_mixTa, [128, DC, S], BF16)

            with ExitStack() as sn:
                Nn = norm_alloc(sn)
                Wo2 = [W((l, "Wo2", k)) for k in range(4)]
                new_hT = [None] * NB
                it = 0
                for n in range(NB + 1):
                    if n < NB:
                        for db in range(8):
                            wv, wt = Wo2[db // 2]
                            b = it % 4
                            it += 1
                            for c in range(DC):
                                rhs = mixTa[:, c, blk(n)] if c < 4 else mixTg[:, c - 4, blk(n)]
                                tr = (t_mixTa if c < 4 else t_mixTg)[4 * n:4 * n + 4]
                                P.op("pe", lambda e: e.matmul(
                                    bank(b), lhsT=wv[:, c, (db % 2) * 128:(db % 2) * 128 + 128], rhs=rhs,
                                    start=(c == 0), stop=(c == DC - 1)),
                                    reads=wt + tr, writes=[t_bank[b]], inc=(c == DC - 1))
                            P.op("dve", lambda e: e.tensor_tensor(xT[:, db, blk(n)], xT[:, db, blk(n)], bank(b), ALU.add),
                                 reads=[t_bank[b], t_xT[n][db]], writes=[t_xT[n][db]])
                        P.retire(t_mixTa[4 * n:4 * n + 4])
                        t_hT[n] = [P.T(f"hT{n}_{c}", hT) for c in range(DC)]
                    if n >= 1:
                        norm_block(Nn, ffng, t_ffng[l], l, n - 1)
                for k in range(4):
                    done((l, "Wo2", k))
                P.retire(Nn[1] + Nn[3])
            if "x1" in debug and l == dbg_l:
                dump("x1", xT, flat(t_xT), [128, DC, S], F32)

            with ExitStack() as sf:
                actT = sb(sf, "actT", [128, HF, S], BF16)
                t_actT = [[P.T(f"actT{n}_{f}", actT) for f in range(HF)] for n in range(NB)]
                NSG = 3
                sg_, t_sg = sbN(sf, "sg", [128, 512], F32, NSG)
                last = (l == n_layers - 1)
                if last:
                    NY = 3
                    yo = [sb(sf, f"yo{i}", [128, D], F32) for i in range(NY)]
                    t_yo = [[P.T(f"yo{i}_{h}", yo[i]) for h in range(2)] for i in range(NY)]
                    out_ctx.update(yo=yo, t_yo=t_yo, NY=NY)
                it = 0
                it2 = 0
                for half in range(2):
                    for fc in range(HF):
                        f = half * HF + fc
                        wv, wt = W((l, "W1", f))
                        for n in range(NB):
                            bA = 2 * (it % 2)
                            bB = bA + 1
                            r = it % NSG
                            it += 1
                            for gi, bX in ((0, bA), (1, bB)):
                                for c in range(DC):
                                    P.op("pe", lambda e: e.matmul(
                                        bank(bX), lhsT=wv[:, gi, c, :], rhs=hT[:, c, blk(n)], start=(c == 0), stop=(c == DC - 1)),
                                        reads=wt + [t_hT[n][c]], writes=[t_bank[bX]], inc=(c == DC - 1))
                            P.op("act", lambda e: e.activation(sg_[r][:], bank(bA), AF.Silu),
                                 reads=[t_bank[bA]], writes=[t_sg[r]])
                            P.op("dve", lambda e: e.tensor_tensor(actT[:, fc, blk(n)], sg_[r][:], bank(bB), ALU.mult),
                                 reads=[t_sg[r], t_bank[bB]], writes=[t_actT[n][fc]])
                        done((l, "W1", f))
                    ngs = 2 if (half == 1 and last) else 1
                    for ng in range(ngs):
                        nlist = list(range(NB)) if ngs == 1 else [2 * ng, 2 * ng + 1]
                        for db in range(8):
                            wv, wt = W((l, "W2", half, ng, db))
                            for n in nlist:
                                b = 4 + it2 % 4
                                it2 += 1
                                for fc in range(HF):
                                    P.op("pe", lambda e: e.matmul(
                                        bank(b), lhsT=wv[:, fc, :], rhs=actT[:, fc, blk(n)], start=(fc == 0), stop=(fc == HF - 1)),
                                        reads=wt + [t_actT[n][fc]], writes=[t_bank[b]], inc=(fc == HF - 1))
                                P.op("dve", lambda e: e.tensor_tensor(xT[:, db, blk(n)], xT[:, db, blk(n)], bank(b), ALU.add),
                                     reads=[t_bank[b], t_xT[n][db]], writes=[t_xT[n][db]])
                            done((l, "W2", half, ng, db))
                            if ngs == 2 and ng == 1:
                                out_tiles(db, db + 1)
                if last:
                    out_tiles(8, NT)
                P.retire(flat(t_actT) + t_sg)

        if True:
            P.wait_all("sp", out_tk + dbg_tk)
            P.replay()
    return nc


_NC_CACHE = {}


_GIND = np.zeros((128, 512), dtype=np.float32)
for _g in range(8):
    _GIND[_g, _g * 64:(_g + 1) * 64] = 1.0


def _host_inputs(inputs):
    f32 = lambda a: np.ascontiguousarray(np.asarray(a, dtype=np.float32))
    rel = f32(inputs["rel_bias"])
    tabx = np.concatenate([rel, np.repeat(rel[..., -1:], TABX - rel.shape[-1], axis=-1)], axis=-1)
    shared = {
        "mix_norm_g": f32(inputs["mix_norm_g"]), "w_in": f32(inputs["w_in"]),
        "q_norm_g": f32(inputs["q_norm_g"]), "k_norm_g": f32(inputs["k_norm_g"]),
        "tabx": np.ascontiguousarray(tabx), "sgu_norm_g": f32(inputs["sgu_norm_g"]),
        "w_spatial": f32(inputs["w_spatial"]), "b_spatial": f32(inputs["b_spatial"]),
        "att_out_norm_g": f32(inputs["att_out_norm_g"]), "gmlp_out_norm_g": f32(inputs["gmlp_out_norm_g"]),
        "w_out": f32(inputs["w_out"]), "ffn_norm_g": f32(inputs["ffn_norm_g"]),
        "w_ffn_in": f32(inputs["w_ffn_in"]), "w_ffn_out": f32(inputs["w_ffn_out"]),
        "ident": np.eye(128, dtype=np.float32),
        "jrev": np.ascontiguousarray(np.eye(128, dtype=np.float32)[:, ::-1]),
        "gind": _GIND,
    }
    return shared


def kernel(**inputs):
    x = np.asarray(inputs["x"], dtype=np.float32)
    B = x.shape[0]
    shared = _host_inputs(inputs)
    if "nc" not in _NC_CACHE:
        _NC_CACHE["nc"] = build_nc()
    nc = _NC_CACHE["nc"]
    in_maps = [dict(shared, x=np.ascontiguousarray(x[b])) for b in range(B)]
    res = run_bass_kernel_spmd(nc, in_maps, core_ids=list(range(B)))
    return np.stack([np.asarray(r["y"], dtype=np.float32) for r in res.results], axis=0)
```

```python
import numpy as np
from contextlib import ExitStack
import concourse.bass as bass
import concourse.mybir as mybir
from concourse.bass_utils import run_bass_kernel_spmd

F32 = mybir.dt.float32
BF16 = mybir.dt.bfloat16
ALU = mybir.AluOpType
AF = mybir.ActivationFunctionType

ENGS = ("pe", "act", "dve", "pool", "sp")

S = 2048
D = 1024
NT = 16
NB = 4
DC = 8
DEPTH = 2
DFF = 2816
FC = 22
EPS = 1e-6
NEG = -30000.0
TABX = 768
SAME_ENGINE_WAW = True
NWARM = 24
NR = 5


class T:
    __slots__ = ("name", "last_w", "readers", "dsem", "dcount", "rng")

    def __init__(self, name, prog=None, rng=None):
        self.name = name
        self.last_w = None
        self.rng = rng
        self.readers = []
        if prog is not None:
            if rng is None:
                self.readers = list(prog.floor.items())
            else:
                m = {}
                for (lo, hi, tk) in prog.retired:
                    if lo < rng[1] and rng[0] < hi:
                        for s_, v in tk.items():
                            if m.get(s_, 0) < v:
                                m[s_] = v
                self.readers = list(m.items())
        self.dsem = None
        self.dcount = 0


class _Rec:
    def __init__(self):
        self.call = None

    def __getattr__(self, name):
        def f(*a, **k):
            self.call = (name, a, k)
        return f


class Prog:
    def __init__(self, nc, stack):
        self.nc = nc
        self.stack = stack
        self.ops = {e: [] for e in ENGS}
        self.sem = {e: stack.enter_context(nc.semaphore("s_" + e)) for e in ENGS}
        self.n = {e: 0 for e in ENGS}
        self.waited = {e: {} for e in ENGS}
        self.semkey = {self.sem[e]: e for e in ENGS}
        self.nd = 0
        self.pending = {e: [] for e in ENGS}
        self.floor = {}
        self.retired = []

    def T(self, name, buf=None):
        rng = None
        if buf is not None:
            ml = self.nc.lookup_mloc(buf)
            rng = (int(ml.addr), int(ml.addr) + int(ml.dims[1]))
        return T(name, self, rng)

    def retire(self, tiles):
        for t in tiles:
            tks = list(t.readers)
            if t.last_w is not None:
                tks.append(t.last_w)
            if t.rng is None:
                for s, v in tks:
                    if self.floor.get(s, 0) < v:
                        self.floor[s] = v
            else:
                d = {}
                for s, v in tks:
                    if d.get(s, 0) < v:
                        d[s] = v
                if d:
                    self.retired.append((t.rng[0], t.rng[1], d))

    def _collect(self, eng, reads, writes):
        w = {}
        own = self.sem.get(eng)

        def add(tk, raw):
            if tk is None:
                return
            s, v = tk
            if s is own and not raw and not SAME_ENGINE_WAW:
                return
            if w.get(s, 0) < v:
                w[s] = v
        for t in reads:
            add(t.last_w, True)
        for t in writes:
            add(t.last_w, False)
            for r in t.readers:
                add(r, False)
        need = []
        for s, v in w.items():
            if self.waited[eng].get(s, 0) >= v:
                continue
            if s is own and eng == "pe":
                continue
            if s in self.semkey:
                e2 = self.semkey[s]
                assert v <= self.n[e2], f"wait on unrecorded inc {e2} {v}>{self.n[e2]}"
            self.waited[eng][s] = v
            need.append((s, v))
        return need

    def op(self, eng, fn, reads=(), writes=(), inc=True):
        rec = _Rec()
        fn(rec)
        name_, a_, k_ = rec.call
        fn = lambda e: getattr(e, name_)(*a_, **k_)
        need = self._collect(eng, reads, writes)
        if inc:
            self.n[eng] += 1
            tk = (self.sem[eng], self.n[eng])
            for t, isw in self.pending[eng]:
                if isw:
                    t.last_w = tk
                    t.readers = []
                else:
                    t.readers.append(tk)
            self.pending[eng] = []
            for t in reads:
                t.readers.append(tk)
            for t in writes:
                t.last_w = tk
                t.readers = []
        else:
            for t in reads:
                self.pending[eng].append((t, False))
            for t in writes:
                self.pending[eng].append((t, True))
        self.ops[eng].append((need, fn, inc, None))

    def dma(self, eng, out, in_, reads=(), writes=(), owner=None, **kw):
        need = self._collect(eng, reads, writes)
        if owner is None:
            owner = writes[0] if writes else reads[0]
        if owner.dsem is None:
            owner.dsem = self.stack.enter_context(self.nc.semaphore(f"d{self.nd}"))
            self.nd += 1
        owner.dcount += 1
        tk = (owner.dsem, 16 * owner.dcount)
        for t in reads:
            t.readers.append(tk)
        for t in writes:
            t.last_w = tk
            t.readers = []
        fn = lambda e: e.dma_start(out=out, in_=in_, **kw)
        self.ops[eng].append((need, fn, False, (owner.dsem, 16)))
        return tk

    def wait_all(self, eng, tickets):
        need = []
        for s, v in tickets:
            if self.waited[eng].get(s, 0) < v:
                self.waited[eng][s] = v
                need.append((s, v))
        self.ops[eng].append((need, None, False, None))

    def replay(self):
        nc = self.nc
        for e in ENGS:
            assert not self.pending[e], f"pending tiles on {e}"
        with nc.Block() as block:
            def run(name):
                def f(e):
                    for need, fn, inc, dinc in self.ops[name]:
                        for s, v in need:
                            e.wait_ge(s, v)
                        if fn is None:
                            continue
                        ins = fn(e)
                        if inc:
                            ins.then_inc(self.sem[name], 1)
                        if dinc is not None:
                            ins.then_inc(dinc[0], dinc[1])
                return f
            block.tensor(run("pe"))
            block.scalar(run("act"))
            block.vector(run("dve"))
            block.gpsimd(run("pool"))
            block.sync(run("sp"))


def build_nc(n_layers=DEPTH, debug=(), dbg_l=0):
    nc = bass.Bass("TRN2", target_bir_lowering=False)
    dt_in = lambda name, shape: nc.dram_tensor(name, shape, F32, kind="ExternalInput")
    x_d = dt_in("x", [S, D]).ap()
    mixg_d = dt_in("mix_norm_g", [DEPTH, D])
    win_d = dt_in("w_in", [DEPTH, D, 2560]).ap()
    qg_d = dt_in("q_norm_g", [DEPTH, 64])
    kg_d = dt_in("k_norm_g", [DEPTH, 64])
    tabx_d = dt_in("tabx", [DEPTH, 8, TABX])
    sgug_d = dt_in("sgu_norm_g", [DEPTH, 512])
    wsp_d = dt_in("w_spatial", [DEPTH, 8, 128, 128]).ap()
    bsp_d = dt_in("b_spatial", [DEPTH, 8, 128]).ap()
    attg_d = dt_in("att_out_norm_g", [DEPTH, 512])
    gmg_d = dt_in("gmlp_out_norm_g", [DEPTH, 512])
    wout_d = dt_in("w_out", [DEPTH, D, D]).ap()
    ffng_d = dt_in("ffn_norm_g", [DEPTH, D])
    wfi_d = dt_in("w_ffn_in", [DEPTH, D, 2 * DFF]).ap()
    wfo_d = dt_in("w_ffn_out", [DEPTH, DFF, D]).ap()
    ident_d = dt_in("ident", [128, 128]).ap()
    jrev_d = dt_in("jrev", [128, 128]).ap()
    gind_d = dt_in("gind", [128, 512]).ap()
    y_d = nc.dram_tensor("y", [S, D], F32, kind="ExternalOutput").ap()
    dbg_tk = []
    HF = FC // 2

    with ExitStack() as st:
        P = Prog(nc, st)
        _cnt = [0]

        def sb(stack, name, shape, dt):
            _cnt[0] += 1
            return stack.enter_context(nc.sbuf_tensor(f"sb{_cnt[0]}_{name}", shape, dt))

        def sbT(stack, name, shape, dt):
            b = sb(stack, name, shape, dt)
            return b, P.T(name, b)

        def sbN(stack, name, shape, dt, n):
            bs = [sb(stack, f"{name}{i}", shape, dt) for i in range(n)]
            return bs, [P.T(f"{name}{i}", bs[i]) for i in range(n)]

        def dump(name, buf, tiles, shape, dt):
            dd = nc.dram_tensor("dbg_" + name, shape, dt, kind="ExternalOutput").ap()
            dbg_tk.append(P.dma("sp", dd, buf[:], reads=list(tiles)))

        def blk(n):
            return slice(n * 512, (n + 1) * 512)

        def tile_(t):
            return slice(t * 128, (t + 1) * 128)

        flat = lambda ll: [x for y in ll for x in y]

        xT = sb(st, "xT", [128, DC, S], F32)
        t_xT = [[P.T(f"xT{n}_{c}", xT) for c in range(DC)] for n in range(NB)]
        hT = sb(st, "hT", [128, DC, S], BF16)
        t_hT = [[P.T(f"hT{n}_{c}", hT) for c in range(DC)] for n in range(NB)]
        mixTg = sb(st, "mixTg", [128, 4, S], BF16)
        t_mixTg = [P.T(f"mTg{t}", mixTg) for t in range(NT)]
        ident, t_ident = sbT(st, "ident", [128, 128], F32)
        jrev, t_jrev = sbT(st, "jrev", [128, 128], F32)
        identb, t_identb = sbT(st, "identb", [128, 128], BF16)
        ones_bf, t_ones = sbT(st, "ones_bf", [128, 128], BF16)
        bd_bf, t_bd = sbT(st, "bd_bf", [128, 128], BF16)
        mhalf, t_mhalf = sbT(st, "mhalf", [128, 1], F32)
        mixg = sb(st, "mixg", [128, DEPTH, 8], F32); t_mixg = [P.T(f"mixg{l}", mixg) for l in range(DEPTH)]
        ffng = sb(st, "ffng", [128, DEPTH, 8], F32); t_ffng = [P.T(f"ffng{l}", ffng) for l in range(DEPTH)]
        attg = sb(st, "attg", [128, DEPTH, 4], F32); t_attg = [P.T(f"attg{l}", attg) for l in range(DEPTH)]
        gmg = sb(st, "gmg", [128, DEPTH, 4], F32); t_gmg = [P.T(f"gmg{l}", gmg) for l in range(DEPTH)]
        qg = sb(st, "qg", [128, DEPTH], F32); t_qg = [[P.T(f"qg{l}{h}", qg) for h in range(2)] for l in range(DEPTH)]
        qg8 = sb(st, "qg8", [128, DEPTH], F32); t_qg8 = [P.T(f"qg8{l}", qg8) for l in range(DEPTH)]
        kg = sb(st, "kg", [128, DEPTH], F32); t_kg = [[P.T(f"kg{l}{h}", kg) for h in range(2)] for l in range(DEPTH)]
        ring = [sb(st, f"ring{i}", [128, 2048], BF16) for i in range(NR)]
        t_ring = [[P.T(f"ring{i}a", ring[i]), P.T(f"ring{i}b", ring[i])] for i in range(NR)]
        pAll = st.enter_context(nc.psum_tensor("pAll", [128, 4096], F32))
        pB = pAll[:, 2048:3072]
        pC = pAll[:, 3072:4096]
        t_bank = [P.T(f"bank{i}") for i in range(8)]

        def bank(i):
            return pAll[:, i * 512:(i + 1) * 512]

        loads = []
        lidx = {}

        def add_load(key, view, src):
            lidx[key] = len(loads)
            if isinstance(src, list):
                loads.append((view, src))
            else:
                loads.append((view, [(view, src, (0, 1))]))

        v_half = lambda b: b[:].rearrange("p (c n) -> p c n", c=4)
        v_blk = lambda b: b[:, 0:1024].rearrange("p (c n) -> p c n", c=8)
        v_blk2 = lambda b: b[:].rearrange("p (c n) -> p c n", c=8)
        v_w1 = lambda b: b[:].rearrange("p (g c n) -> p g c n", g=2, c=8)
        v_w1g = lambda b: b[:, 0:1024].rearrange("p (c n) -> p c n", c=8)
        v_w1u = lambda b: b[:, 1024:2048].rearrange("p (c n) -> p c n", c=8)
        v_w2 = lambda b: b[:, 0:HF * 128].rearrange("p (f n) -> p f n", f=HF)
        for l in range(n_layers):
            win_v = win_d[l].rearrange("(c p) n -> p c n", p=128)
            for nm, c0 in (("Wu", 1536), ("Wg", 2048), ("Wv", 1024)):
                for hc in range(2):
                    add_load((l, nm, hc), v_half, win_v[:, hc * 4:(hc + 1) * 4, c0:c0 + 512])
            for fb in range(8):
                add_load((l, "Wqk", fb), v_blk, win_v[:, :, fb * 128:(fb + 1) * 128])
            wout_v = wout_d[l].rearrange("(c p) n -> p c n", p=128)
            for k in range(4):
                add_load((l, "Wo2", k), v_blk2, wout_v[:, :, k * 256:(k + 1) * 256])
            wfi_v = wfi_d[l].rearrange("(c p) (g f n) -> p c g f n", p=128, g=2, n=128)
            for half in range(2):
                for fc in range(HF):
                    f = half * HF + fc
                    add_load((l, "W1", f), v_w1, [(v_w1g, wfi_v[:, :, 0, f, :], (0,)), (v_w1u, wfi_v[:, :, 1, f, :], (1,))])
                ngs = 2 if (half == 1 and l == n_layers - 1) else 1
                for ng in range(ngs):
                    for db in range(8):
                        src = wfo_d[l][half * HF * 128:(half + 1) * HF * 128, db * 128:(db + 1) * 128].rearrange(
                            "(f p) n -> p f n", p=128)
                        add_load((l, "W2", half, ng, db), v_w2, src)
        issued = [0]

        def issue_upto(k):
            while issued[0] <= k and issued[0] < len(loads):
                j = issued[0]
                view, parts = loads[j]
                for (dfn, src, sel) in parts:
                    P.dma("pool", dfn(ring[j % NR]), src, writes=[t_ring[j % NR][k2] for k2 in sel])
                issued[0] += 1

        def done(key):
            issue_upto(lidx[key] + NR)

        def W(key):
            j = lidx[key]
            assert j < issued[0], f"load {key} not issued"
            view, _ = loads[j]
            return view(ring[j % NR]), list(t_ring[j % NR])

        P.dma("sp", ident[:], ident_d, writes=[t_ident])
        P.dma("sp", jrev[:], jrev_d, writes=[t_jrev])
        P.op("dve", lambda e: e.memset(ones_bf[:], 1.0), writes=[t_ones])
        P.op("dve", lambda e: e.memset(bd_bf[:], 0.0), writes=[t_bd])
        P.op("dve", lambda e: e.memset(bd_bf[0:64, 0:64], 1.0), writes=[t_bd])
        P.op("dve", lambda e: e.memset(bd_bf[64:128, 64:128], 1.0), writes=[t_bd])
        P.op("dve", lambda e: e.memset(mhalf[:], -0.5), writes=[t_mhalf])
        P.op("dve", lambda e: e.tensor_copy(identb[:], ident[:]), reads=[t_ident], writes=[t_identb])

        def load_params():
            for l in range(DEPTH):
                def colload(dst, src_t, n, c, tl):
                    src = bass.AP(src_t, l * n, [[1, 128], [128, c]])
                    P.dma("sp", dst, src, writes=[tl], allow_slow_non_contiguous=True)
                colload(mixg[:, l, :], mixg_d, D, 8, t_mixg[l])
                colload(ffng[:, l, :], ffng_d, D, 8, t_ffng[l])
                colload(attg[:, l, :], attg_d, 512, 4, t_attg[l])
                colload(gmg[:, l, :], gmg_d, 512, 4, t_gmg[l])
                for hb in range(2):
                    P.dma("sp", qg[hb * 64:(hb + 1) * 64, l:l + 1], bass.AP(qg_d, l * 64, [[1, 64], [1, 1]]),
                          writes=[t_qg[l][hb]])
                    P.dma("sp", kg[hb * 64:(hb + 1) * 64, l:l + 1], bass.AP(kg_d, l * 64, [[1, 64], [1, 1]]),
                          writes=[t_kg[l][hb]])
                P.op("dve", lambda e: e.tensor_scalar(qg8[:, l:l + 1], qg[:, l:l + 1], 0.125, None, ALU.mult),
                     reads=t_qg[l], writes=[t_qg8[l]])

        NQ = 6

        def norm_alloc(stack):
            sq, t_sq = sbN(stack, "nsq", [128, 512], BF16, NQ)
            rs, t_rs = sbN(stack, "nrs", [128, 512], F32, 2)
            return (sq, t_sq, rs, t_rs)

        def norm_block(N, gcols, t_g, l, n):
            sq, t_sq, rs, t_rs = N
            b = 6 + n % 2
            for c in range(DC):
                k = (n * DC + c) % NQ
                if c % 3 == 1:
                    P.op("pool", lambda e: e.tensor_tensor(sq[k][:], xT[:, c, blk(n)], xT[:, c, blk(n)], ALU.mult),
                         reads=[t_xT[n][c]], writes=[t_sq[k]])
                else:
                    P.op("act", lambda e: e.activation(sq[k][:], xT[:, c, blk(n)], AF.Square),
                         reads=[t_xT[n][c]], writes=[t_sq[k]])
                P.op("pe", lambda e: e.matmul(bank(b), lhsT=ones_bf[:], rhs=sq[k][:],
                                              start=(c == 0), stop=(c == DC - 1)),
                     reads=[t_ones, t_sq[k]], writes=[t_bank[b]], inc=True)
            r = n % 2
            P.op("act", lambda e: e.activation(rs[r][:], bank(b), AF.Ln, bias=EPS, scale=1.0 / D),
                 reads=[t_bank[b]], writes=[t_rs[r]])
            P.op("act", lambda e: e.activation(rs[r][:], rs[r][:], AF.Exp, scale=-0.5),
                 reads=[t_rs[r]], writes=[t_rs[r]])
            for c in range(DC):
                P.op("dve", lambda e: e.scalar_tensor_tensor(
                    out=hT[:, c, blk(n)], in0=xT[:, c, blk(n)], scalar=gcols[:, l, c:c + 1], in1=rs[r][:],
                    op0=ALU.mult, op1=ALU.mult),
                    reads=[t_xT[n][c], t_g, t_rs[r]], writes=[t_hT[n][c]])

        out_tk = []
        out_ctx = {}

        def out_tiles(t0_, t1_):
            yo, t_yo, NY = out_ctx["yo"], out_ctx["t_yo"], out_ctx["NY"]
            for tt in range(t0_, t1_):
                sl_ = tt % NY
                for half in range(2):
                    b = 2 * (tt % 2) + half
                    for cc in range(4):
                        c = half * 4 + cc
                        P.op("pe", lambda e: e.transpose(
                            bank(b)[:, cc * 128:(cc + 1) * 128], xT[:, c, tile_(tt)], ident[:]),
                            reads=[t_xT[tt // 4][c], t_ident], writes=[t_bank[b]], inc=(cc == 3))
                    if half == 0:
                        P.op("act", lambda e: e.copy(yo[sl_][:, 0:512], bank(b)), reads=[t_bank[b]], writes=[t_yo[sl_][0]])
                    else:
                        P.op("dve", lambda e: e.tensor_copy(yo[sl_][:, 512:1024], bank(b)), reads=[t_bank[b]], writes=[t_yo[sl_][1]])
                out_tk.append(P.dma("sp", y_d[tt * 128:(tt + 1) * 128, :], yo[sl_][:], reads=t_yo[sl_], owner=t_yo[sl_][0]))

        for l in range(n_layers):
            with ExitStack() as sl:
                sgug, t_sgug = sbT(sl, "sgug", [128, 512], F32)
                wnat, t_wnat = sbT(sl, "wnat", [128, 8, 128], F32)
                wT, t_wT = sbT(sl, "wT", [128, 8, 128], BF16)
                bnat, t_bnat = sbT(sl, "bnat", [8, 128], F32)
                bhi, t_bhi = sbT(sl, "bhi", [128, 128], BF16)
                blo, t_blo = sbT(sl, "blo", [128, 128], BF16)
                gindb, t_gindb = sbT(sl, "gindb", [128, 512], BF16)
                P.dma("pool", gindb[:], gind_d, writes=[t_gindb])
                P.dma("sp", sgug[:], bass.AP(sgug_d, l * 512, [[0, 128], [1, 512]]), writes=[t_sgug])
                P.dma("sp", wnat[:], wsp_d[l].rearrange("g t s -> t g s"), writes=[t_wnat])
                P.dma("sp", bnat[:], bsp_d[l], writes=[t_bnat])
                P.op("dve", lambda e: e.memset(bhi[:], 0.0), writes=[t_bhi])
                P.op("dve", lambda e: e.memset(blo[:], 0.0), writes=[t_blo])
                P.op("dve", lambda e: e.tensor_copy(bhi[0:8, :], bnat[:]), reads=[t_bnat], writes=[t_bhi])
                P.op("dve", lambda e: e.tensor_tensor(blo[0:8, :], bnat[:], bhi[0:8, :], ALU.subtract),
                     reads=[t_bnat, t_bhi], writes=[t_blo])

                if l == 0:
                    with ExitStack() as s0:
                        NX = 3
                        xin, t_xin = sbN(s0, "xin", [128, D], F32, NX)
                        N0 = norm_alloc(s0)
                        for tt in range(NT):
                            sl_ = tt % NX
                            P.dma("sp", xin[sl_][:], x_d[tt * 128:(tt + 1) * 128, :], writes=[t_xin[sl_]])
                            if tt == 1:
                                load_params()
                            if tt == 3:
                                issue_upto(NR - 1)
                            for half in range(2):
                                b = 2 * sl_ + half
                                for cc in range(4):
                                    c = half * 4 + cc
                                    P.op("pe", lambda e: e.transpose(
                                        bank(b)[:, cc * 128:(cc + 1) * 128], xin[sl_][:, c * 128:(c + 1) * 128], ident[:]),
                                        reads=[t_xin[sl_], t_ident], writes=[t_bank[b]], inc=(cc == 3))
                                dst = xT[:, half * 4:half * 4 + 4, tile_(tt)]
                                src = bank(b).rearrange("p (c t) -> p c t", c=4)
                                tw = t_xT[tt // 4][half * 4:half * 4 + 4]
                                if half == 0:
                                    P.op("act", lambda e: e.copy(dst, src), reads=[t_bank[b]], writes=tw)
                                else:
                                    P.op("dve", lambda e: e.tensor_copy(dst, src), reads=[t_bank[b]], writes=tw)
                            if tt % 4 == 3 and tt >= 7:
                                norm_block(N0, mixg, t_mixg[l], l, tt // 4 - 1)
                        norm_block(N0, mixg, t_mixg[l], l, NB - 1)
                        P.retire(t_xin + N0[1] + N0[3])

                for hb in range(2):
                    for gg in range(4):
                        g = hb * 4 + gg
                        P.op("pe", lambda e: e.transpose(
                            bank(4 + hb)[:, gg * 128:(gg + 1) * 128], wnat[:, g, :], ident[:]),
                            reads=[t_wnat, t_ident], writes=[t_bank[4 + hb]], inc=(gg == 3))
                    P.op("dve", lambda e: e.tensor_copy(
                        wT[:, hb * 4:hb * 4 + 4, :], bank(4 + hb).rearrange("p (g t) -> p g t", g=4)),
                        reads=[t_bank[4 + hb]], writes=[t_wT])
                P.op("dve", lambda e: e.memset(wT[64:128, :, 0:64], 0.0), writes=[t_wT])

                if l > 0:
                    with ExitStack() as sn:
                        Nn = norm_alloc(sn)
                        for n in range(NB):
                            norm_block(Nn, mixg, t_mixg[l], l, n)
                        P.retire(Nn[1] + Nn[3])
                if "hT" in debug and l == dbg_l:
                    dump("hT", hT, flat(t_hT), [128, DC, S], BF16)

                with ExitStack() as sg:
                    NS = 7
                    u_sb, t_u = sbN(sg, "u_sb", [128, 512], F32, NS)
                    vgg, t_vgg = sbN(sg, "vgg", [128, 512], F32, NS)
                    vgn, t_vgn = sbN(sg, "vgn", [128, 512], BF16, NS)
                    gm, t_gm = sbN(sg, "gm", [128, 512], F32, NS)
                    junk, t_junk = sbT(sg, "gjunk", [128, 512], BF16)
                    st4 = [sb(sg, f"gst{i}", [128, 4], F32) for i in range(NS)]
                    t_st = [[P.T(f"gst{i}{j}", st4[i]) for j in range(4)] for i in range(NS)]
                    Wu = [W((l, "Wu", hc)) for hc in range(2)]
                    Wg = [W((l, "Wg", hc)) for hc in range(2)]

                    def gA(tt):
                        bU, bG = 2 * (tt % 2), 2 * (tt % 2) + 1
                        n = tt // 4
                        for (Wx, bX) in ((Wg, bG), (Wu, bU)):
                            for c in range(DC):
                                wv, wt = Wx[c // 4]
                                P.op("pe", lambda e: e.matmul(
                                    bank(bX), lhsT=hT[:, c, tile_(tt)], rhs=wv[:, c % 4, :], start=(c == 0), stop=(c == DC - 1)),
                                    reads=[t_hT[n][c]] + wt, writes=[t_bank[bX]], inc=(c == DC - 1))

                    def rstd_chain(stt, t_s, i0):
                        P.op("pool", lambda e: e.tensor_scalar(stt[:, i0:i0 + 1], stt[:, i0:i0 + 1], 1.0 / 512, EPS, ALU.mult, ALU.add),
                             reads=[t_s[i0]], writes=[t_s[i0]])
                        P.op("pool", lambda e: e.tensor_tensor(stt[:, i0 + 1:i0 + 2], stt[:, i0:i0 + 1], mhalf[:], ALU.pow),
                             reads=[t_s[i0], t_mhalf], writes=[t_s[i0 + 1]])

                    def gB(tt):
                        s_ = tt % NS
                        bU, bG = 2 * (tt % 2), 2 * (tt % 2) + 1
                        P.op("act", lambda e: e.activation(vgg[s_][:], bank(bG), AF.Gelu_apprx_tanh),
                             reads=[t_bank[bG]], writes=[t_vgg[s_]])
                        P.op("act", lambda e: e.activation(junk[:], vgg[s_][:], AF.Square, accum_out=st4[s_][:, 0:1]),
                             reads=[t_vgg[s_]], writes=[t_junk, t_st[s_][0]])
                        rstd_chain(st4[s_], t_st[s_], 0)
                        P.op("act", lambda e: e.activation(u_sb[s_][:], bank(bU), AF.Gelu_apprx_tanh),
                             reads=[t_bank[bU]], writes=[t_u[s_]])
                        P.op("dve", lambda e: e.scalar_tensor_tensor(
                            out=vgn[s_][:], in0=vgg[s_][:], scalar=st4[s_][:, 1:2], in1=sgug[:], op0=ALU.mult, op1=ALU.mult),
                            reads=[t_vgg[s_], t_st[s_][1], t_sgug], writes=[t_vgn[s_]])

                    def gC(tt):
                        s_ = tt % NS
                        bM = 4 + tt % 2
                        P.op("pe", lambda e: e.matmul(bank(bM), lhsT=bhi[:], rhs=gindb[:], start=True, stop=False),
                             reads=[t_bhi, t_gindb], writes=[t_bank[bM]], inc=False)
                        P.op("pe", lambda e: e.matmul(bank(bM), lhsT=blo[:], rhs=gindb[:], start=False, stop=False),
                             reads=[t_blo, t_gindb], writes=[t_bank[bM]], inc=False)
                        for g in range(8):
                            P.op("pe", lambda e: e.matmul(
                                bank(bM)[:, g * 64:(g + 1) * 64], lhsT=wT[:, g, :], rhs=vgn[s_][:, g * 64:(g + 1) * 64],
                                start=False, stop=(g == 7)),
                                reads=[t_wT, t_vgn[s_]], writes=[t_bank[bM]], inc=(g == 7))

                    def gD(tt):
                        s_ = tt % NS
                        bM = 4 + tt % 2
                        P.op("dve", lambda e: e.tensor_tensor(gm[s_][:], bank(bM), u_sb[s_][:], ALU.mult),
                             reads=[t_bank[bM], t_u[s_]], writes=[t_gm[s_]])
                        P.op("act", lambda e: e.activation(junk[:], gm[s_][:], AF.Square, accum_out=st4[s_][:, 2:3]),
                             reads=[t_gm[s_]], writes=[t_junk, t_st[s_][2]])
                        rstd_chain(st4[s_], t_st[s_], 2)
                        P.op("act", lambda e: e.activation(gm[s_][:], gm[s_][:], AF.Copy, scale=st4[s_][:, 3:4]),
                             reads=[t_gm[s_], t_st[s_][3]], writes=[t_gm[s_]])

                    def gE(tt):
                        s_ = tt % NS
                        bTr = 6 + tt % 2
                        for j in range(4):
                            P.op("pe", lambda e: e.transpose(
                                bank(bTr)[:, j * 128:(j + 1) * 128], gm[s_][:, j * 128:(j + 1) * 128], ident[:]),
                                reads=[t_gm[s_], t_ident], writes=[t_bank[bTr]], inc=(j == 3))
                        P.op("dve", lambda e: e.tensor_tensor(
                            mixTg[:, :, tile_(tt)], bank(bTr).rearrange("p (j t) -> p j t", j=4),
                            gmg[:, l, :].rearrange("p (j o) -> p j o", o=1).to_broadcast([128, 4, 128]), ALU.mult),
                            reads=[t_bank[bTr], t_gmg[l]], writes=[t_mixTg[tt]])

                    for s_ in range(NT + 6):
                        if 0 <= s_ - 3 < NT:
                            gC(s_ - 3); gD(s_ - 3)
                        if 0 <= s_ - 6 < NT:
                            gE(s_ - 6)
                        if s_ < NT:
                            gA(s_); gB(s_)
                        if s_ == NT - 1:
                            for hc in range(2):
                                done((l, "Wu", hc)); done((l, "Wg", hc))
                    P.retire([t_junk] + t_u + t_vgg + t_vgn + t_gm + flat(t_st))
                P.retire([t_sgug, t_wnat, t_wT, t_bnat, t_bhi, t_blo, t_gindb])
            if "mixTg" in debug and l == dbg_l:
                dump("mixTg", mixTg, t_mixTg, [128, 4, S], BF16)

            with ExitStack() as sa:
                bias = sb(sa, "expB", [128, 4, 5, 2, 128], BF16)
                t_bias = [P.T(f"expB{h}", bias) for h in range(8)]
                V = sb(sa, "V", [128, NT, 8, 65], BF16)
                t_V = [P.T(f"V{t}", V) for t in range(NT)]
                qT = sb(sa, "qT", [128, 4, S], BF16)
                kT = sb(sa, "kT", [128, 4, S], BF16)
                t_qT = [[P.T(f"qT{f}{n}", qT) for n in range(NB)] for f in range(4)]
                t_kT = [[P.T(f"kT{f}{n}", kT) for n in range(NB)] for f in range(4)]
                P.op("dve", lambda e: e.memset(V[:, :, :, 64:65], 1.0), writes=t_V)

                with ExitStack() as shk:
                    hk, t_hk = sbN(shk, "hk", [128, 640], F32, 2)

                    def bias_head(h):
                        k = h % 2
                        src = bass.AP(tabx_d, (l * 8 + h) * TABX + 1, [[1, 128], [128, 5], [1, 128]])
                        P.dma("sp", hk[k][:].rearrange("p (d q) -> p d q", d=5), src, writes=[t_hk[k]])
                        b0 = 4 + 2 * k
                        pX = pB if k == 0 else pC
                        P.op("pe", lambda e: e.matmul(pX[:, 0:512], lhsT=jrev[:], rhs=hk[k][:, 0:512], start=True, stop=True),
                             reads=[t_jrev, t_hk[k]], writes=[t_bank[b0]])
                        P.op("pe", lambda e: e.matmul(pX[:, 512:640], lhsT=jrev[:], rhs=hk[k][:, 512:640], start=True, stop=True),
                             reads=[t_jrev, t_hk[k]], writes=[t_bank[b0 + 1]])
                        P.op("act", lambda e: e.activation(bias[:, h // 2, :, h % 2, :],
                                                          pX[:, 0:640].rearrange("p (d q) -> p d q", d=5), AF.Exp),
                             reads=[t_bank[b0], t_bank[b0 + 1]], writes=[t_bias[h]])
                        P.op("dve", lambda e: e.memset(bias[64:128, h // 2, 0, h % 2, 0:64], 0.0), writes=[t_bias[h]])
                        P.op("dve", lambda e: e.memset(bias[0:64, h // 2, 4, h % 2, 64:128], 0.0), writes=[t_bias[h]])

                    Wv = [W((l, "Wv", hc)) for hc in range(2)]
                    for tt in range(NT):
                        b = tt % 4
                        n = tt // 4
                        for c in range(DC):
                            wv, wt = Wv[c // 4]
                            P.op("pe", lambda e: e.matmul(
                                bank(b), lhsT=hT[:, c, tile_(tt)], rhs=wv[:, c % 4, :], start=(c == 0), stop=(c == DC - 1)),
                                reads=[t_hT[n][c]] + wt, writes=[t_bank[b]], inc=(c == DC - 1))
                        src = bank(b).rearrange("p (h d) -> p h d", h=8)
                        if tt % 2 == 0:
                            P.op("act", lambda e: e.copy(V[:, tt, :, 0:64], src), reads=[t_bank[b]], writes=[t_V[tt]])
                        else:
                            P.op("dve", lambda e: e.tensor_copy(V[:, tt, :, 0:64], src), reads=[t_bank[b]], writes=[t_V[tt]])
                        if tt % 2 == 1:
                            bias_head(tt // 2)
                    for hc in range(2):
                        done((l, "Wv", hc))
                    P.retire(t_hk)

                with ExitStack() as sq_:
                    NQK = 3
                    qsq, t_qsq = sbN(sq_, "qsq", [128, 512], BF16, NQK)
                    qrs, t_qrs = sbN(sq_, "qrs", [128, 512], F32, NQK)
                    NIT = 32

                    def qA(it):
                        fb, n = it // 4, it % 4
                        bQ = it % 3
                        wv, wt = W((l, "Wqk", fb))
                        for c in range(DC):
                            P.op("pe", lambda e: e.matmul(
                                bank(bQ), lhsT=wv[:, c, :], rhs=hT[:, c, blk(n)], start=(c == 0), stop=(c == DC - 1)),
                                reads=wt + [t_hT[n][c]], writes=[t_bank[bQ]], inc=(c == DC - 1))
                        if n == 3:
                            done((l, "Wqk", fb))
                        r = it % NQK
                        P.op("act", lambda e: e.activation(qsq[r][:], bank(bQ), AF.Square),
                             reads=[t_bank[bQ]], writes=[t_qsq[r]])

                    def qC(it):
                        fb, n = it // 4, it % 4
                        bQ = it % 3
                        bS = 4 + it % 2
                        r = it % NQK
                        isq = fb < 4
                        dstT = qT if isq else kT
                        t_dst = t_qT if isq else t_kT
                        gcol = qg8 if isq else kg
                        t_gc = [t_qg8[l]] if isq else t_kg[l]
                        P.op("pe", lambda e: e.matmul(bank(bS), lhsT=bd_bf[:], rhs=qsq[r][:], start=True, stop=True),
                             reads=[t_bd, t_qsq[r]], writes=[t_bank[bS]])
                        P.op("act", lambda e: e.activation(qrs[r][:], bank(bS), AF.Ln, bias=EPS, scale=1.0 / 64),
                             reads=[t_bank[bS]], writes=[t_qrs[r]])
                        P.op("act", lambda e: e.activation(qrs[r][:], qrs[r][:], AF.Exp, scale=-0.5),
                             reads=[t_qrs[r]], writes=[t_qrs[r]])
                        P.op("dve", lambda e: e.scalar_tensor_tensor(
                            out=dstT[:, fb % 4, blk(n)], in0=bank(bQ), scalar=gcol[:, l:l + 1], in1=qrs[r][:],
                            op0=ALU.mult, op1=ALU.mult),
                            reads=[t_bank[bQ]] + t_gc + [t_qrs[r]], writes=[t_dst[fb % 4][n]])

                    for s_ in range(NIT + 1):
                        if s_ < NIT:
                            qA(s_)
                        if s_ - 1 >= 0:
                            qC(s_ - 1)
                    P.retire(t_qsq + t_qrs)
                if "qT" in debug and l == dbg_l:
                    dump("qT", qT, flat(t_qT), [128, 4, S], BF16)
                    dump("kT", kT, flat(t_kT), [128, 4, S], BF16)
                    dump("V", V, t_V, [128, NT, 8, 65], BF16)

                P.retire(flat(t_hT))
                t_mixTa = [P.T(f"mTa{t}", hT) for t in range(NT)]
                mixTa = hT

                with ExitStack() as sat:
                    NQM = 3
                    qm, t_qm = sbN(sat, "qm", [128, 256], BF16, NQM)
                    NEP = 3
                    EP, t_EP = sbN(sat, "EP", [128, 1280], BF16, NEP)
                    rc, t_rc = sbT(sat, "rc", [128, 8], F32)
                    atok, t_atok = sbT(sat, "atok", [128, 512], F32)
                    anb, t_anb = sbT(sat, "anb", [128, 512], BF16)
                    ast = sb(sat, "ast", [128, 2], F32)
                    t_ast = [P.T(f"ast{j}", ast) for j in range(2)]
                    for i in range(NQM):
                        P.op("pool", lambda e: e.memset(qm[i][:], 0.0), writes=[t_qm[i]])

                    P.retire(t_bank[0:6])
                    t_S = [P.T(f"S{i}") for i in range(2)]
                    NP = NT * 4
                    trb = pAll[:, 7 * 512 + 128:7 * 512 + 384].bitcast(BF16)

                    def aQ(p):
                        m, hp = p // 4, p % 4
                        k = p % NQM
                        P.op("pool", lambda e: e.tensor_copy(qm[k][0:64, 0:128], qT[0:64, hp, tile_(m)]),
                             reads=[t_qT[hp][m // 4]], writes=[t_qm[k]])
                        P.op("pool", lambda e: e.tensor_copy(qm[k][64:128, 128:256], qT[64:128, hp, tile_(m)]),
                             reads=[t_qT[hp][m // 4]], writes=[t_qm[k]])

                    def aA(p):
                        m, hp = p // 4, p % 4
                        nb_ = min(m, 4) + 1
                        st_ = p % 2
                        c0 = st_ * 1536
                        k = p % NQM
                        for d in range(nb_):
                            j = m - d
                            P.op("pe", lambda e: e.matmul(
                                pAll[:, c0 + d * 256:c0 + (d + 1) * 256], lhsT=kT[:, hp, tile_(j)], rhs=qm[k][:],
                                start=True, stop=True),
                                reads=[t_kT[hp][j // 4], t_qm[k]], writes=[t_S[st_]], inc=(d == nb_ - 1))
                        w = nb_ * 256
                        ep = p % NEP
                        P.op("act", lambda e: e.activation(EP[ep][:, 0:w], pAll[:, c0:c0 + w], AF.Exp),
                             reads=[t_S[st_]], writes=[t_EP[ep]])
                        P.op("dve", lambda e: e.tensor_tensor(
                            EP[ep][:, 0:w], EP[ep][:, 0:w], bias[:, hp, 0:nb_, :, :].rearrange("p d u q -> p (d u q)"), ALU.mult),
                            reads=[t_EP[ep], t_bias[2 * hp], t_bias[2 * hp + 1]], writes=[t_EP[ep]])

                    deferred = {}

                    def aC(p, step):
                        m, hp = p // 4, p % 4
                        nb_ = min(m, 4) + 1
                        ep = p % NEP
                        for u in range(2):
                            h = 2 * hp + u
                            ob = 6 if h < 7 else 7
                            oc = (6 * 512 + h * 65) if h < 7 else 7 * 512
                            for d in range(nb_):
                                j = m - d
                                P.op("pe", lambda e: e.matmul(
                                    pAll[:, oc:oc + 65], lhsT=EP[ep][:, (d * 2 + u) * 128:(d * 2 + u + 1) * 128], rhs=V[:, j, h, :],
                                    start=(d == 0), stop=(d == nb_ - 1)),
                                    reads=[t_EP[ep], t_V[j]], writes=[t_bank[ob]], inc=(d == nb_ - 1))
                        if hp == 3:
                            aD1(m)
                            deferred.setdefault(step + 1, []).append((aD2, m))
                            deferred.setdefault(step + 2, []).append((aD3, m))
                            deferred.setdefault(step + 3, []).append((aE, m))

                    def aD1(m):
                        ov6 = pAll[:, 6 * 512:6 * 512 + 455].rearrange("p (h d) -> p h d", h=7)
                        ov7 = pAll[:, 7 * 512:7 * 512 + 65]
                        P.op("dve", lambda e: e.reciprocal(rc[:, 0:7].rearrange("p (h o) -> p h o", o=1), ov6[:, :, 64:65]),
                             reads=[t_bank[6]], writes=[t_rc])
                        P.op("dve", lambda e: e.reciprocal(rc[:, 7:8], ov7[:, 64:65]), reads=[t_bank[7]], writes=[t_rc])
                        P.op("dve", lambda e: e.tensor_tensor(
                            atok[:, 0:448].rearrange("p (h d) -> p h d", h=7), ov6[:, :, 0:64],
                            rc[:, 0:7].rearrange("p (h o) -> p h o", o=1).to_broadcast([128, 7, 64]), ALU.mult),
                            reads=[t_bank[6], t_rc], writes=[t_atok])
                        P.op("dve", lambda e: e.tensor_scalar(atok[:, 448:512], ov7[:, 0:64], rc[:, 7:8], None, ALU.mult),
                             reads=[t_bank[7], t_rc], writes=[t_atok])

                    def aD2(m):
                        P.op("act", lambda e: e.activation(anb[:], atok[:], AF.Square, accum_out=ast[:, 0:1]),
                             reads=[t_atok], writes=[t_anb, t_ast[0]])
                        P.op("pool", lambda e: e.tensor_scalar(ast[:, 0:1], ast[:, 0:1], 1.0 / 512, EPS, ALU.mult, ALU.add),
                             reads=[t_ast[0]], writes=[t_ast[0]])
                        P.op("pool", lambda e: e.tensor_tensor(ast[:, 1:2], ast[:, 0:1], mhalf[:], ALU.pow),
                             reads=[t_ast[0], t_mhalf], writes=[t_ast[1]])

                    def aD3(m):
                        P.op("act", lambda e: e.activation(anb[:], atok[:], AF.Copy, scale=ast[:, 1:2]),
                             reads=[t_atok, t_ast[1]], writes=[t_anb])

                    def aE(m):
                        for j in range(4):
                            P.op("pe", lambda e: e.transpose(
                                trb[:, j * 128:(j + 1) * 128], anb[:, j * 128:(j + 1) * 128], identb[:]),
                                reads=[t_anb, t_identb], writes=[t_bank[7]], inc=(j == 3))
                        P.op("dve", lambda e: e.tensor_tensor(
                            mixTa[:, 0:4, tile_(m)], trb.rearrange("p (j t) -> p j t", j=4),
                            attg[:, l, :].rearrange("p (j o) -> p j o", o=1).to_broadcast([128, 4, 128]), ALU.mult),
                            reads=[t_bank[7], t_attg[l]], writes=[t_mixTa[m]])

                    aQ(0); aQ(1)
                    for i_ in range(NWARM):
                        P.op("pe", lambda e: e.matmul(bank(7), lhsT=kT[:, 0, 0:128], rhs=qT[:, 0, 0:512], start=True, stop=True),
                             reads=[t_kT[0][0], t_qT[0][0]], writes=[t_bank[7]], inc=(i_ == NWARM - 1))
                    for s_ in range(NP + 6):
                        if s_ + 2 < NP:
                            aQ(s_ + 2)
                        if s_ < NP:
                            aA(s_)
                        if 0 <= s_ - 1 < NP:
                            aC(s_ - 1, s_)
                        for (fn_, m) in deferred.pop(s_, []):
                            fn_(m)
                    assert not deferred
                    P.retire(t_S)
                    t_bank[0:6] = [P.T(f"bank{i}") for i in range(6)]
                    P.retire(t_qm + t_EP + [t_rc, t_atok, t_anb] + t_ast)
                P.retire(t_V + flat(t_qT) + flat(t_kT) + t_bias)
            if "mixTa" in debug and l == dbg_l:
                dump("mixTa", hT, t_mixTa, [128, DC, S], BF16)

            with ExitStack() as sn:
                Nn = norm_alloc(sn)
                Wo2 = [W((l, "Wo2", k)) for k in range(4)]
                new_hT = [None] * NB
                it = 0
                for n in range(NB + 1):
                    if n < NB:
                        for db in range(8):
                            wv, wt = Wo2[db // 2]
                            b = it % 4
                            it += 1
                            for c in range(DC):
                                rhs = mixTa[:, c, blk(n)] if c < 4 else mixTg[:, c - 4, blk(n)]
                                tr = (t_mixTa if c < 4 else t_mixTg)[4 * n:4 * n + 4]
                                P.op("pe", lambda e: e.matmul(
                                    bank(b), lhsT=wv[:, c, (db % 2) * 128:(db % 2) * 128 + 128], rhs=rhs,
                                    start=(c == 0), stop=(c == DC - 1)),
                                    reads=wt + tr, writes=[t_bank[b]], inc=(c == DC - 1))
                            P.op("dve", lambda e: e.tensor_tensor(xT[:, db, blk(n)], xT[:, db, blk(n)], bank(b), ALU.add),
                                 reads=[t_bank[b], t_xT[n][db]], writes=[t_xT[n][db]])
                        P.retire(t_mixTa[4 * n:4 * n + 4])
                        t_hT[n] = [P.T(f"hT{n}_{c}", hT) for c in range(DC)]
                    if n >= 1:
                        norm_block(Nn, ffng, t_ffng[l], l, n - 1)
                for k in range(4):
                    done((l, "Wo2", k))
                P.retire(Nn[1] + Nn[3])
            if "x1" in debug and l == dbg_l:
                dump("x1", xT, flat(t_xT), [128, DC, S], F32)

            with ExitStack() as sf:
                actT = sb(sf, "actT", [128, HF, S], BF16)
                t_actT = [[P.T(f"actT{n}_{f}", actT) for f in range(HF)] for n in range(NB)]
                NSG = 3
                sg_, t_sg = sbN(sf, "sg", [128, 512], F32, NSG)
                last = (l == n_layers - 1)
                if last:
                    NY = 3
                    yo = [sb(sf, f"yo{i}", [128, D], F32) for i in range(NY)]
                    t_yo = [[P.T(f"yo{i}_{h}", yo[i]) for h in range(2)] for i in range(NY)]
                    out_ctx.update(yo=yo, t_yo=t_yo, NY=NY)
                it = 0
                it2 = 0
                for half in range(2):
                    for fc in range(HF):
                        f = half * HF + fc
                        wv, wt = W((l, "W1", f))
                        for n in range(NB):
                            bA = 2 * (it % 2)
                            bB = bA + 1
                            r = it % NSG
                            it += 1
                            for gi, bX in ((0, bA), (1, bB)):
                                for c in range(DC):
                                    P.op("pe", lambda e: e.matmul(
                                        bank(bX), lhsT=wv[:, gi, c, :], rhs=hT[:, c, blk(n)], start=(c == 0), stop=(c == DC - 1)),
                                        reads=wt + [t_hT[n][c]], writes=[t_bank[bX]], inc=(c == DC - 1))
                            P.op("act", lambda e: e.activation(sg_[r][:], bank(bA), AF.Silu),
                                 reads=[t_bank[bA]], writes=[t_sg[r]])
                            P.op("dve", lambda e: e.tensor_tensor(actT[:, fc, blk(n)], sg_[r][:], bank(bB), ALU.mult),
                                 reads=[t_sg[r], t_bank[bB]], writes=[t_actT[n][fc]])
                        done((l, "W1", f))
                    ngs = 2 if (half == 1 and last) else 1
                    for ng in range(ngs):
                        nlist = list(range(NB)) if ngs == 1 else [2 * ng, 2 * ng + 1]
                        for db in range(8):
                            wv, wt = W((l, "W2", half, ng, db))
                            for n in nlist:
                                b = 4 + it2 % 4
                                it2 += 1
                                for fc in range(HF):
                                    P.op("pe", lambda e: e.matmul(
                                        bank(b), lhsT=wv[:, fc, :], rhs=actT[:, fc, blk(n)], start=(fc == 0), stop=(fc == HF - 1)),
                                        reads=wt + [t_actT[n][fc]], writes=[t_bank[b]], inc=(fc == HF - 1))
                                P.op("dve", lambda e: e.tensor_tensor(xT[:, db, blk(n)], xT[:, db, blk(n)], bank(b), ALU.add),
                                     reads=[t_bank[b], t_xT[n][db]], writes=[t_xT[n][db]])
                            done((l, "W2", half, ng, db))
                            if ngs == 2 and ng == 1:
                                out_tiles(db, db + 1)
                if last:
                    out_tiles(8, NT)
                P.retire(flat(t_actT) + t_sg)

        if True:
            P.wait_all("sp", out_tk + dbg_tk)
            P.replay()
    return nc


_NC_CACHE = {}


_GIND = np.zeros((128, 512), dtype=np.float32)
for _g in range(8):
    _GIND[_g, _g * 64:(_g + 1) * 64] = 1.0


def _host_inputs(inputs):
    f32 = lambda a: np.ascontiguousarray(np.asarray(a, dtype=np.float32))
    rel = f32(inputs["rel_bias"])
    tabx = np.concatenate([rel, np.repeat(rel[..., -1:], TABX - rel.shape[-1], axis=-1)], axis=-1)
    shared = {
        "mix_norm_g": f32(inputs["mix_norm_g"]), "w_in": f32(inputs["w_in"]),
        "q_norm_g": f32(inputs["q_norm_g"]), "k_norm_g": f32(inputs["k_norm_g"]),
        "tabx": np.ascontiguousarray(tabx), "sgu_norm_g": f32(inputs["sgu_norm_g"]),
        "w_spatial": f32(inputs["w_spatial"]), "b_spatial": f32(inputs["b_spatial"]),
        "att_out_norm_g": f32(inputs["att_out_norm_g"]), "gmlp_out_norm_g": f32(inputs["gmlp_out_norm_g"]),
        "w_out": f32(inputs["w_out"]), "ffn_norm_g": f32(inputs["ffn_norm_g"]),
        "w_ffn_in": f32(inputs["w_ffn_in"]), "w_ffn_out": f32(inputs["w_ffn_out"]),
        "ident": np.eye(128, dtype=np.float32),
        "jrev": np.ascontiguousarray(np.eye(128, dtype=np.float32)[:, ::-1]),
        "gind": _GIND,
    }
    return shared


def kernel(**inputs):
    x = np.asarray(inputs["x"], dtype=np.float32)
    B = x.shape[0]
    shared = _host_inputs(inputs)
    if "nc" not in _NC_CACHE:
        _NC_CACHE["nc"] = build_nc()
    nc = _NC_CACHE["nc"]
    in_maps = [dict(shared, x=np.ascontiguousarray(x[b])) for b in range(B)]
    res = run_bass_kernel_spmd(nc, in_maps, core_ids=list(range(B)))
    return np.stack([np.asarray(r["y"], dtype=np.float32) for r in res.results], axis=0)
```

```python
import numpy as np
from contextlib import ExitStack
import concourse.bass as bass
import concourse.mybir as mybir
from concourse.bass_utils import run_bass_kernel_spmd

F32 = mybir.dt.float32
BF16 = mybir.dt.bfloat16
ALU = mybir.AluOpType
AF = mybir.ActivationFunctionType

ENGS = ("pe", "act", "dve", "pool", "sp")

S = 2048
D = 1024
NT = 16
NB = 4
DC = 8
DEPTH = 2
DFF = 2816
FC = 22
EPS = 1e-6
NEG = -30000.0
TABX = 768
SAME_ENGINE_WAW = True
NWARM = 24
NR = 5


class T:
    __slots__ = ("name", "last_w", "readers", "dsem", "dcount", "rng")

    def __init__(self, name, prog=None, rng=None):
        self.name = name
        self.last_w = None
        self.rng = rng
        self.readers = []
        if prog is not None:
            if rng is None:
                self.readers = list(prog.floor.items())
            else:
                m = {}
                for (lo, hi, tk) in prog.retired:
                    if lo < rng[1] and rng[0] < hi:
                        for s_, v in tk.items():
                            if m.get(s_, 0) < v:
                                m[s_] = v
                self.readers = list(m.items())
        self.dsem = None
        self.dcount = 0


class _Rec:
    def __init__(self):
        self.call = None

    def __getattr__(self, name):
        def f(*a, **k):
            self.call = (name, a, k)
        return f


class Prog:
    def __init__(self, nc, stack):
        self.nc = nc
        self.stack = stack
        self.ops = {e: [] for e in ENGS}
        self.sem = {e: stack.enter_context(nc.semaphore("s_" + e)) for e in ENGS}
        self.n = {e: 0 for e in ENGS}
        self.waited = {e: {} for e in ENGS}
        self.semkey = {self.sem[e]: e for e in ENGS}
        self.nd = 0
        self.pending = {e: [] for e in ENGS}
        self.floor = {}
        self.retired = []

    def T(self, name, buf=None):
        rng = None
        if buf is not None:
            ml = self.nc.lookup_mloc(buf)
            rng = (int(ml.addr), int(ml.addr) + int(ml.dims[1]))
        return T(name, self, rng)

    def retire(self, tiles):
        for t in tiles:
            tks = list(t.readers)
            if t.last_w is not None:
                tks.append(t.last_w)
            if t.rng is None:
                for s, v in tks:
                    if self.floor.get(s, 0) < v:
                        self.floor[s] = v
            else:
                d = {}
                for s, v in tks:
                    if d.get(s, 0) < v:
                        d[s] = v
                if d:
                    self.retired.append((t.rng[0], t.rng[1], d))

    def _collect(self, eng, reads, writes):
        w = {}
        own = self.sem.get(eng)

        def add(tk, raw):
            if tk is None:
                return
            s, v = tk
            if s is own and not raw and not SAME_ENGINE_WAW:
                return
            if w.get(s, 0) < v:
                w[s] = v
        for t in reads:
            add(t.last_w, True)
        for t in writes:
            add(t.last_w, False)
            for r in t.readers:
                add(r, False)
        need = []
        for s, v in w.items():
            if self.waited[eng].get(s, 0) >= v:
                continue
            if s is own and eng == "pe":
                continue
            if s in self.semkey:
                e2 = self.semkey[s]
                assert v <= self.n[e2], f"wait on unrecorded inc {e2} {v}>{self.n[e2]}"
            self.waited[eng][s] = v
            need.append((s, v))
        return need

    def op(self, eng, fn, reads=(), writes=(), inc=True):
        rec = _Rec()
        fn(rec)
        name_, a_, k_ = rec.call
        fn = lambda e: getattr(e, name_)(*a_, **k_)
        need = self._collect(eng, reads, writes)
        if inc:
            self.n[eng] += 1
            tk = (self.sem[eng], self.n[eng])
            for t, isw in self.pending[eng]:
                if isw:
                    t.last_w = tk
                    t.readers = []
                else:
                    t.readers.append(tk)
            self.pending[eng] = []
            for t in reads:
                t.readers.append(tk)
            for t in writes:
                t.last_w = tk
                t.readers = []
        else:
            for t in reads:
                self.pending[eng].append((t, False))
            for t in writes:
                self.pending[eng].append((t, True))
        self.ops[eng].append((need, fn, inc, None))

    def dma(self, eng, out, in_, reads=(), writes=(), owner=None, **kw):
        need = self._collect(eng, reads, writes)
        if owner is None:
            owner = writes[0] if writes else reads[0]
        if owner.dsem is None:
            owner.dsem = self.stack.enter_context(self.nc.semaphore(f"d{self.nd}"))
            self.nd += 1
        owner.dcount += 1
        tk = (owner.dsem, 16 * owner.dcount)
        for t in reads:
            t.readers.append(tk)
        for t in writes:
            t.last_w = tk
            t.readers = []
        fn = lambda e: e.dma_start(out=out, in_=in_, **kw)
        self.ops[eng].append((need, fn, False, (owner.dsem, 16)))
        return tk

    def wait_all(self, eng, tickets):
        need = []
        for s, v in tickets:
            if self.waited[eng].get(s, 0) < v:
                self.waited[eng][s] = v
                need.append((s, v))
        self.ops[eng].append((need, None, False, None))

    def replay(self):
        nc = self.nc
        for e in ENGS:
            assert not self.pending[e], f"pending tiles on {e}"
        with nc.Block() as block:
            def run(name):
                def f(e):
                    for need, fn, inc, dinc in self.ops[name]:
                        for s, v in need:
                            e.wait_ge(s, v)
                        if fn is None:
                            continue
                        ins = fn(e)
                        if inc:
                            ins.then_inc(self.sem[name], 1)
                        if dinc is not None:
                            ins.then_inc(dinc[0], dinc[1])
                return f
            block.tensor(run("pe"))
            block.scalar(run("act"))
            block.vector(run("dve"))
            block.gpsimd(run("pool"))
            block.sync(run("sp"))


def build_nc(n_layers=DEPTH, debug=(), dbg_l=0):
    nc = bass.Bass("TRN2", target_bir_lowering=False)
    dt_in = lambda name, shape: nc.dram_tensor(name, shape, F32, kind="ExternalInput")
    x_d = dt_in("x", [S, D]).ap()
    mixg_d = dt_in("mix_norm_g", [DEPTH, D])
    win_d = dt_in("w_in", [DEPTH, D, 2560]).ap()
    qg_d = dt_in("q_norm_g", [DEPTH, 64])
    kg_d = dt_in("k_norm_g", [DEPTH, 64])
    tabx_d = dt_in("tabx", [DEPTH, 8, TABX])
    sgug_d = dt_in("sgu_norm_g", [DEPTH, 512])
    wsp_d = dt_in("w_spatial", [DEPTH, 8, 128, 128]).ap()
    bsp_d = dt_in("b_spatial", [DEPTH, 8, 128]).ap()
    attg_d = dt_in("att_out_norm_g", [DEPTH, 512])
    gmg_d = dt_in("gmlp_out_norm_g", [DEPTH, 512])
    wout_d = dt_in("w_out", [DEPTH, D, D]).ap()
    ffng_d = dt_in("ffn_norm_g", [DEPTH, D])
    wfi_d = dt_in("w_ffn_in", [DEPTH, D, 2 * DFF]).ap()
    wfo_d = dt_in("w_ffn_out", [DEPTH, DFF, D]).ap()
    ident_d = dt_in("ident", [128, 128]).ap()
    jrev_d = dt_in("jrev", [128, 128]).ap()
    gind_d = dt_in("gind", [128, 512]).ap()
    y_d = nc.dram_tensor("y", [S, D], F32, kind="ExternalOutput").ap()
    dbg_tk = []
    HF = FC // 2

    with ExitStack() as st:
        P = Prog(nc, st)
        _cnt = [0]

        def sb(stack, name, shape, dt):
            _cnt[0] += 1
            return stack.enter_context(nc.sbuf_tensor(f"sb{_cnt[0]}_{name}", shape, dt))

        def sbT(stack, name, shape, dt):
            b = sb(stack, name, shape, dt)
            return b, P.T(name, b)

        def sbN(stack, name, shape, dt, n):
            bs = [sb(stack, f"{name}{i}", shape, dt) for i in range(n)]
            return bs, [P.T(f"{name}{i}", bs[i]) for i in range(n)]

        def dump(name, buf, tiles, shape, dt):
            dd = nc.dram_tensor("dbg_" + name, shape, dt, kind="ExternalOutput").ap()
            dbg_tk.append(P.dma("sp", dd, buf[:], reads=list(tiles)))

        def blk(n):
            return slice(n * 512, (n + 1) * 512)

        def tile_(t):
            return slice(t * 128, (t + 1) * 128)

        flat = lambda ll: [x for y in ll for x in y]

        xT = sb(st, "xT", [128, DC, S], F32)
        t_xT = [[P.T(f"xT{n}_{c}", xT) for c in range(DC)] for n in range(NB)]
        hT = sb(st, "hT", [128, DC, S], BF16)
        t_hT = [[P.T(f"hT{n}_{c}", hT) for c in range(DC)] for n in range(NB)]
        mixTg = sb(st, "mixTg", [128, 4, S], BF16)
        t_mixTg = [P.T(f"mTg{t}", mixTg) for t in range(NT)]
        ident, t_ident = sbT(st, "ident", [128, 128], F32)
        jrev, t_jrev = sbT(st, "jrev", [128, 128], F32)
        identb, t_identb = sbT(st, "identb", [128, 128], BF16)
        ones_bf, t_ones = sbT(st, "ones_bf", [128, 128], BF16)
        bd_bf, t_bd = sbT(st, "bd_bf", [128, 128], BF16)
        mhalf, t_mhalf = sbT(st, "mhalf", [128, 1], F32)
        mixg = sb(st, "mixg", [128, DEPTH, 8], F32); t_mixg = [P.T(f"mixg{l}", mixg) for l in range(DEPTH)]
        ffng = sb(st, "ffng", [128, DEPTH, 8], F32); t_ffng = [P.T(f"ffng{l}", ffng) for l in range(DEPTH)]
        attg = sb(st, "attg", [128, DEPTH, 4], F32); t_attg = [P.T(f"attg{l}", attg) for l in range(DEPTH)]
        gmg = sb(st, "gmg", [128, DEPTH, 4], F32); t_gmg = [P.T(f"gmg{l}", gmg) for l in range(DEPTH)]
        qg = sb(st, "qg", [128, DEPTH], F32); t_qg = [[P.T(f"qg{l}{h}", qg) for h in range(2)] for l in range(DEPTH)]
        qg8 = sb(st, "qg8", [128, DEPTH], F32); t_qg8 = [P.T(f"qg8{l}", qg8) for l in range(DEPTH)]
        kg = sb(st, "kg", [128, DEPTH], F32); t_kg = [[P.T(f"kg{l}{h}", kg) for h in range(2)] for l in range(DEPTH)]
        ring = [sb(st, f"ring{i}", [128, 2048], BF16) for i in range(NR)]
        t_ring = [[P.T(f"ring{i}a", ring[i]), P.T(f"ring{i}b", ring[i])] for i in range(NR)]
        pAll = st.enter_context(nc.psum_tensor("pAll", [128, 4096], F32))
        pB = pAll[:, 2048:3072]
        pC = pAll[:, 3072:4096]
        t_bank = [P.T(f"bank{i}") for i in range(8)]

        def bank(i):
            return pAll[:, i * 512:(i + 1) * 512]

        loads = []
        lidx = {}

        def add_load(key, view, src):
            lidx[key] = len(loads)
            if isinstance(src, list):
                loads.append((view, src))
            else:
                loads.append((view, [(view, src, (0, 1))]))

        v_half = lambda b: b[:].rearrange("p (c n) -> p c n", c=4)
        v_blk = lambda b: b[:, 0:1024].rearrange("p (c n) -> p c n", c=8)
        v_blk2 = lambda b: b[:].rearrange("p (c n) -> p c n", c=8)
        v_w1 = lambda b: b[:].rearrange("p (g c n) -> p g c n", g=2, c=8)
        v_w1g = lambda b: b[:, 0:1024].rearrange("p (c n) -> p c n", c=8)
        v_w1u = lambda b: b[:, 1024:2048].rearrange("p (c n) -> p c n", c=8)
        v_w2 = lambda b: b[:, 0:HF * 128].rearrange("p (f n) -> p f n", f=HF)
        for l in range(n_layers):
            win_v = win_d[l].rearrange("(c p) n -> p c n", p=128)
            for nm, c0 in (("Wu", 1536), ("Wg", 2048), ("Wv", 1024)):
                for hc in range(2):
                    add_load((l, nm, hc), v_half, win_v[:, hc * 4:(hc + 1) * 4, c0:c0 + 512])
            for fb in range(8):
                add_load((l, "Wqk", fb), v_blk, win_v[:, :, fb * 128:(fb + 1) * 128])
            wout_v = wout_d[l].rearrange("(c p) n -> p c n", p=128)
            for k in range(4):
                add_load((l, "Wo2", k), v_blk2, wout_v[:, :, k * 256:(k + 1) * 256])
            wfi_v = wfi_d[l].rearrange("(c p) (g f n) -> p c g f n", p=128, g=2, n=128)
            for half in range(2):
                for fc in range(HF):
                    f = half * HF + fc
                    add_load((l, "W1", f), v_w1, [(v_w1g, wfi_v[:, :, 0, f, :], (0,)), (v_w1u, wfi_v[:, :, 1, f, :], (1,))])
                ngs = 2 if (half == 1 and l == n_layers - 1) else 1
                for ng in range(ngs):
                    for db in range(8):
                        src = wfo_d[l][half * HF * 128:(half + 1) * HF * 128, db * 128:(db + 1) * 128].rearrange(
                            "(f p) n -> p f n", p=128)
                        add_load((l, "W2", half, ng, db), v_w2, src)
        issued = [0]

        def issue_upto(k):
            while issued[0] <= k and issued[0] < len(loads):
                j = issued[0]
                view, parts = loads[j]
                for (dfn, src, sel) in parts:
                    P.dma("pool", dfn(ring[j % NR]), src, writes=[t_ring[j % NR][k2] for k2 in sel])
                issued[0] += 1

        def done(key):
            issue_upto(lidx[key] + NR)

        def W(key):
            j = lidx[key]
            assert j < issued[0], f"load {key} not issued"
            view, _ = loads[j]
            return view(ring[j % NR]), list(t_ring[j % NR])

        P.dma("sp", ident[:], ident_d, writes=[t_ident])
        P.dma("sp", jrev[:], jrev_d, writes=[t_jrev])
        P.op("dve", lambda e: e.memset(ones_bf[:], 1.0), writes=[t_ones])
        P.op("dve", lambda e: e.memset(bd_bf[:], 0.0), writes=[t_bd])
        P.op("dve", lambda e: e.memset(bd_bf[0:64, 0:64], 1.0), writes=[t_bd])
        P.op("dve", lambda e: e.memset(bd_bf[64:128, 64:128], 1.0), writes=[t_bd])
        P.op("dve", lambda e: e.memset(mhalf[:], -0.5), writes=[t_mhalf])
        P.op("dve", lambda e: e.tensor_copy(identb[:], ident[:]), reads=[t_ident], writes=[t_identb])

        def load_params():
            for l in range(DEPTH):
                def colload(dst, src_t, n, c, tl):
                    src = bass.AP(src_t, l * n, [[1, 128], [128, c]])
                    P.dma("sp", dst, src, writes=[tl], allow_slow_non_contiguous=True)
                colload(mixg[:, l, :], mixg_d, D, 8, t_mixg[l])
                colload(ffng[:, l, :], ffng_d, D, 8, t_ffng[l])
                colload(attg[:, l, :], attg_d, 512, 4, t_attg[l])
                colload(gmg[:, l, :], gmg_d, 512, 4, t_gmg[l])
                for hb in range(2):
                    P.dma("sp", qg[hb * 64:(hb + 1) * 64, l:l + 1], bass.AP(qg_d, l * 64, [[1, 64], [1, 1]]),
                          writes=[t_qg[l][hb]])
                    P.dma("sp", kg[hb * 64:(hb + 1) * 64, l:l + 1], bass.AP(kg_d, l * 64, [[1, 64], [1, 1]]),
                          writes=[t_kg[l][hb]])
                P.op("dve", lambda e: e.tensor_scalar(qg8[:, l:l + 1], qg[:, l:l + 1], 0.125, None, ALU.mult),
                     reads=t_qg[l], writes=[t_qg8[l]])

        NQ = 6

        def norm_alloc(stack):
            sq, t_sq = sbN(stack, "nsq", [128, 512], BF16, NQ)
            rs, t_rs = sbN(stack, "nrs", [128, 512], F32, 2)
            return (sq, t_sq, rs, t_rs)

        def norm_block(N, gcols, t_g, l, n):
            sq, t_sq, rs, t_rs = N
            b = 6 + n % 2
            for c in range(DC):
                k = (n * DC + c) % NQ
                if c % 3 == 1:
                    P.op("pool", lambda e: e.tensor_tensor(sq[k][:], xT[:, c, blk(n)], xT[:, c, blk(n)], ALU.mult),
                         reads=[t_xT[n][c]], writes=[t_sq[k]])
                else:
                    P.op("act", lambda e: e.activation(sq[k][:], xT[:, c, blk(n)], AF.Square),
                         reads=[t_xT[n][c]], writes=[t_sq[k]])
                P.op("pe", lambda e: e.matmul(bank(b), lhsT=ones_bf[:], rhs=sq[k][:],
                                              start=(c == 0), stop=(c == DC - 1)),
                     reads=[t_ones, t_sq[k]], writes=[t_bank[b]], inc=True)
            r = n % 2
            P.op("act", lambda e: e.activation(rs[r][:], bank(b), AF.Ln, bias=EPS, scale=1.0 / D),
                 reads=[t_bank[b]], writes=[t_rs[r]])
            P.op("act", lambda e: e.activation(rs[r][:], rs[r][:], AF.Exp, scale=-0.5),
                 reads=[t_rs[r]], writes=[t_rs[r]])
            for c in range(DC):
                P.op("dve", lambda e: e.scalar_tensor_tensor(
                    out=hT[:, c, blk(n)], in0=xT[:, c, blk(n)], scalar=gcols[:, l, c:c + 1], in1=rs[r][:],
                    op0=ALU.mult, op1=ALU.mult),
                    reads=[t_xT[n][c], t_g, t_rs[r]], writes=[t_hT[n][c]])

        out_tk = []
        out_ctx = {}

        def out_tiles(t0_, t1_):
            yo, t_yo, NY = out_ctx["yo"], out_ctx["t_yo"], out_ctx["NY"]
            for tt in range(t0_, t1_):
                sl_ = tt % NY
                for half in range(2):
                    b = 2 * (tt % 2) + half
                    for cc in range(4):
                        c = half * 4 + cc
                        P.op("pe", lambda e: e.transpose(
                            bank(b)[:, cc * 128:(cc + 1) * 128], xT[:, c, tile_(tt)], ident[:]),
                            reads=[t_xT[tt // 4][c], t_ident], writes=[t_bank[b]], inc=(cc == 3))
                    if half == 0:
                        P.op("act", lambda e: e.copy(yo[sl_][:, 0:512], bank(b)), reads=[t_bank[b]], writes=[t_yo[sl_][0]])
                    else:
                        P.op("dve", lambda e: e.tensor_copy(yo[sl_][:, 512:1024], bank(b)), reads=[t_bank[b]], writes=[t_yo[sl_][1]])
                out_tk.append(P.dma("sp", y_d[tt * 128:(tt + 1) * 128, :], yo[sl_][:], reads=t_yo[sl_], owner=t_yo[sl_][0]))

        for l in range(n_layers):
            with ExitStack() as sl:
                sgug, t_sgug = sbT(sl, "sgug", [128, 512], F32)
                wnat, t_wnat = sbT(sl, "wnat", [128, 8, 128], F32)
                wT, t_wT = sbT(sl, "wT", [128, 8, 128], BF16)
                bnat, t_bnat = sbT(sl, "bnat", [8, 128], F32)
                bhi, t_bhi = sbT(sl, "bhi", [128, 128], BF16)
                blo, t_blo = sbT(sl, "blo", [128, 128], BF16)
                gindb, t_gindb = sbT(sl, "gindb", [128, 512], BF16)
                P.dma("pool", gindb[:], gind_d, writes=[t_gindb])
                P.dma("sp", sgug[:], bass.AP(sgug_d, l * 512, [[0, 128], [1, 512]]), writes=[t_sgug])
                P.dma("sp", wnat[:], wsp_d[l].rearrange("g t s -> t g s"), writes=[t_wnat])
                P.dma("sp", bnat[:], bsp_d[l], writes=[t_bnat])
                P.op("dve", lambda e: e.memset(bhi[:], 0.0), writes=[t_bhi])
                P.op("dve", lambda e: e.memset(blo[:], 0.0), writes=[t_blo])
                P.op("dve", lambda e: e.tensor_copy(bhi[0:8, :], bnat[:]), reads=[t_bnat], writes=[t_bhi])
                P.op("dve", lambda e: e.tensor_tensor(blo[0:8, :], bnat[:], bhi[0:8, :], ALU.subtract),
                     reads=[t_bnat, t_bhi], writes=[t_blo])

                if l == 0:
                    with ExitStack() as s0:
                        NX = 3
                        xin, t_xin = sbN(s0, "xin", [128, D], F32, NX)
                        N0 = norm_alloc(s0)
                        for tt in range(NT):
                            sl_ = tt % NX
                            P.dma("sp", xin[sl_][:], x_d[tt * 128:(tt + 1) * 128, :], writes=[t_xin[sl_]])
                            if tt == 1:
                                load_params()
                            if tt == 3:
                                issue_upto(NR - 1)
                            for half in range(2):
                                b = 2 * sl_ + half
                                for cc in range(4):
                                    c = half * 4 + cc
                                    P.op("pe", lambda e: e.transpose(
                                        bank(b)[:, cc * 128:(cc + 1) * 128], xin[sl_][:, c * 128:(c + 1) * 128], ident[:]),
                                        reads=[t_xin[sl_], t_ident], writes=[t_bank[b]], inc=(cc == 3))
                                dst = xT[:, half * 4:half * 4 + 4, tile_(tt)]
                                src = bank(b).rearrange("p (c t) -> p c t", c=4)
                                tw = t_xT[tt // 4][half * 4:half * 4 + 4]
                                if half == 0:
                                    P.op("act", lambda e: e.copy(dst, src), reads=[t_bank[b]], writes=tw)
                                else:
                                    P.op("dve", lambda e: e.tensor_copy(dst, src), reads=[t_bank[b]], writes=tw)
                            if tt % 4 == 3 and tt >= 7:
                                norm_block(N0, mixg, t_mixg[l], l, tt // 4 - 1)
                        norm_block(N0, mixg, t_mixg[l], l, NB - 1)
                        P.retire(t_xin + N0[1] + N0[3])

                for hb in range(2):
                    for gg in range(4):
                        g = hb * 4 + gg
                        P.op("pe", lambda e: e.transpose(
                            bank(4 + hb)[:, gg * 128:(gg + 1) * 128], wnat[:, g, :], ident[:]),
                            reads=[t_wnat, t_ident], writes=[t_bank[4 + hb]], inc=(gg == 3))
                    P.op("dve", lambda e: e.tensor_copy(
                        wT[:, hb * 4:hb * 4 + 4, :], bank(4 + hb).rearrange("p (g t) -> p g t", g=4)),
                        reads=[t_bank[4 + hb]], writes=[t_wT])
                P.op("dve", lambda e: e.memset(wT[64:128, :, 0:64], 0.0), writes=[t_wT])

                if l > 0:
                    with ExitStack() as sn:
                        Nn = norm_alloc(sn)
                        for n in range(NB):
                            norm_block(Nn, mixg, t_mixg[l], l, n)
                        P.retire(Nn[1] + Nn[3])
                if "hT" in debug and l == dbg_l:
                    dump("hT", hT, flat(t_hT), [128, DC, S], BF16)

                with ExitStack() as sg:
                    NS = 7
                    u_sb, t_u = sbN(sg, "u_sb", [128, 512], F32, NS)
                    vgg, t_vgg = sbN(sg, "vgg", [128, 512], F32, NS)
                    vgn, t_vgn = sbN(sg, "vgn", [128, 512], BF16, NS)
                    gm, t_gm = sbN(sg, "gm", [128, 512], F32, NS)
                    junk, t_junk = sbT(sg, "gjunk", [128, 512], BF16)
                    st4 = [sb(sg, f"gst{i}", [128, 4], F32) for i in range(NS)]
                    t_st = [[P.T(f"gst{i}{j}", st4[i]) for j in range(4)] for i in range(NS)]
                    Wu = [W((l, "Wu", hc)) for hc in range(2)]
                    Wg = [W((l, "Wg", hc)) for hc in range(2)]

                    def gA(tt):
                        bU, bG = 2 * (tt % 2), 2 * (tt % 2) + 1
                        n = tt // 4
                        for (Wx, bX) in ((Wg, bG), (Wu, bU)):
                            for c in range(DC):
                                wv, wt = Wx[c // 4]
                                P.op("pe", lambda e: e.matmul(
                                    bank(bX), lhsT=hT[:, c, tile_(tt)], rhs=wv[:, c % 4, :], start=(c == 0), stop=(c == DC - 1)),
                                    reads=[t_hT[n][c]] + wt, writes=[t_bank[bX]], inc=(c == DC - 1))

                    def rstd_chain(stt, t_s, i0):
                        P.op("pool", lambda e: e.tensor_scalar(stt[:, i0:i0 + 1], stt[:, i0:i0 + 1], 1.0 / 512, EPS, ALU.mult, ALU.add),
                             reads=[t_s[i0]], writes=[t_s[i0]])
                        P.op("pool", lambda e: e.tensor_tensor(stt[:, i0 + 1:i0 + 2], stt[:, i0:i0 + 1], mhalf[:], ALU.pow),
                             reads=[t_s[i0], t_mhalf], writes=[t_s[i0 + 1]])

                    def gB(tt):
                        s_ = tt % NS
                        bU, bG = 2 * (tt % 2), 2 * (tt % 2) + 1
                        P.op("act", lambda e: e.activation(vgg[s_][:], bank(bG), AF.Gelu_apprx_tanh),
                             reads=[t_bank[bG]], writes=[t_vgg[s_]])
                        P.op("act", lambda e: e.activation(junk[:], vgg[s_][:], AF.Square, accum_out=st4[s_][:, 0:1]),
                             reads=[t_vgg[s_]], writes=[t_junk, t_st[s_][0]])
                        rstd_chain(st4[s_], t_st[s_], 0)
                        P.op("act", lambda e: e.activation(u_sb[s_][:], bank(bU), AF.Gelu_apprx_tanh),
                             reads=[t_bank[bU]], writes=[t_u[s_]])
                        P.op("dve", lambda e: e.scalar_tensor_tensor(
                            out=vgn[s_][:], in0=vgg[s_][:], scalar=st4[s_][:, 1:2], in1=sgug[:], op0=ALU.mult, op1=ALU.mult),
                            reads=[t_vgg[s_], t_st[s_][1], t_sgug], writes=[t_vgn[s_]])

                    def gC(tt):
                        s_ = tt % NS
                        bM = 4 + tt % 2
                        P.op("pe", lambda e: e.matmul(bank(bM), lhsT=bhi[:], rhs=gindb[:], start=True, stop=False),
                             reads=[t_bhi, t_gindb], writes=[t_bank[bM]], inc=False)
                        P.op("pe", lambda e: e.matmul(bank(bM), lhsT=blo[:], rhs=gindb[:], start=False, stop=False),
                             reads=[t_blo, t_gindb], writes=[t_bank[bM]], inc=False)
                        for g in range(8):
                            P.op("pe", lambda e: e.matmul(
                                bank(bM)[:, g * 64:(g + 1) * 64], lhsT=wT[:, g, :], rhs=vgn[s_][:, g * 64:(g + 1) * 64],
                                start=False, stop=(g == 7)),
                                reads=[t_wT, t_vgn[s_]], writes=[t_bank[bM]], inc=(g == 7))

                    def gD1(tt):
                        s_ = tt % NS
                        bM = 4 + tt % 2
                        P.op("dve", lambda e: e.tensor_tensor(gm[s_][:], bank(bM), u_sb[s_][:], ALU.mult),
                             reads=[t_bank[bM], t_u[s_]], writes=[t_gm[s_]])

                    def gD2(tt):
                        s_ = tt % NS
                        P.op("act", lambda e: e.activation(junk[:], gm[s_][:], AF.Square, accum_out=st4[s_][:, 2:3]),
                             reads=[t_gm[s_]], writes=[t_junk, t_st[s_][2]])
                        rstd_chain(st4[s_], t_st[s_], 2)
                        P.op("act", lambda e: e.activation(gm[s_][:], gm[s_][:], AF.Copy, scale=st4[s_][:, 3:4]),
                             reads=[t_gm[s_], t_st[s_][3]], writes=[t_gm[s_]])

                    def gE(tt):
                        s_ = tt % NS
                        bTr = 6 + tt % 2
                        for j in range(4):
                            P.op("pe", lambda e: e.transpose(
                                bank(bTr)[:, j * 128:(j + 1) * 128], gm[s_][:, j * 128:(j + 1) * 128], ident[:]),
                                reads=[t_gm[s_], t_ident], writes=[t_bank[bTr]], inc=(j == 3))
                        P.op("dve", lambda e: e.tensor_tensor(
                            mixTg[:, :, tile_(tt)], bank(bTr).rearrange("p (j t) -> p j t", j=4),
                            gmg[:, l, :].rearrange("p (j o) -> p j o", o=1).to_broadcast([128, 4, 128]), ALU.mult),
                            reads=[t_bank[bTr], t_gmg[l]], writes=[t_mixTg[tt]])

                    for s_ in range(NT + 6):
                        if 0 <= s_ - 3 < NT:
                            gC(s_ - 3); gD1(s_ - 3)
                        if 0 <= s_ - 6 < NT:
                            gE(s_ - 6)
                        if s_ < NT:
                            gA(s_); gB(s_)
                        if 0 <= s_ - 3 < NT:
                            gD2(s_ - 3)
                        if s_ == NT - 1:
                            for hc in range(2):
                                done((l, "Wu", hc)); done((l, "Wg", hc))
                    P.retire([t_junk] + t_u + t_vgg + t_vgn + t_gm + flat(t_st))
                P.retire([t_sgug, t_wnat, t_wT, t_bnat, t_bhi, t_blo, t_gindb])
            if "mixTg" in debug and l == dbg_l:
                dump("mixTg", mixTg, t_mixTg, [128, 4, S], BF16)

            with ExitStack() as sa:
                bias = sb(sa, "expB", [128, 4, 5, 2, 128], BF16)
                t_bias = [P.T(f"expB{h}", bias) for h in range(8)]
                V = sb(sa, "V", [128, NT, 8, 65], BF16)
                t_V = [P.T(f"V{t}", V) for t in range(NT)]
                qT = sb(sa, "qT", [128, 4, S], BF16)
                kT = sb(sa, "kT", [128, 4, S], BF16)
                t_qT = [[P.T(f"qT{f}{n}", qT) for n in range(NB)] for f in range(4)]
                t_kT = [[P.T(f"kT{f}{n}", kT) for n in range(NB)] for f in range(4)]
                P.op("dve", lambda e: e.memset(V[:, :, :, 64:65], 1.0), writes=t_V)

                with ExitStack() as shk:
                    hk, t_hk = sbN(shk, "hk", [128, 640], F32, 2)

                    def bias_head(h):
                        k = h % 2
                        src = bass.AP(tabx_d, (l * 8 + h) * TABX + 1, [[1, 128], [128, 5], [1, 128]])
                        P.dma("sp", hk[k][:].rearrange("p (d q) -> p d q", d=5), src, writes=[t_hk[k]])
                        b0 = 4 + 2 * k
                        pX = pB if k == 0 else pC
                        P.op("pe", lambda e: e.matmul(pX[:, 0:512], lhsT=jrev[:], rhs=hk[k][:, 0:512], start=True, stop=True),
                             reads=[t_jrev, t_hk[k]], writes=[t_bank[b0]])
                        P.op("pe", lambda e: e.matmul(pX[:, 512:640], lhsT=jrev[:], rhs=hk[k][:, 512:640], start=True, stop=True),
                             reads=[t_jrev, t_hk[k]], writes=[t_bank[b0 + 1]])
                        P.op("act", lambda e: e.activation(bias[:, h // 2, :, h % 2, :],
                                                          pX[:, 0:640].rearrange("p (d q) -> p d q", d=5), AF.Exp),
                             reads=[t_bank[b0], t_bank[b0 + 1]], writes=[t_bias[h]])
                        P.op("dve", lambda e: e.memset(bias[64:128, h // 2, 0, h % 2, 0:64], 0.0), writes=[t_bias[h]])
                        P.op("dve", lambda e: e.memset(bias[0:64, h // 2, 4, h % 2, 64:128], 0.0), writes=[t_bias[h]])

                    Wv = [W((l, "Wv", hc)) for hc in range(2)]
                    for tt in range(NT):
                        b = tt % 4
                        n = tt // 4
                        for c in range(DC):
                            wv, wt = Wv[c // 4]
                            P.op("pe", lambda e: e.matmul(
                                bank(b), lhsT=hT[:, c, tile_(tt)], rhs=wv[:, c % 4, :], start=(c == 0), stop=(c == DC - 1)),
                                reads=[t_hT[n][c]] + wt, writes=[t_bank[b]], inc=(c == DC - 1))
                        src = bank(b).rearrange("p (h d) -> p h d", h=8)
                        if tt % 2 == 0:
                            P.op("act", lambda e: e.copy(V[:, tt, :, 0:64], src), reads=[t_bank[b]], writes=[t_V[tt]])
                        else:
                            P.op("dve", lambda e: e.tensor_copy(V[:, tt, :, 0:64], src), reads=[t_bank[b]], writes=[t_V[tt]])
                        if tt % 2 == 1:
                            bias_head(tt // 2)
                    for hc in range(2):
                        done((l, "Wv", hc))
                    P.retire(t_hk)

                with ExitStack() as sq_:
                    NQK = 3
                    qsq, t_qsq = sbN(sq_, "qsq", [128, 512], BF16, NQK)
                    qrs, t_qrs = sbN(sq_, "qrs", [128, 512], F32, NQK)
                    NIT = 32

                    def qA(it):
                        fb, n = it // 4, it % 4
                        bQ = it % 3
                        wv, wt = W((l, "Wqk", fb))
                        for c in range(DC):
                            P.op("pe", lambda e: e.matmul(
                                bank(bQ), lhsT=wv[:, c, :], rhs=hT[:, c, blk(n)], start=(c == 0), stop=(c == DC - 1)),
                                reads=wt + [t_hT[n][c]], writes=[t_bank[bQ]], inc=(c == DC - 1))
                        if n == 3:
                            done((l, "Wqk", fb))
                        r = it % NQK
                        P.op("act", lambda e: e.activation(qsq[r][:], bank(bQ), AF.Square),
                             reads=[t_bank[bQ]], writes=[t_qsq[r]])

                    def qC(it):
                        fb, n = it // 4, it % 4
                        bQ = it % 3
                        bS = 4 + it % 2
                        r = it % NQK
                        isq = fb < 4
                        dstT = qT if isq else kT
                        t_dst = t_qT if isq else t_kT
                        gcol = qg8 if isq else kg
                        t_gc = [t_qg8[l]] if isq else t_kg[l]
                        P.op("pe", lambda e: e.matmul(bank(bS), lhsT=bd_bf[:], rhs=qsq[r][:], start=True, stop=True),
                             reads=[t_bd, t_qsq[r]], writes=[t_bank[bS]])
                        P.op("act", lambda e: e.activation(qrs[r][:], bank(bS), AF.Ln, bias=EPS, scale=1.0 / 64),
                             reads=[t_bank[bS]], writes=[t_qrs[r]])
                        P.op("act", lambda e: e.activation(qrs[r][:], qrs[r][:], AF.Exp, scale=-0.5),
                             reads=[t_qrs[r]], writes=[t_qrs[r]])
                        P.op("dve", lambda e: e.scalar_tensor_tensor(
                            out=dstT[:, fb % 4, blk(n)], in0=bank(bQ), scalar=gcol[:, l:l + 1], in1=qrs[r][:],
                            op0=ALU.mult, op1=ALU.mult),
                            reads=[t_bank[bQ]] + t_gc + [t_qrs[r]], writes=[t_dst[fb % 4][n]])

                    for s_ in range(NIT + 1):
                        if s_ < NIT:
                            qA(s_)
                        if s_ - 1 >= 0:
                            qC(s_ - 1)
                    P.retire(t_qsq + t_qrs)
                if "qT" in debug and l == dbg_l:
                    dump("qT", qT, flat(t_qT), [128, 4, S], BF16)
                    dump("kT", kT, flat(t_kT), [128, 4, S], BF16)
                    dump("V", V, t_V, [128, NT, 8, 65], BF16)

                P.retire(flat(t_hT))
                t_mixTa = [P.T(f"mTa{t}", hT) for t in range(NT)]
                mixTa = hT

                with ExitStack() as sat:
                    NQM = 3
                    qm, t_qm = sbN(sat, "qm", [128, 256], BF16, NQM)
                    NEP = 3
                    EP, t_EP = sbN(sat, "EP", [128, 1280], BF16, NEP)
                    rc, t_rc = sbT(sat, "rc", [128, 8], F32)
                    atok, t_atok = sbT(sat, "atok", [128, 512], F32)
                    anb, t_anb = sbT(sat, "anb", [128, 512], BF16)
                    ast = sb(sat, "ast", [128, 2], F32)
                    t_ast = [P.T(f"ast{j}", ast) for j in range(2)]
                    for i in range(NQM):
                        P.op("pool", lambda e: e.memset(qm[i][:], 0.0), writes=[t_qm[i]])

                    P.retire(t_bank[0:6])
                    t_S = [P.T(f"S{i}") for i in range(2)]
                    NP = NT * 4
                    trb = pAll[:, 7 * 512 + 128:7 * 512 + 384].bitcast(BF16)

                    def aQ(p):
                        m, hp = p // 4, p % 4
                        k = p % NQM
                        P.op("pool", lambda e: e.tensor_copy(qm[k][0:64, 0:128], qT[0:64, hp, tile_(m)]),
                             reads=[t_qT[hp][m // 4]], writes=[t_qm[k]])
                        P.op("pool", lambda e: e.tensor_copy(qm[k][64:128, 128:256], qT[64:128, hp, tile_(m)]),
                             reads=[t_qT[hp][m // 4]], writes=[t_qm[k]])

                    def aA(p):
                        m, hp = p // 4, p % 4
                        nb_ = min(m, 4) + 1
                        st_ = p % 2
                        c0 = st_ * 1536
                        k = p % NQM
                        for d in range(nb_):
                            j = m - d
                            P.op("pe", lambda e: e.matmul(
                                pAll[:, c0 + d * 256:c0 + (d + 1) * 256], lhsT=kT[:, hp, tile_(j)], rhs=qm[k][:],
                                start=True, stop=True),
                                reads=[t_kT[hp][j // 4], t_qm[k]], writes=[t_S[st_]], inc=(d == nb_ - 1))
                        w = nb_ * 256
                        ep = p % NEP
                        P.op("act", lambda e: e.activation(EP[ep][:, 0:w], pAll[:, c0:c0 + w], AF.Exp),
                             reads=[t_S[st_]], writes=[t_EP[ep]])
                        P.op("dve", lambda e: e.tensor_tensor(
                            EP[ep][:, 0:w], EP[ep][:, 0:w], bias[:, hp, 0:nb_, :, :].rearrange("p d u q -> p (d u q)"), ALU.mult),
                            reads=[t_EP[ep], t_bias[2 * hp], t_bias[2 * hp + 1]], writes=[t_EP[ep]])

                    deferred = {}

                    def aC(p, step):
                        m, hp = p // 4, p % 4
                        nb_ = min(m, 4) + 1
                        ep = p % NEP
                        for u in range(2):
                            h = 2 * hp + u
                            ob = 6 if h < 7 else 7
                            oc = (6 * 512 + h * 65) if h < 7 else 7 * 512
                            for d in range(nb_):
                                j = m - d
                                P.op("pe", lambda e: e.matmul(
                                    pAll[:, oc:oc + 65], lhsT=EP[ep][:, (d * 2 + u) * 128:(d * 2 + u + 1) * 128], rhs=V[:, j, h, :],
                                    start=(d == 0), stop=(d == nb_ - 1)),
                                    reads=[t_EP[ep], t_V[j]], writes=[t_bank[ob]], inc=(d == nb_ - 1))
                        if hp == 3:
                            aD1(m)
                            deferred.setdefault(step + 1, []).append((aD2, m))
                            deferred.setdefault(step + 2, []).append((aD3, m))
                            deferred.setdefault(step + 3, []).append((aE, m))

                    def aD1(m):
                        ov6 = pAll[:, 6 * 512:6 * 512 + 455].rearrange("p (h d) -> p h d", h=7)
                        ov7 = pAll[:, 7 * 512:7 * 512 + 65]
                        P.op("dve", lambda e: e.reciprocal(rc[:, 0:7].rearrange("p (h o) -> p h o", o=1), ov6[:, :, 64:65]),
                             reads=[t_bank[6]], writes=[t_rc])
                        P.op("dve", lambda e: e.reciprocal(rc[:, 7:8], ov7[:, 64:65]), reads=[t_bank[7]], writes=[t_rc])
                        P.op("dve", lambda e: e.tensor_tensor(
                            atok[:, 0:448].rearrange("p (h d) -> p h d", h=7), ov6[:, :, 0:64],
                            rc[:, 0:7].rearrange("p (h o) -> p h o", o=1).to_broadcast([128, 7, 64]), ALU.mult),
                            reads=[t_bank[6], t_rc], writes=[t_atok])
                        P.op("dve", lambda e: e.tensor_scalar(atok[:, 448:512], ov7[:, 0:64], rc[:, 7:8], None, ALU.mult),
                             reads=[t_bank[7], t_rc], writes=[t_atok])

                    def aD2(m):
                        P.op("act", lambda e: e.activation(anb[:], atok[:], AF.Square, accum_out=ast[:, 0:1]),
                             reads=[t_atok], writes=[t_anb, t_ast[0]])
                        P.op("pool", lambda e: e.tensor_scalar(ast[:, 0:1], ast[:, 0:1], 1.0 / 512, EPS, ALU.mult, ALU.add),
                             reads=[t_ast[0]], writes=[t_ast[0]])
                        P.op("pool", lambda e: e.tensor_tensor(ast[:, 1:2], ast[:, 0:1], mhalf[:], ALU.pow),
                             reads=[t_ast[0], t_mhalf], writes=[t_ast[1]])

                    def aD3(m):
                        P.op("act", lambda e: e.activation(anb[:], atok[:], AF.Copy, scale=ast[:, 1:2]),
                             reads=[t_atok, t_ast[1]], writes=[t_anb])

                    def aE(m):
                        for j in range(4):
                            P.op("pe", lambda e: e.transpose(
                                trb[:, j * 128:(j + 1) * 128], anb[:, j * 128:(j + 1) * 128], identb[:]),
                                reads=[t_anb, t_identb], writes=[t_bank[7]], inc=(j == 3))
                        P.op("dve", lambda e: e.tensor_tensor(
                            mixTa[:, 0:4, tile_(m)], trb.rearrange("p (j t) -> p j t", j=4),
                            attg[:, l, :].rearrange("p (j o) -> p j o", o=1).to_broadcast([128, 4, 128]), ALU.mult),
                            reads=[t_bank[7], t_attg[l]], writes=[t_mixTa[m]])

                    aQ(0); aQ(1)
                    for i_ in range(NWARM):
                        P.op("pe", lambda e: e.matmul(bank(7), lhsT=kT[:, 0, 0:128], rhs=qT[:, 0, 0:512], start=True, stop=True),
                             reads=[t_kT[0][0], t_qT[0][0]], writes=[t_bank[7]], inc=(i_ == NWARM - 1))
                    for s_ in range(NP + 6):
                        if s_ + 2 < NP:
                            aQ(s_ + 2)
                        if s_ < NP:
                            aA(s_)
                        if 0 <= s_ - 1 < NP:
                            aC(s_ - 1, s_)
                        for (fn_, m) in deferred.pop(s_, []):
                            fn_(m)
                    assert not deferred
                    P.retire(t_S)
                    t_bank[0:6] = [P.T(f"bank{i}") for i in range(6)]
                    P.retire(t_qm + t_EP + [t_rc, t_atok, t_anb] + t_ast)
                P.retire(t_V + flat(t_qT) + flat(t_kT) + t_bias)
            if "mixTa" in debug and l == dbg_l:
                dump("mixTa", hT, t_mixTa, [128, DC, S], BF16)

            with ExitStack() as sn:
                Nn = norm_alloc(sn)
                Wo2 = [W((l, "Wo2", k)) for k in range(4)]
                new_hT = [None] * NB
                it = 0
                for n in range(NB + 1):
                    if n < NB:
                        for db in range(8):
                            wv, wt = Wo2[db // 2]
                            b = it % 4
                            it += 1
                            for c in range(DC):
                                rhs = mixTa[:, c, blk(n)] if c < 4 else mixTg[:, c - 4, blk(n)]
                                tr = (t_mixTa if c < 4 else t_mixTg)[4 * n:4 * n + 4]
                                P.op("pe", lambda e: e.matmul(
                                    bank(b), lhsT=wv[:, c, (db % 2) * 128:(db % 2) * 128 + 128], rhs=rhs,
                                    start=(c == 0), stop=(c == DC - 1)),
                                    reads=wt + tr, writes=[t_bank[b]], inc=(c == DC - 1))
                            P.op("dve", lambda e: e.tensor_tensor(xT[:, db, blk(n)], xT[:, db, blk(n)], bank(b), ALU.add),
                                 reads=[t_bank[b], t_xT[n][db]], writes=[t_xT[n][db]])
                        P.retire(t_mixTa[4 * n:4 * n + 4])
                        t_hT[n] = [P.T(f"hT{n}_{c}", hT) for c in range(DC)]
                    if n >= 1:
                        norm_block(Nn, ffng, t_ffng[l], l, n - 1)
                for k in range(4):
                    done((l, "Wo2", k))
                P.retire(Nn[1] + Nn[3])
            if "x1" in debug and l == dbg_l:
                dump("x1", xT, flat(t_xT), [128, DC, S], F32)

            with ExitStack() as sf:
                actT = sb(sf, "actT", [128, HF, S], BF16)
                t_actT = [[P.T(f"actT{n}_{f}", actT) for f in range(HF)] for n in range(NB)]
                NSG = 3
                sg_, t_sg = sbN(sf, "sg", [128, 512], F32, NSG)
                last = (l == n_layers - 1)
                if last:
                    NY = 3
                    yo = [sb(sf, f"yo{i}", [128, D], F32) for i in range(NY)]
                    t_yo = [[P.T(f"yo{i}_{h}", yo[i]) for h in range(2)] for i in range(NY)]
                    out_ctx.update(yo=yo, t_yo=t_yo, NY=NY)
                it = 0
                it2 = 0
                for half in range(2):
                    for fc in range(HF):
                        f = half * HF + fc
                        wv, wt = W((l, "W1", f))
                        for n in range(NB):
                            bA = 2 * (it % 2)
                            bB = bA + 1
                            r = it % NSG
                            it += 1
                            for gi, bX in ((0, bA), (1, bB)):
                                for c in range(DC):
                                    P.op("pe", lambda e: e.matmul(
                                        bank(bX), lhsT=wv[:, gi, c, :], rhs=hT[:, c, blk(n)], start=(c == 0), stop=(c == DC - 1)),
                                        reads=wt + [t_hT[n][c]], writes=[t_bank[bX]], inc=(c == DC - 1))
                            P.op("act", lambda e: e.activation(sg_[r][:], bank(bA), AF.Silu),
                                 reads=[t_bank[bA]], writes=[t_sg[r]])
                            P.op("dve", lambda e: e.tensor_tensor(actT[:, fc, blk(n)], sg_[r][:], bank(bB), ALU.mult),
                                 reads=[t_sg[r], t_bank[bB]], writes=[t_actT[n][fc]])
                        done((l, "W1", f))
                    ngs = 2 if (half == 1 and last) else 1
                    for ng in range(ngs):
                        nlist = list(range(NB)) if ngs == 1 else [2 * ng, 2 * ng + 1]
                        for db in range(8):
                            wv, wt = W((l, "W2", half, ng, db))
                            for n in nlist:
                                b = 4 + it2 % 4
                                it2 += 1
                                for fc in range(HF):
                                    P.op("pe", lambda e: e.matmul(
                                        bank(b), lhsT=wv[:, fc, :], rhs=actT[:, fc, blk(n)], start=(fc == 0), stop=(fc == HF - 1)),
                                        reads=wt + [t_actT[n][fc]], writes=[t_bank[b]], inc=(fc == HF - 1))
                                P.op("dve", lambda e: e.tensor_tensor(xT[:, db, blk(n)], xT[:, db, blk(n)], bank(b), ALU.add),
                                     reads=[t_bank[b], t_xT[n][db]], writes=[t_xT[n][db]])
                            done((l, "W2", half, ng, db))
                            if ngs == 2 and ng == 1:
                                out_tiles(db, db + 1)
                if last:
                    out_tiles(8, NT)
                P.retire(flat(t_actT) + t_sg)

        if True:
            P.wait_all("sp", out_tk + dbg_tk)
            P.replay()
    return nc


_NC_CACHE = {}


_GIND = np.zeros((128, 512), dtype=np.float32)
for _g in range(8):
    _GIND[_g, _g * 64:(_g + 1) * 64] = 1.0


def _host_inputs(inputs):
    f32 = lambda a: np.ascontiguousarray(np.asarray(a, dtype=np.float32))
    rel = f32(inputs["rel_bias"])
    tabx = np.concatenate([rel, np.repeat(rel[..., -1:], TABX - rel.shape[-1], axis=-1)], axis=-1)
    shared = {
        "mix_norm_g": f32(inputs["mix_norm_g"]), "w_in": f32(inputs["w_in"]),
        "q_norm_g": f32(inputs["q_norm_g"]), "k_norm_g": f32(inputs["k_norm_g"]),
        "tabx": np.ascontiguousarray(tabx), "sgu_norm_g": f32(inputs["sgu_norm_g"]),
        "w_spatial": f32(inputs["w_spatial"]), "b_spatial": f32(inputs["b_spatial"]),
        "att_out_norm_g": f32(inputs["att_out_norm_g"]), "gmlp_out_norm_g": f32(inputs["gmlp_out_norm_g"]),
        "w_out": f32(inputs["w_out"]), "ffn_norm_g": f32(inputs["ffn_norm_g"]),
        "w_ffn_in": f32(inputs["w_ffn_in"]), "w_ffn_out": f32(inputs["w_ffn_out"]),
        "ident": np.eye(128, dtype=np.float32),
        "jrev": np.ascontiguousarray(np.eye(128, dtype=np.float32)[:, ::-1]),
        "gind": _GIND,
    }
    return shared


def kernel(**inputs):
    x = np.asarray(inputs["x"], dtype=np.float32)
    B = x.shape[0]
    shared = _host_inputs(inputs)
    if "nc" not in _NC_CACHE:
        _NC_CACHE["nc"] = build_nc()
    nc = _NC_CACHE["nc"]
    in_maps = [dict(shared, x=np.ascontiguousarray(x[b])) for b in range(B)]
    res = run_bass_kernel_spmd(nc, in_maps, core_ids=list(range(B)))
    return np.stack([np.asarray(r["y"], dtype=np.float32) for r in res.results], axis=0)
```
